# Optimizing a Trainium2 kernel written in Bass

```python
import math
import jax, jax.numpy as jnp
from jax import lax
import numpy as np

D_MODEL = 1024
BATCH = 8
SEQ = 4096
DEPTH = 2
DEC_BATCH = 1
DEC_SEQ = 16384
PAST_LEN = 128

HEAD_DIM = 64
N_Q_HEADS = 8
N_KV_HEADS = 2
GQA_GROUP = N_Q_HEADS // N_KV_HEADS
D_ATTN = N_Q_HEADS * HEAD_DIM
D_KV = N_KV_HEADS * HEAD_DIM
WINDOW = 128
BLOCK = 128
POOL_WINDOWS = (2, 4, 8, 16)
N_POOL_GROUPS = len(POOL_WINDOWS)
D_POOL = 256
POOL_GROUP_DIM = D_POOL // N_POOL_GROUPS
D_HYENA = 256
HYENA_ORDER = 2
FILTER_BANDS = 16
FILTER_EMB = 1 + 2 * FILTER_BANDS
FILTER_HIDDEN = 64
N_FILTERS = 2 * HYENA_ORDER
DECAY_FAST_PCT = 0.3
DECAY_SLOW_PCT = 1.5
DECAY_TARGET = 1e-2
SHORT_CONV = 3
D_IN_PROJ = D_ATTN + 2 * D_KV + D_POOL + (HYENA_ORDER + 1) * D_HYENA
D_CAT = D_ATTN + D_POOL + D_HYENA
D_FF = 2816
FFN_CONV = 3
EPS = 1e-6
NEG_INF = -1e30

kernel_name = "hymba_style_bidir_hybrid_encoder"


def rms_norm(x, g):
    xf = x.astype(jnp.float32)
    y = xf * lax.rsqrt(jnp.mean(xf * xf, axis=-1, keepdims=True) + EPS) * g.astype(jnp.float32)
    return y.astype(x.dtype)


def dwconv3(x, w, b):
    xp = jnp.pad(x, ((0, 0), (1, 1), (0, 0)))
    return xp[:, :-2] * w[0] + xp[:, 1:-1] * w[1] + xp[:, 2:] * w[2] + b


def alibi_slopes():
    h = jnp.arange(N_Q_HEADS, dtype=jnp.float32)
    return 2.0 ** (-8.0 * (h + 1.0) / N_Q_HEADS)


def windowed_attention(q, k, v, q_norm_g, k_norm_g, sink):
    B, L = q.shape[0], q.shape[1]
    nb = L // BLOCK
    q = rms_norm(q.reshape(B, L, N_Q_HEADS, HEAD_DIM), q_norm_g)
    k = rms_norm(k.reshape(B, L, N_KV_HEADS, HEAD_DIM), k_norm_g)
    v = v.reshape(B, L, N_KV_HEADS, HEAD_DIM)
    pad = ((0, 0), (BLOCK, BLOCK), (0, 0), (0, 0))
    kp = jnp.pad(k, pad).reshape(B, nb + 2, BLOCK, N_KV_HEADS, HEAD_DIM)
    vp = jnp.pad(v, pad).reshape(B, nb + 2, BLOCK, N_KV_HEADS, HEAD_DIM)
    kb = jnp.concatenate([kp[:, :-2], kp[:, 1:-1], kp[:, 2:]], axis=2)
    vb = jnp.concatenate([vp[:, :-2], vp[:, 1:-1], vp[:, 2:]], axis=2)
    qb = q.reshape(B, nb, BLOCK, N_KV_HEADS, GQA_GROUP, HEAD_DIM)
    scale = 1.0 / math.sqrt(HEAD_DIM)
    s = jnp.einsum('bnqkgd,bnskd->bnkgqs', qb, kb, preferred_element_type=jnp.float32) * scale
    blk = jnp.arange(nb)[:, None]
    qpos = blk * BLOCK + jnp.arange(BLOCK)[None, :]
    kpos = (blk - 1) * BLOCK + jnp.arange(3 * BLOCK)[None, :]
    dist = jnp.abs(qpos[:, :, None] - kpos[:, None, :])
    valid = (dist <= WINDOW) & (kpos[:, None, :] >= 0) & (kpos[:, None, :] < L)
    slopes = alibi_slopes().reshape(N_KV_HEADS, GQA_GROUP)
    s = s - slopes[None, None, :, :, None, None] * dist.astype(jnp.float32)[None, :, None, None, :, :]
    s = jnp.where(valid[None, :, None, None, :, :], s, NEG_INF)
    sink_col = jnp.broadcast_to(sink.astype(jnp.float32).reshape(1, 1, N_KV_HEADS, GQA_GROUP, 1, 1),
                                s.shape[:-1] + (1,))
    p = jax.nn.softmax(jnp.concatenate([s, sink_col], axis=-1), axis=-1)[..., :-1]
    o = jnp.einsum('bnkgqs,bnskd->bnqkgd', p.astype(v.dtype), vb)
    return o.reshape(B, L, D_ATTN)


def multiscale_pool(u, pool_w, pool_scale):
    B, L, _ = u.shape
    uf = u.astype(jnp.float32)
    cs = jnp.concatenate([jnp.zeros((B, 1, D_POOL), jnp.float32), jnp.cumsum(uf, axis=1)], axis=1)
    t = jnp.arange(L)
    outs = []
    for gi, w in enumerate(POOL_WINDOWS):
        sl = slice(gi * POOL_GROUP_DIM, (gi + 1) * POOL_GROUP_DIM)
        lo = jnp.clip(t - w // 2, 0, L)
        hi = jnp.clip(t - w // 2 + w, 0, L)
        csg = cs[:, :, sl]
        mean = (jnp.take(csg, hi, axis=1) - jnp.take(csg, lo, axis=1)) / (hi - lo).astype(jnp.float32)[None, :, None]
        d = mean - uf[:, :, sl]
        outs.append(d @ pool_w[gi].astype(jnp.float32))
    y = jnp.concatenate(outs, axis=-1) * pool_scale.astype(jnp.float32)
    return y.astype(u.dtype)


def hyena_filters(L, w1, b1, freq1, w2, b2, freq2, w3):
    t_norm = jnp.linspace(0.0, 1.0, L, dtype=jnp.float32)[:, None]
    n = jnp.arange(L, dtype=jnp.float32)[:, None]
    bands = jnp.linspace(1e-4, FILTER_BANDS - 1, FILTER_BANDS, dtype=jnp.float32)[None, :]
    ang = 2.0 * math.pi * n * bands / L
    z = jnp.concatenate([t_norm, jnp.cos(ang), -jnp.sin(ang)], axis=-1)
    f32 = jnp.float32
    h = jnp.sin(freq1.astype(f32) * (z @ w1.astype(f32) + b1.astype(f32)))
    h = jnp.sin(freq2.astype(f32) * (h @ w2.astype(f32) + b2.astype(f32)))
    h = (h @ w3.astype(f32)).reshape(L, N_FILTERS, D_HYENA)
    max_decay = math.log(DECAY_TARGET) / DECAY_FAST_PCT
    min_decay = math.log(DECAY_TARGET) / DECAY_SLOW_PCT
    deltas = jnp.abs(jnp.linspace(min_decay, max_decay, D_HYENA, dtype=jnp.float32))
    decay = jnp.exp(-t_norm[:, :, None] * deltas[None, None, :])
    return h * decay


def bidir_fftconv(u, h_fwd, h_bwd, bias):
    L = u.shape[1]
    C = u.shape[2]
    k = jnp.concatenate([h_fwd, jnp.zeros((1, C), jnp.float32), h_bwd[1:][::-1]], axis=0)
    K = jnp.fft.rfft(k, axis=0)
    uf = u.astype(jnp.float32)
    U = jnp.fft.rfft(uf, n=2 * L, axis=1)
    y = jnp.fft.irfft(U * K[None], n=2 * L, axis=1)[:, :L]
    return (y + bias.astype(jnp.float32) * uf).astype(u.dtype)


def hyena_mixer(u, conv_w, conv_b, filters, hy_bias):
    u = dwconv3(u, conv_w, conv_b)
    v = u[..., :D_HYENA]
    gates = (u[..., D_HYENA:2 * D_HYENA], u[..., 2 * D_HYENA:])
    z = v
    for o in range(HYENA_ORDER):
        z = gates[o] * bidir_fftconv(z, filters[:, 2 * o], filters[:, 2 * o + 1], hy_bias[o])
    return z


def encoder_layer(x, norm1_g, w_in, q_norm_g, k_norm_g, attn_sink, pool_w, pool_scale,
                  hy_conv_w, hy_conv_b, filt_w1, filt_b1, filt_freq1, filt_w2, filt_b2, filt_freq2,
                  filt_w3, hy_bias, out_norm_g, w_out, norm2_g, w_ffn_in, ffn_conv_w, ffn_conv_b, w_ffn_out):
    L = x.shape[1]
    h = rms_norm(x, norm1_g)
    p = h @ w_in
    o0 = D_ATTN
    o1 = o0 + D_KV
    o2 = o1 + D_KV
    o3 = o2 + D_POOL
    q, k, v = p[..., :o0], p[..., o0:o1], p[..., o1:o2]
    pool_in, hy_in = p[..., o2:o3], p[..., o3:]
    a = windowed_attention(q, k, v, q_norm_g, k_norm_g, attn_sink)
    b = multiscale_pool(pool_in, pool_w, pool_scale)
    filters = hyena_filters(L, filt_w1, filt_b1, filt_freq1, filt_w2, filt_b2, filt_freq2, filt_w3)
    c = hyena_mixer(hy_in, hy_conv_w, hy_conv_b, filters, hy_bias)
    cat = jnp.concatenate([rms_norm(a, out_norm_g[:D_ATTN]),
                           rms_norm(b, out_norm_g[D_ATTN:D_ATTN + D_POOL]),
                           rms_norm(c, out_norm_g[D_ATTN + D_POOL:])], axis=-1)
    x = x + cat @ w_out
    h = rms_norm(x, norm2_g)
    u = dwconv3(h @ w_ffn_in, ffn_conv_w, ffn_conv_b)
    act = jax.nn.gelu(u[..., :D_FF], approximate=False) * u[..., D_FF:]
    return x + act @ w_ffn_out


def run_trunk(x, norm1_g, w_in, q_norm_g, k_norm_g, attn_sink, pool_w, pool_scale,
              hy_conv_w, hy_conv_b, filt_w1, filt_b1, filt_freq1, filt_w2, filt_b2, filt_freq2,
              filt_w3, hy_bias, out_norm_g, w_out, norm2_g, w_ffn_in, ffn_conv_w, ffn_conv_b, w_ffn_out):
    for l in range(DEPTH):
        x = encoder_layer(x, norm1_g[l], w_in[l], q_norm_g[l], k_norm_g[l], attn_sink[l], pool_w[l],
                          pool_scale[l], hy_conv_w[l], hy_conv_b[l], filt_w1[l], filt_b1[l], filt_freq1[l],
                          filt_w2[l], filt_b2[l], filt_freq2[l], filt_w3[l], hy_bias[l], out_norm_g[l],
                          w_out[l], norm2_g[l], w_ffn_in[l], ffn_conv_w[l], ffn_conv_b[l], w_ffn_out[l])
    return x


def setup_inputs(seed: int = 0) -> dict:
    key = jax.random.key(seed)
    ks = jax.random.split(key, 32)
    f32 = jnp.float32

    def nrm(k, shape, scale):
        return jax.random.normal(k, shape, f32) * scale

    return {
        "x_prompt": nrm(ks[0], (BATCH, SEQ, D_MODEL), 1.0),
        "x_sample": nrm(ks[1], (DEC_BATCH, DEC_SEQ, D_MODEL), 1.0),
        "norm1_g": 1.0 + nrm(ks[2], (DEPTH, D_MODEL), 0.05),
        "w_in": nrm(ks[3], (DEPTH, D_MODEL, D_IN_PROJ), D_MODEL ** -0.5),
        "q_norm_g": 1.0 + nrm(ks[4], (DEPTH, HEAD_DIM), 0.05),
        "k_norm_g": 1.0 + nrm(ks[5], (DEPTH, HEAD_DIM), 0.05),
        "attn_sink": nrm(ks[6], (DEPTH, N_Q_HEADS), 0.5),
        "pool_w": nrm(ks[7], (DEPTH, N_POOL_GROUPS, POOL_GROUP_DIM, POOL_GROUP_DIM), POOL_GROUP_DIM ** -0.5),
        "pool_scale": 1.0 + nrm(ks[8], (DEPTH, D_POOL), 0.1),
        "hy_conv_w": nrm(ks[9], (DEPTH, SHORT_CONV, (HYENA_ORDER + 1) * D_HYENA), SHORT_CONV ** -0.5),
        "hy_conv_b": nrm(ks[10], (DEPTH, (HYENA_ORDER + 1) * D_HYENA), 0.02),
        "filt_w1": nrm(ks[11], (DEPTH, FILTER_EMB, FILTER_HIDDEN), FILTER_EMB ** -0.5),
        "filt_b1": nrm(ks[12], (DEPTH, FILTER_HIDDEN), 0.02),
        "filt_freq1": 1.0 + nrm(ks[13], (DEPTH, FILTER_HIDDEN), 0.1),
        "filt_w2": nrm(ks[14], (DEPTH, FILTER_HIDDEN, FILTER_HIDDEN), FILTER_HIDDEN ** -0.5),
        "filt_b2": nrm(ks[15], (DEPTH, FILTER_HIDDEN), 0.02),
        "filt_freq2": 1.0 + nrm(ks[16], (DEPTH, FILTER_HIDDEN), 0.1),
        "filt_w3": nrm(ks[17], (DEPTH, FILTER_HIDDEN, N_FILTERS * D_HYENA), 0.05 * FILTER_HIDDEN ** -0.5),
        "hy_bias": nrm(ks[18], (DEPTH, HYENA_ORDER, D_HYENA), 0.1),
        "out_norm_g": 1.0 + nrm(ks[19], (DEPTH, D_CAT), 0.05),
        "w_out": nrm(ks[20], (DEPTH, D_CAT, D_MODEL), (2.0 * DEPTH * D_CAT) ** -0.5),
        "norm2_g": 1.0 + nrm(ks[21], (DEPTH, D_MODEL), 0.05),
        "w_ffn_in": nrm(ks[22], (DEPTH, D_MODEL, 2 * D_FF), D_MODEL ** -0.5),
        "ffn_conv_w": nrm(ks[23], (DEPTH, FFN_CONV, 2 * D_FF), FFN_CONV ** -0.5),
        "ffn_conv_b": nrm(ks[24], (DEPTH, 2 * D_FF), 0.02),
        "w_ffn_out": nrm(ks[25], (DEPTH, D_FF, D_MODEL), (2.0 * DEPTH * D_FF) ** -0.5),
    }


def reference(x_prompt, x_sample, norm1_g, w_in, q_norm_g, k_norm_g, attn_sink, pool_w, pool_scale,
              hy_conv_w, hy_conv_b, filt_w1, filt_b1, filt_freq1, filt_w2, filt_b2, filt_freq2,
              filt_w3, hy_bias, out_norm_g, w_out, norm2_g, w_ffn_in, ffn_conv_w, ffn_conv_b, w_ffn_out):
    y_prompt = run_trunk(x_prompt, norm1_g, w_in, q_norm_g, k_norm_g, attn_sink, pool_w, pool_scale,
                         hy_conv_w, hy_conv_b, filt_w1, filt_b1, filt_freq1, filt_w2, filt_b2, filt_freq2,
                         filt_w3, hy_bias, out_norm_g, w_out, norm2_g, w_ffn_in, ffn_conv_w, ffn_conv_b, w_ffn_out)
    y_sample = run_trunk(x_sample, norm1_g, w_in, q_norm_g, k_norm_g, attn_sink, pool_w, pool_scale,
                         hy_conv_w, hy_conv_b, filt_w1, filt_b1, filt_freq1, filt_w2, filt_b2, filt_freq2,
                         filt_w3, hy_bias, out_norm_g, w_out, norm2_g, w_ffn_in, ffn_conv_w, ffn_conv_b, w_ffn_out)
    return (y_prompt, y_sample)
```

```python
import contextlib
import math

import numpy as np
import ml_dtypes

import concourse.bass as bass
import concourse.mybir as mybir
from concourse.bass_utils import run_bass_kernel_spmd

F32 = mybir.dt.float32
BF16 = mybir.dt.bfloat16
I32 = mybir.dt.int32
ALU = mybir.AluOpType
AF = mybir.ActivationFunctionType

D = 1024
DIN = 1792
DFF = 2816
EPS = 1e-6
NQH = 8
TWO_PI = 2.0 * math.pi


class Sched:
    def __init__(self, nc, estack, n_dma_sems=16):
        self.nc = nc
        self.eng = {"pe": nc.tensor, "act": nc.scalar, "dve": nc.vector,
                    "pool": nc.gpsimd, "sp": nc.sync}
        self.sem = {k: estack.enter_context(nc.semaphore("s_" + k)) for k in self.eng}
        self.cnt = {k: 0 for k in self.eng}
        self.waited = {k: {} for k in self.eng}
        self.dsems = [estack.enter_context(nc.semaphore("d%d" % i)) for i in range(n_dma_sems)]
        self.dtarget = [0] * n_dma_sems
        self.dnext = 0
        self.last_write = {}
        self.readers = {}
        self.ninstr = 0

    def _wait(self, e, dep):
        h = self.eng[e]
        if dep[0] == "dma":
            _, si, tgt = dep
            key = ("dma", si)
            if self.waited[e].get(key, 0) >= tgt:
                return
            h.wait_ge(self.dsems[si], tgt)
            self.waited[e][key] = tgt
        else:
            oe, idx = dep
            if oe == e and e == "pe":
                return
            if self.waited[e].get(oe, 0) >= idx:
                return
            h.wait_ge(self.sem[oe], idx)
            self.waited[e][oe] = idx
        self.ninstr += 1

    def _deps(self, e, reads, writes):
        deps = []
        for b in reads:
            lw = self.last_write.get(b)
            if lw is not None:
                deps.append(lw)
        for b in writes:
            lw = self.last_write.get(b)
            if lw is not None and lw[0] != e:
                deps.append(lw)
            for d in self.readers.get(b, {}).values():
                if d[0] != e:
                    deps.append(d)
        return deps

    def join(self, q="sp"):
        for e in self.eng:
            for si, t in enumerate(self.dtarget):
                if t > 0:
                    self._wait(e, ("dma", si, t))
            for oe in self.eng:
                if oe != e and self.cnt[oe] > 0:
                    self._wait(e, (oe, self.cnt[oe]))

    def _record(self, me, reads, writes):
        key = me[0] if me[0] != "dma" else ("dma", me[1])
        for b in reads:
            self.readers.setdefault(b, {})[key] = me
        for b in writes:
            self.last_write[b] = me
            self.readers[b] = {}

    def op(self, e, fn, reads=(), writes=()):
        for d in self._deps(e, reads, writes):
            self._wait(e, d)
        inst = fn(self.eng[e])
        self.cnt[e] += 1
        inst.then_inc(self.sem[e], 1)
        self._record((e, self.cnt[e]), reads, writes)
        self.ninstr += 1

    def dma(self, q, out, in_, reads=(), writes=()):
        for d in self._deps(None, reads, writes):
            self._wait(q, d)
        si = self.dnext
        self.dnext = (self.dnext + 1) % len(self.dsems)
        if self.dtarget[si] > 0:
            self._wait(q, ("dma", si, self.dtarget[si]))
        inst = self.eng[q].dma_start(out=out, in_=in_)
        self.dtarget[si] += 16
        inst.then_inc(self.dsems[si], 16)
        self._record(("dma", si, self.dtarget[si]), reads, writes)
        self.ninstr += 1

    def collective(self, estack, kind, in_ap, out_ap, scratch_ap, reads=(), writes=()):
        for d in self._deps("pool", reads, writes):
            self._wait("pool", d)
        g = self.nc.gpsimd
        csem = estack.enter_context(self.nc.semaphore("cc%d" % self.ninstr))
        g.collective_compute(kind, mybir.AluOpType.bypass, replica_groups=[list(range(8))],
                             ins=[in_ap], outs=[out_ap]).then_inc(csem)
        g.wait_ge(csem, 1)
        self.op("pool", lambda e: e.memset(scratch_ap, 0.0), reads=reads, writes=list(writes) + ["ccscratch"])

    def finish(self, q="sp"):
        for si, t in enumerate(self.dtarget):
            if t > 0:
                self._wait(q, ("dma", si, t))
        for e in self.eng:
            if e != q and self.cnt[e] > 0:
                self._wait(q, (e, self.cnt[e]))


def rnames(base, t0, t1, gran=512):
    return ["%s:%d" % (base, i) for i in range(t0 // gran, (t1 - 1) // gran + 1)]


def q_perm():
    idx = []
    for i in range(4):
        idx.extend(range(i * 64, i * 64 + 64))
        idx.extend(range((4 + i) * 64, (4 + i) * 64 + 64))
    return np.array(idx)


def alibi_table():
    h = np.arange(NQH, dtype=np.float32)
    slopes = (2.0 ** (-8.0 * (h + 1.0) / NQH)).astype(np.float32)
    j = np.arange(128)[:, None]
    q = np.arange(128)[None, :]
    out = np.zeros((128, 3, 2, 4, 128), np.float32)
    for rel in range(3):
        dist = np.abs(q - (j + (rel - 1) * 128)).astype(np.float32)
        for kvh in range(2):
            for i in range(4):
                b = -slopes[4 * kvh + i] * dist
                b = np.where(dist <= 128, b, -1e30)
                out[:, rel, kvh, i, :] = b
    return out


def fft_tables(L):
    J = L // 128
    N1 = 2 * J
    N = 128 * N1
    bf = ml_dtypes.bfloat16
    Jp = np.arange(J)[:, None].astype(np.float64)
    k1 = np.arange(N1)[None, :].astype(np.float64)
    ang = TWO_PI * Jp * k1 / N1
    FA = np.concatenate([np.cos(ang), -np.sin(ang)], axis=1)
    j = np.arange(128).astype(np.float64)
    k2 = np.arange(128).astype(np.float64)
    kk1 = np.arange(N1).astype(np.float64)
    ph = TWO_PI * np.outer(j, k2) / 128.0
    TB = np.stack([np.cos(ph), -np.sin(ph), np.sin(ph)], axis=1)
    pt = TWO_PI * np.outer(j, kk1) / N
    TW = np.stack([np.cos(pt), np.cos(pt), -np.sin(pt), -np.sin(pt)], axis=1).astype(np.float32)
    a = TWO_PI * np.outer(k2, j) / 128.0
    Gr, Gi = np.cos(a), np.sin(a)
    G = np.stack([np.concatenate([Gr, Gi], 1), np.concatenate([-Gi, Gr], 1)], axis=1)
    n1 = np.arange(128).astype(np.float64)
    n2 = np.arange(J).astype(np.float64)
    e = ((n1[:, None, None] + 128 * n2[None, None, :]) * kk1[None, :, None]) % N
    psi = TWO_PI * e / N
    TD = np.stack([np.cos(psi), -np.sin(psi)], axis=2)
    return (FA.astype(np.float32).astype(bf), TB.astype(np.float32).astype(bf),
            G.astype(np.float32).astype(bf), TD.astype(np.float32).astype(bf), TW)


def z_table(L):
    t_norm = np.linspace(0.0, 1.0, L, dtype=np.float32)[:, None]
    n = np.arange(L, dtype=np.float32)[:, None]
    bands = np.linspace(1e-4, 15, 16, dtype=np.float32)[None, :]
    ang = (np.float32(TWO_PI) * n * bands / np.float32(L)).astype(np.float32)
    z = np.concatenate([t_norm, np.cos(ang), -np.sin(ang)], axis=-1).astype(np.float32)
    return np.ascontiguousarray(z.T)


def neg_deltas():
    max_decay = math.log(1e-2) / 0.3
    min_decay = math.log(1e-2) / 1.5
    d = np.abs(np.linspace(min_decay, max_decay, 256, dtype=np.float32))
    return (-d).reshape(1, 256).astype(np.float32)


PP = {}
_o = 0
for _n, _w in [("g1", 8), ("g2", 8), ("gq", 1), ("gk", 1), ("sink", 8), ("pscale", 2),
               ("hcw", 18), ("hcb", 6), ("fb1", 1), ("ff1", 1), ("fb2", 1), ("ff2", 1),
               ("hyb", 4), ("gout", 8), ("fcw", 132), ("fcb", 44)]:
    PP[_n] = (_o, _w)
    _o += _w
NPP = _o


def pack_pp(inp, l):
    pp = np.zeros((128, NPP), np.float32)

    def put(name, arr):
        o, w = PP[name]
        pp[:, o:o + w] = arr.reshape(128, w)

    qp = q_perm()
    put("g1", inp["norm1_g"][l].reshape(8, 128).T)
    put("g2", inp["norm2_g"][l].reshape(8, 128).T)
    put("gq", np.tile(inp["q_norm_g"][l], 2))
    put("gk", np.tile(inp["k_norm_g"][l], 2))
    put("sink", np.tile(inp["attn_sink"][l][None, :], (128, 1)))
    put("pscale", inp["pool_scale"][l].reshape(2, 128).T)
    put("hcw", inp["hy_conv_w"][l].reshape(3, 6, 128).transpose(2, 1, 0))
    put("hcb", inp["hy_conv_b"][l].reshape(6, 128).T)
    z64 = np.zeros(64, np.float32)
    put("fb1", np.concatenate([inp["filt_b1"][l], z64]))
    put("ff1", np.concatenate([inp["filt_freq1"][l], z64]))
    put("fb2", np.concatenate([inp["filt_b2"][l], z64]))
    put("ff2", np.concatenate([inp["filt_freq2"][l], z64]))
    put("hyb", inp["hy_bias"][l].reshape(2, 2, 128).transpose(2, 0, 1))
    g = inp["out_norm_g"][l].copy()
    g[:512] = g[:512][qp]
    put("gout", g.reshape(8, 128).T)
    put("fcw", inp["ffn_conv_w"][l].reshape(3, 44, 128).transpose(2, 1, 0))
    put("fcb", inp["ffn_conv_b"][l].reshape(44, 128).T)
    return pp


def build_program(Ls, depth, dbg=(), stop_after=None, shard=None):
    nc = bass.Bass("TRN2", target_bir_lowering=False)
    nseq = len(Ls)
    shard = list(shard) if shard is not None else [False] * nseq
    uL = sorted(set(L for L, sh in zip(Ls, shard) if not sh))
    uLs = sorted(set(L for L, sh in zip(Ls, shard) if sh))
    allL = sorted(set(Ls))
    NCO = 32

    def din(name, shape, dt=F32):
        return nc.dram_tensor(name, list(shape), dt, kind="ExternalInput").ap()

    def dscr(name, shape, dt=F32):
        kind = "ExternalOutput" if name in dbg else "Internal"
        return nc.dram_tensor(name, list(shape), dt, kind=kind).ap()

    x_in = [din("x%d" % s, [Ls[s], D]) for s in range(nseq)]
    y_out = [nc.dram_tensor("y%d" % s, [Ls[s], D], F32, kind="ExternalOutput").ap() for s in range(nseq)]
    w_in = din("w_in", [depth, D, DIN])
    w_out = din("w_out", [depth, D, D])
    w_fi = din("w_ffn_in", [depth, D, 2 * DFF])
    w_fo = din("w_ffn_out", [depth, DFF, D])
    pool_w = din("pool_w", [depth, 4, 64, 64])
    fw1 = din("filt_w1", [depth, 33, 64])
    fw2 = din("filt_w2", [depth, 64, 64])
    fw3 = din("filt_w3", [depth, 64, 1024])
    pp_in = din("pp", [depth, 128, NPP])
    hyb_raw = din("hyb_raw", [depth, 2, 256])
    c_alibi = din("c_alibi", [128, 3 * 2 * 4 * 128])
    c_ndelta = din("c_ndelta", [1, 256])
    c_ident = din("c_ident", [128, 128])
    c_bones = din("c_bones", [128, 128])
    cz, cFA, cTB, cG, cTD, cTW = {}, {}, {}, {}, {}, {}
    for L in allL:
        J = L // 128
        N1 = 2 * J
        cz[L] = din("c_z%d" % L, [33, L])
        cFA[L] = din("c_FA%d" % L, [J, 2 * N1], BF16)
        cTB[L] = din("c_TB%d" % L, [128, 3, 128], BF16)
        cTW[L] = din("c_TW%d" % L, [128, 4, N1])
        cG[L] = din("c_G%d" % L, [128, 2, 256], BF16)
        cTD[L] = din("c_TD%d" % L, [128, N1, 2, J], BF16)

    XA = [dscr("XA%d" % s, [D, Ls[s]]) for s in range(nseq)]
    XB = [dscr("XB%d" % s, [D, Ls[s]]) for s in range(nseq)]
    QK = [dscr("QK%d" % s, [640, Ls[s]], BF16) for s in range(nseq)]
    VV = [dscr("VV%d" % s, [Ls[s], 128], BF16) for s in range(nseq)]
    PH = [dscr("PH%d" % s, [1024, Ls[s]]) for s in range(nseq)]
    CT = [dscr("CT%d" % s, [D, Ls[s]], BF16) for s in range(nseq)]
    ZT = [dscr("ZT%d" % s, [256, Ls[s]], BF16) for s in range(nseq)]
    UU = [dscr("UU%d" % s, [256, Ls[s]]) for s in range(nseq)]
    GT = [dscr("GT%d" % s, [512, Ls[s]]) for s in range(nseq)]
    YT = [dscr("YT%d" % s, [256, Ls[s]]) for s in range(nseq)]
    FILT = {L: dscr("FILT%d" % L, [4, 256, L], BF16) for L in uL}
    KS = {}
    for L in uL:
        N1 = 2 * (L // 128)
        KS[L] = dscr("KS%d" % L, [2, 128, 256 * N1 * 2], BF16)
    FILTs = {L: dscr("FILTs%d" % L, [128, L], BF16) for L in uLs}
    KSs = {L: dscr("KSs%d" % L, [2, 128, NCO * 2 * (L // 128) * 2], BF16) for L in uLs}
    shs = [s_ for s_ in range(nseq) if shard[s_]]
    ZTo = {s_: dscr("ZTo%d" % s_, [NCO, Ls[s_]], BF16) for s_ in shs}
    Uo = {s_: dscr("Uo%d" % s_, [NCO, Ls[s_]]) for s_ in shs}
    Go = {s_: dscr("Go%d" % s_, [2, NCO, Ls[s_]]) for s_ in shs}
    YTo = {s_: dscr("YTo%d" % s_, [NCO, Ls[s_]]) for s_ in shs}
    Co = {s_: nc.dram_tensor("Co%d" % s_, [NCO, Ls[s_]], F32) for s_ in shs}
    Cg = {s_: nc.dram_tensor("Cg%d" % s_, [256, Ls[s_]], F32) for s_ in shs}
    WB_in = dscr("WB_in", [D, DIN], BF16)
    WB_out = dscr("WB_out", [D, D], BF16)
    WB_fi = dscr("WB_fi", [D, 2 * DFF], BF16)
    WB_fo = dscr("WB_fo", [DFF, D], BF16)

    with contextlib.ExitStack() as es:
        S = Sched(nc, es)

        uid = [0]

        def sbt(st, name, shape, dt):
            uid[0] += 1
            return st.enter_context(nc.sbuf_tensor("%s_%d" % (name, uid[0]), list(shape), dt))

        dumped = set()

        def dump(name, ap, shape, dt, reads):
            if name not in dbg or name in dumped:
                return
            dumped.add(name)
            d = nc.dram_tensor(name, list(shape), dt, kind="ExternalOutput").ap()
            S.dma("sp", d, ap, reads=reads)

        pid = nc.partition_id()
        ccs = sbt(es, "ccs", [128, 1], F32)

        identF = sbt(es, "identF", [128, 128], F32)
        identB = sbt(es, "identB", [128, 128], BF16)
        onesB = sbt(es, "onesB", [128, 128], BF16)
        bonesB = sbt(es, "bonesB", [128, 128], BF16)
        epsT = sbt(es, "epsT", [128, 1], F32)
        ppT = sbt(es, "ppT", [128, NPP], F32)
        esink = sbt(es, "esink", [128, 8], F32)
        fb1f = sbt(es, "fb1f", [128, 2], F32)
        S.dma("sp", identF[:], c_ident, writes=["identF"])
        S.dma("pool", identB[:], c_ident, writes=["identB"])
        S.dma("pool", bonesB[:], c_bones, writes=["bonesB"])
        S.op("pool", lambda e: e.memset(onesB[:], 1.0), writes=["onesB"])
        S.op("pool", lambda e: e.memset(epsT[:], EPS), writes=["epsT"])

        psb = [es.enter_context(nc.psum_tensor("ps%d" % i, [128, 512], F32)) for i in range(7)]
        psT = es.enter_context(nc.psum_tensor("psT", [128, 1024], BF16))
        psctr = [0]

        def nextps(lo=0, hi=7):
            i = lo + psctr[0] % (hi - lo)
            psctr[0] += 1
            return i

        def ppc(name, j=0, n=1):
            o, w = PP[name]
            return ppT[:, o + j:o + j + n]

        def rsqrt_ln(out_ap, in_ap, scale, rd, wr, npart=128):
            S.op("act", lambda e: e.activation(out=out_ap, in_=in_ap, func=AF.Ln,
                                               bias=epsT[0:npart, :], scale=scale), reads=rd, writes=wr)
            S.op("act", lambda e: e.activation(out=out_ap, in_=out_ap, func=AF.Exp, scale=-0.5),
                 reads=wr, writes=wr)

        def load_layer_params(l):
            S.dma("sp", ppT[:], pp_in[l], writes=["ppT"])
            o, w = PP["sink"]
            S.op("act", lambda e: e.activation(out=esink[:], in_=ppT[:, o:o + w], func=AF.Exp),
                 reads=["ppT"], writes=["esink"])
            S.op("dve", lambda e: e.tensor_tensor(out=fb1f[:, 0:1], in0=ppc("fb1"), in1=ppc("ff1"), op=ALU.mult),
                 reads=["ppT"], writes=["fb1f"])
            S.op("dve", lambda e: e.tensor_tensor(out=fb1f[:, 1:2], in0=ppc("fb2"), in1=ppc("ff2"), op=ALU.mult),
                 reads=["ppT", "fb1f"], writes=["fb1f"])
            for r in range(0, D, 128):
                S.dma("pool", WB_in[r:r + 128, :], w_in[l, r:r + 128, :], writes=["WB_in:%d" % r])
                S.dma("pool", WB_out[r:r + 128, :], w_out[l, r:r + 128, :], writes=["WB_out:%d" % r])
                S.dma("pool", WB_fi[r:r + 128, :], w_fi[l, r:r + 128, :], writes=["WB_fi:%d" % r])
            for r in range(0, DFF, 128):
                S.dma("pool", WB_fo[r:r + 128, :], w_fo[l, r:r + 128, :], writes=["WB_fo:%d" % r])

        def phase1(l, s, first):
            L = Ls[s]
            nch = L // 512
            XAv = XA[s].rearrange("(k p) t -> p k t", p=128)
            QKv = QK[s].rearrange("(k p) t -> p k t", p=128)
            PHv = PH[s].rearrange("(k p) t -> p k t", p=128)
            VVv = VV[s].rearrange("(n p) f -> p n f", p=128)
            with contextlib.ExitStack() as st:
                win = sbt(st, "win", [128, 8, DIN], BF16)
                S.dma("sp", win[:], WB_in.rearrange("(k p) n -> p k n", p=128), reads=["WB_in:%d" % r for r in range(0, D, 128)], writes=["win"])
                xt = [sbt(st, "xt%d" % i, [128, 8, 512], F32) for i in range(2)]
                xtok = [sbt(st, "xtok%d" % i, [128, 4, D], F32) for i in range(2)] if first else None
                sq = sbt(st, "sq", [128, 8, 512], BF16)
                hT = [sbt(st, "hT%d" % i, [128, 8, 512], BF16) for i in range(2)]
                rstd = sbt(st, "rstd", [128, 512], F32)
                sqh = [sbt(st, "sqh%d" % i, [128, 512], BF16) for i in range(2)]
                rq = [sbt(st, "rq%d" % i, [128, 512], F32) for i in range(2)]
                qko = [sbt(st, "qko%d" % i, [128, 5, 512], BF16) for i in range(2)]
                vo = [sbt(st, "vo%d" % i, [128, 4, 128], BF16) for i in range(2)]
                pho = [sbt(st, "pho%d" % i, [128, 8, 512], F32) for i in range(2)]

                def load(c):
                    b = c % 2
                    if first:
                        S.dma("sp", xtok[b][:], x_in[s][c * 512:(c + 1) * 512, :].rearrange("(n p) d -> p n d", p=128),
                              writes=["xtok%d" % b])
                    else:
                        S.dma("sp", xt[b][:], XAv[:, :, c * 512:(c + 1) * 512],
                              reads=rnames("XA%d" % s, c * 512, (c + 1) * 512), writes=["xt%d" % b])

                load(0)
                for c in range(nch):
                    b = c % 2
                    if c + 1 < nch:
                        load(c + 1)
                    if first:
                        for k in range(8):
                            pi = nextps(0, 2)
                            for n in range(4):
                                S.op("pe", lambda e, pi=pi, n=n, k=k: e.transpose(
                                    psb[pi][:, n * 128:(n + 1) * 128], xtok[b][:, n, k * 128:(k + 1) * 128], identF[:]),
                                    reads=["xtok%d" % b, "identF"], writes=["ps%d" % pi])
                            eng = "act" if k % 2 == 0 else "dve"
                            if eng == "act":
                                S.op("act", lambda e, pi=pi, k=k: e.activation(out=xt[b][:, k, :], in_=psb[pi][:], func=AF.Copy),
                                     reads=["ps%d" % pi], writes=["xt%d:%d" % (b, k)])
                            else:
                                S.op("dve", lambda e, pi=pi, k=k: e.tensor_copy(out=xt[b][:, k, :], in_=psb[pi][:]),
                                     reads=["ps%d" % pi], writes=["xt%d:%d" % (b, k)])
                        xtn = ["xt%d:%d" % (b, k) for k in range(8)]
                        S.dma("sp", XAv[:, :, c * 512:(c + 1) * 512], xt[b][:], reads=xtn,
                              writes=rnames("XA%d" % s, c * 512, (c + 1) * 512))
                    else:
                        xtn = ["xt%d" % b]
                    S.op("pool", lambda e: e.tensor_tensor(out=sq[:], in0=xt[b][:], in1=xt[b][:], op=ALU.mult),
                         reads=xtn, writes=["sq"])
                    pss = nextps(2, 7)
                    for k in range(8):
                        S.op("pe", lambda e, k=k: e.matmul(psb[pss][:], onesB[:], sq[:, k, :], start=(k == 0), stop=(k == 7)),
                             reads=["sq", "onesB"], writes=["ps%d" % pss])
                    rsqrt_ln(rstd[:], psb[pss][:], 1.0 / D, ["ps%d" % pss, "epsT"], ["rstd"])
                    for k in range(8):
                        S.op("dve", lambda e, k=k: e.scalar_tensor_tensor(
                            out=hT[b][:, k, :], in0=xt[b][:, k, :], scalar=ppc("g1", k), in1=rstd[:],
                            op0=ALU.mult, op1=ALU.mult), reads=xtn + ["rstd", "ppT"], writes=["hT%d" % b])

                    def proj(m, pi):
                        for k in range(8):
                            S.op("pe", lambda e, k=k: e.matmul(psb[pi][:], win[:, k, m * 128:(m + 1) * 128], hT[b][:, k, :],
                                                               start=(k == 0), stop=(k == 7)),
                                 reads=["win", "hT%d" % b], writes=["ps%d" % pi])

                    pend = None
                    for m in range(5):
                        pi = nextps(2, 7)
                        proj(m, pi)
                        hb = m % 2
                        S.op("act", lambda e, pi=pi, hb=hb: e.activation(out=sqh[hb][:], in_=psb[pi][:], func=AF.Square),
                             reads=["ps%d" % pi], writes=["sqh%d" % hb])
                        if pend is not None:
                            pend()

                        def fin(m=m, pi=pi, hb=hb):
                            p2 = nextps(2, 7)
                            S.op("pe", lambda e: e.matmul(psb[p2][:], bonesB[:], sqh[hb][:], start=True, stop=True),
                                 reads=["sqh%d" % hb, "bonesB"], writes=["ps%d" % p2])
                            rsqrt_ln(rq[hb][:], psb[p2][:], 1.0 / 64, ["ps%d" % p2, "epsT"], ["rq%d" % hb])
                            gname = "gq" if m < 4 else "gk"
                            S.op("dve", lambda e: e.scalar_tensor_tensor(
                                out=qko[b][:, m, :], in0=psb[pi][:], scalar=ppc(gname), in1=rq[hb][:],
                                op0=ALU.mult, op1=ALU.mult), reads=["ps%d" % pi, "rq%d" % hb, "ppT"], writes=["qko%d" % b])
                        pend = fin
                    pend()
                    S.dma("sp", QKv[:, :, c * 512:(c + 1) * 512], qko[b][:], reads=["qko%d" % b],
                          writes=rnames("QK%d" % s, c * 512, (c + 1) * 512))
                    pv = nextps(2, 7)
                    for n in range(4):
                        for k in range(8):
                            S.op("pe", lambda e, n=n, k=k: e.matmul(psb[pv][:, n * 128:(n + 1) * 128], hT[b][:, k, n * 128:(n + 1) * 128],
                                                                   win[:, k, 640:768], start=(k == 0), stop=(k == 7)),
                                 reads=["win", "hT%d" % b], writes=["ps%d" % pv])
                    S.op("act", lambda e: e.activation(out=vo[b][:].rearrange("p n f -> p (n f)"), in_=psb[pv][:], func=AF.Copy),
                         reads=["ps%d" % pv], writes=["vo%d" % b])
                    S.dma("sp", VVv[:, c * 4:(c + 1) * 4, :], vo[b][:], reads=["vo%d" % b],
                          writes=rnames("VV%d" % s, c * 512, (c + 1) * 512))
                    for m in range(6, 14):
                        pi = nextps(2, 7)
                        proj(m, pi)
                        if m % 2 == 0:
                            S.op("act", lambda e, pi=pi, m=m: e.activation(out=pho[b][:, m - 6, :], in_=psb[pi][:], func=AF.Copy),
                                 reads=["ps%d" % pi], writes=["pho%d:%d" % (b, m)])
                        else:
                            S.op("dve", lambda e, pi=pi, m=m: e.tensor_copy(out=pho[b][:, m - 6, :], in_=psb[pi][:]),
                                 reads=["ps%d" % pi], writes=["pho%d:%d" % (b, m)])
                    S.dma("sp", PHv[:, :, c * 512:(c + 1) * 512], pho[b][:],
                          reads=["pho%d:%d" % (b, m) for m in range(6, 14)],
                          writes=rnames("PH%d" % s, c * 512, (c + 1) * 512))

        def phase2(l, s):
            L = Ls[s]
            nch = L // 512
            nb = L // 128
            QKv = QK[s].rearrange("(k p) t -> p k t", p=128)
            VVv = VV[s].rearrange("(n p) (h d) -> p n h d", p=128, h=2)
            CTv = CT[s].rearrange("(k p) t -> p k t", p=128)
            with contextlib.ExitStack() as st:
                bias = sbt(st, "abias", [128, 3, 2, 512], F32)
                S.dma("sp", bias[:].rearrange("p a b c -> p (a b c)"), c_alibi, writes=["abias"])
                qT = [sbt(st, "qT%d" % i, [128, 4, 512], BF16) for i in range(2)]
                kT = [sbt(st, "kT%d" % i, [128, 768], BF16) for i in range(2)]
                v1 = [sbt(st, "v1%d" % i, [128, 6, 2, 65], BF16) for i in range(2)]
                for i in range(2):
                    S.op("pool", lambda e, i=i: e.memset(v1[i][:], 1.0), writes=["v1%d:0" % i, "v1%d:1" % i])
                sT = [sbt(st, "sT%d" % i, [128, 512], F32) for i in range(3)]
                pT = [sbt(st, "pT%d" % i, [128, 512], BF16) for i in range(6)]
                den = sbt(st, "den", [128, 8], F32)
                aa = [sbt(st, "aa%d" % i, [128, 4, 2, 64], F32) for i in range(2)]
                junk = sbt(st, "junk", [128, 512], F32)
                ssa = sbt(st, "ssa", [128, 1], F32)
                an = [sbt(st, "an%d" % i, [128, 512], BF16) for i in range(2)]
                catA = [sbt(st, "catA%d" % i, [128, 4, 512], BF16) for i in range(2)]

                def load(c):
                    b = c % 2
                    S.dma("sp", qT[b][:], QKv[:, 0:4, c * 512:(c + 1) * 512],
                          reads=rnames("QK%d" % s, c * 512, (c + 1) * 512), writes=["qT%d" % b])
                    t0 = max(0, c * 512 - 128)
                    t1 = min(L, c * 512 + 640)
                    o0 = t0 - (c * 512 - 128)
                    S.dma("sp", kT[b][:, o0:o0 + (t1 - t0)], QK[s][512:640, t0:t1],
                          reads=rnames("QK%d" % s, t0, t1), writes=["kT%d" % b])
                    for hh in range(2):
                        S.dma("sp", v1[b][:, o0 // 128:o0 // 128 + (t1 - t0) // 128, hh, 0:64], VVv[:, t0 // 128:t1 // 128, hh, :],
                              reads=rnames("VV%d" % s, t0, t1), writes=["v1%d:%d" % (b, hh)])

                pT2 = pT + [sbt(st, "pTb%d" % i, [128, 512], BF16) for i in range(6)]
                den2 = [den, sbt(st, "den_b", [128, 8], F32)]
                ssa2 = [ssa, sbt(st, "ssa_b", [128, 1], F32)]
                sT4 = sT + [sbt(st, "sT3", [128, 512], F32)]
                sctr = [0]

                def kbs_of(g):
                    return [kb for kb in (g - 1, g, g + 1) if 0 <= kb < nb]

                def stage1(c, qb):
                    b = c % 2
                    g = 4 * c + qb
                    st_ = g % 2
                    for kvh in range(2):
                        pr = slice(64 * kvh, 64 * kvh + 64)
                        for kb in kbs_of(g):
                            rel = kb - g + 1
                            slot = kb - (4 * c - 1)
                            pi = nextps(0, 3)
                            for i in range(4):
                                S.op("pe", lambda e, pi=pi, i=i, slot=slot: e.matmul(
                                    psb[pi][:, i * 128:(i + 1) * 128], kT[b][pr, slot * 128:(slot + 1) * 128],
                                    qT[b][pr, i, qb * 128:(qb + 1) * 128], start=True, stop=True),
                                    reads=["kT%d" % b, "qT%d" % b], writes=["ps%d" % pi])
                            si = sctr[0] % 4
                            sctr[0] += 1
                            S.op("dve", lambda e, pi=pi, si=si, rel=rel: e.scalar_tensor_tensor(
                                out=sT4[si][:], in0=psb[pi][:], scalar=0.125, in1=bias[:, rel, kvh, :],
                                op0=ALU.mult, op1=ALU.add), reads=["ps%d" % pi, "abias"], writes=["sT%d" % si])
                            pidx = st_ * 6 + kvh * 3 + rel
                            S.op("act", lambda e, si=si, pidx=pidx: e.activation(out=pT2[pidx][:], in_=sT4[si][:], func=AF.Exp),
                                 reads=["sT%d" % si], writes=["pT%d" % pidx])

                def stage2(c, qb):
                    b = c % 2
                    g = 4 * c + qb
                    st_ = g % 2
                    ab = g % 2
                    kbs = kbs_of(g)
                    pO = [3 + 2 * (g % 2), 4 + 2 * (g % 2)]
                    den_, ssa_ = den2[ab], ssa2[ab]
                    dn, sn = "den%d" % ab, "ssa%d" % ab
                    for kvh in range(2):
                        for i in range(4):
                            for n_, kb in enumerate(kbs):
                                rel = kb - g + 1
                                slot = kb - (4 * c - 1)
                                pidx = st_ * 6 + kvh * 3 + rel
                                S.op("pe", lambda e, i=i, slot=slot, pidx=pidx, n_=n_, kvh=kvh: e.matmul(
                                    psb[pO[kvh]][:, i * 65:(i + 1) * 65], pT2[pidx][:, i * 128:(i + 1) * 128],
                                    v1[b][:, slot, kvh, :], start=(n_ == 0), stop=(n_ == len(kbs) - 1)),
                                    reads=["pT%d" % pidx, "v1%d:%d" % (b, kvh)], writes=["ps%d" % pO[kvh]])
                    for kvh in range(2):
                        pov = psb[pO[kvh]][:, 0:260].rearrange("p (i d) -> p i d", d=65)
                        S.op("dve", lambda e, kvh=kvh, pov=pov: e.tensor_tensor(
                            out=den_[:, kvh * 4:(kvh + 1) * 4], in0=pov[:, :, 64], in1=esink[:, kvh * 4:(kvh + 1) * 4], op=ALU.add),
                            reads=["ps%d" % pO[kvh], "esink"], writes=[dn])
                    S.op("dve", lambda e: e.reciprocal(out=den_[:], in_=den_[:]), reads=[dn], writes=[dn])
                    for kvh in range(2):
                        pov = psb[pO[kvh]][:, 0:260].rearrange("p (i d) -> p i d", d=65)
                        for i in range(4):
                            S.op("dve", lambda e, kvh=kvh, i=i, pov=pov: e.tensor_scalar(
                                out=aa[ab][:, i, kvh, :], in0=pov[:, i, 0:64], scalar1=den_[:, kvh * 4 + i:kvh * 4 + i + 1],
                                scalar2=None, op0=ALU.mult), reads=["ps%d" % pO[kvh], dn], writes=["aa%d" % ab])
                    aflat = aa[ab][:].rearrange("p i k d -> p (i k d)")
                    S.op("act", lambda e: e.activation(out=junk[:], in_=aflat, func=AF.Square, accum_out=ssa_[:]),
                         reads=["aa%d" % ab], writes=["junk", sn])
                    rsqrt_ln(ssa_[:], ssa_[:], 1.0 / 512, [sn, "epsT"], [sn])
                    S.op("dve", lambda e: e.tensor_scalar(out=an[ab][:], in0=aflat, scalar1=ssa_[:, 0:1], scalar2=None, op0=ALU.mult),
                         reads=["aa%d" % ab, sn], writes=["an%d" % ab])
                    for i in range(4):
                        S.op("pe", lambda e, i=i: e.transpose(psT[:, i * 128:(i + 1) * 128], an[ab][:, i * 128:(i + 1) * 128], identB[:]),
                             reads=["an%d" % ab, "identB"], writes=["psT"])
                    for i in range(4):
                        S.op("dve", lambda e, i=i: e.tensor_scalar(
                            out=catA[b][:, i, qb * 128:(qb + 1) * 128], in0=psT[:, i * 128:(i + 1) * 128],
                            scalar1=ppc("gout", i), scalar2=None, op0=ALU.mult),
                            reads=["psT", "ppT"], writes=["catA%d" % b])
                    if qb == 3:
                        S.dma("sp", CTv[:, 0:4, c * 512:(c + 1) * 512], catA[b][:], reads=["catA%d" % b],
                              writes=rnames("CTa%d" % s, c * 512, (c + 1) * 512))

                blocks = [(c, qb) for c in range(nch) for qb in range(4)]
                load(0)
                if nch > 1:
                    load(1)
                stage1(*blocks[0])
                for i_, (c, qb) in enumerate(blocks):
                    if i_ + 1 < len(blocks):
                        stage1(*blocks[i_ + 1])
                    stage2(c, qb)
                    if qb == 3 and c + 2 < nch:
                        load(c + 2)

        def phase3(l, s):
            L = Ls[s]
            WO = 496
            PHv = PH[s].rearrange("(k p) t -> p k t", p=128)
            CTv = CT[s].rearrange("(k p) t -> p k t", p=128)
            with contextlib.ExitStack() as st:
                wbd = sbt(st, "wbd", [128, 2, 128], BF16)
                S.op("pool", lambda e: e.memset(wbd[:], 0.0), writes=["wbd"])
                for g in range(4):
                    ci, hf = g // 2, g % 2
                    S.dma("pool", wbd[64 * hf:64 * hf + 64, ci, 64 * hf:64 * hf + 64], pool_w[l, g], reads=["wbd"], writes=["wbd%d" % g])
                wbdn = ["wbd%d" % g for g in range(4)]
                u = [sbt(st, "pu%d" % i, [128, 2, 512], F32) for i in range(2)]
                a1_ = [sbt(st, "pa1_%d" % i, [128, 2, 512], F32) for i in range(2)]
                a2_ = [sbt(st, "pa2_%d" % i, [128, 2, 512], F32) for i in range(2)]
                t1_ = [sbt(st, "pt1_%d" % i, [128, 2, 512], F32) for i in range(2)]
                dT_ = [sbt(st, "pdT_%d" % i, [128, 2, 512], BF16) for i in range(2)]
                yb_ = [sbt(st, "pyb_%d" % i, [128, 2, 512], F32) for i in range(2)]
                sqb_ = [sbt(st, "psqb_%d" % i, [128, 2, 512], BF16) for i in range(2)]
                rsb_ = [sbt(st, "prsb_%d" % i, [128, 512], F32) for i in range(2)]
                cb = [sbt(st, "pcb%d" % i, [128, 2, 512], BF16) for i in range(2)]
                wins = list(range(0, L, WO))

                def load(wi):
                    b = wi % 2
                    s0 = wins[wi]
                    n = min(WO, L - s0)
                    t0, t1_ = max(0, s0 - 8), min(L, s0 + n + 8)
                    if t0 > s0 - 8 or t1_ < s0 + n + 8 or n < WO:
                        S.op("pool", lambda e: e.memset(u[b][:], 0.0), writes=["pu%d" % b])
                    o0 = t0 - (s0 - 8)
                    S.dma("sp", u[b][:, :, o0:o0 + (t1_ - t0)], PHv[:, 0:2, t0:t1_],
                          reads=rnames("PH%d" % s, t0, t1_), writes=["pu%d" % b])

                load(0)
                for wi, s0 in enumerate(wins):
                    b = wi % 2
                    n = min(WO, L - s0)
                    if wi + 1 < len(wins):
                        load(wi + 1)
                    W = n + 16
                    ub = u[b]
                    a1, a2, t1, dT, yb, sqb, rsb = a1_[b], a2_[b], t1_[b], dT_[b], yb_[b], sqb_[b], rsb_[b]
                    S.op("dve", lambda e: e.tensor_tensor(out=a1[:, :, 0:W - 1], in0=ub[:, :, 0:W - 1], in1=ub[:, :, 1:W], op=ALU.add),
                         reads=["pu%d" % b], writes=["pa1_%d" % b])
                    S.op("dve", lambda e: e.tensor_tensor(out=a2[:, :, 0:W - 3], in0=a1[:, :, 0:W - 3], in1=a1[:, :, 2:W - 1], op=ALU.add),
                         reads=["pa1_%d" % b], writes=["pa2_%d" % b])
                    S.op("dve", lambda e: e.tensor_tensor(out=a1[:, 1, 0:W - 7], in0=a2[:, 1, 0:W - 7], in1=a2[:, 1, 4:W - 3], op=ALU.add),
                         reads=["pa2_%d" % b, "pa1_%d" % b], writes=["pa1b_%d" % b])
                    S.op("dve", lambda e: e.tensor_tensor(out=a2[64:128, 1, 0:W - 15], in0=a1[64:128, 1, 0:W - 15], in1=a1[64:128, 1, 8:W - 7], op=ALU.add),
                         reads=["pa1b_%d" % b, "pa2_%d" % b], writes=["pa2b_%d" % b])
                    srcs = [(a1, 0, 0, 7, 2, "pa1_%d" % b), (a2, 0, 1, 6, 4, "pa2_%d" % b), (a1, 1, 0, 4, 8, "pa1b_%d" % b), (a2, 1, 1, 0, 16, "pa2b_%d" % b)]
                    for (src, ci, hf, off, w, nm) in srcs:
                        pr = slice(64 * hf, 64 * hf + 64)
                        S.op("dve", lambda e, src=src, ci=ci, pr=pr, off=off, w=w: e.scalar_tensor_tensor(
                            out=t1[pr, ci, 0:n], in0=src[pr, ci, off:off + n], scalar=1.0 / w, in1=ub[pr, ci, 8:8 + n],
                            op0=ALU.mult, op1=ALU.subtract), reads=[nm, "pu%d" % b, "pa1_%d" % b, "pa2_%d" % b], writes=["pt1_%d" % b])
                        h = w // 2
                        edge = []
                        for tok in list(range(0, h)) + list(range(L - h, L)):
                            lo = max(0, tok - h)
                            hi = min(L, tok - h + w)
                            cnt = hi - lo
                            if cnt != w and s0 <= tok < s0 + n:
                                edge.append((tok - s0, cnt))
                        for (col, cnt) in edge:
                            S.op("dve", lambda e, src=src, ci=ci, pr=pr, off=off, col=col, cnt=cnt: e.scalar_tensor_tensor(
                                out=t1[pr, ci, col:col + 1], in0=src[pr, ci, off + col:off + col + 1], scalar=1.0 / cnt,
                                in1=ub[pr, ci, 8 + col:9 + col], op0=ALU.mult, op1=ALU.subtract),
                                reads=[nm, "pu%d" % b, "pt1_%d" % b], writes=["pt1_%d" % b])
                    S.op("dve", lambda e: e.tensor_copy(out=dT[:, :, 0:n], in_=t1[:, :, 0:n]), reads=["pt1_%d" % b], writes=["pdT_%d" % b])
                    dump("d_t1", t1[:], [128, 2, 512], F32, ["pt1_%d" % b])
                    dump("d_a1", a1[:], [128, 2, 512], F32, ["pa1_%d" % b, "pa1b_%d" % b])
                    dump("d_u", ub[:], [128, 2, 512], F32, ["pu%d" % b])
                    dump("d_wbd", wbd[:], [128, 2, 128], BF16, wbdn)
                    pis = []
                    for ci in range(2):
                        pi = nextps(0, 7)
                        pis.append(pi)
                        S.op("pe", lambda e, ci=ci, pi=pi: e.matmul(psb[pi][:, 0:n], wbd[:, ci, :], dT[:, ci, 0:n], start=True, stop=True),
                             reads=["pdT_%d" % b] + wbdn, writes=["ps%d" % pi])
                        S.op("act", lambda e, ci=ci, pi=pi: e.activation(out=yb[:, ci, 0:n], in_=psb[pi][:, 0:n], func=AF.Identity,
                                                                         scale=ppc("pscale", ci)),
                             reads=["ps%d" % pi, "ppT"], writes=["pyb%d_%d" % (ci, b)])
                    ybn = ["pyb0_%d" % b, "pyb1_%d" % b]
                    S.op("dve", lambda e: e.tensor_tensor(out=sqb[:, :, 0:n], in0=yb[:, :, 0:n], in1=yb[:, :, 0:n], op=ALU.mult),
                         reads=ybn, writes=["psqb_%d" % b])
                    pi = nextps(0, 7)
                    for ci in range(2):
                        S.op("pe", lambda e, ci=ci: e.matmul(psb[pi][:, 0:n], onesB[:], sqb[:, ci, 0:n], start=(ci == 0), stop=(ci == 1)),
                             reads=["psqb_%d" % b, "onesB"], writes=["ps%d" % pi])
                    rsqrt_ln(rsb[:, 0:n], psb[pi][:, 0:n], 1.0 / 256, ["ps%d" % pi, "epsT"], ["prsb_%d" % b])
                    dump("d_yb", yb[:], [128, 2, 512], F32, ybn)
                    dump("d_rsb", rsb[:], [128, 512], F32, ["prsb_%d" % b])
                    for ci in range(2):
                        S.op("dve", lambda e, ci=ci: e.scalar_tensor_tensor(
                            out=cb[b][:, ci, 0:n], in0=yb[:, ci, 0:n], scalar=ppc("gout", 4 + ci), in1=rsb[:, 0:n],
                            op0=ALU.mult, op1=ALU.mult), reads=ybn + ["prsb_%d" % b, "ppT"], writes=["pcb%d" % b])
                    S.dma("sp", CTv[:, 4:6, s0:s0 + n], cb[b][:, :, 0:n], reads=["pcb%d" % b],
                          writes=["CTb%d:%d" % (s, wi)])
                return len(wins)

        def phase4a(l, s):
            L = Ls[s]
            WO = 510
            PHv = PH[s].rearrange("(k p) t -> p k t", p=128)
            ZTv = ZT[s].rearrange("(k p) t -> p k t", p=128)
            UUv = UU[s].rearrange("(k p) t -> p k t", p=128)
            GTv = GT[s].rearrange("(k p) t -> p k t", p=128)
            with contextlib.ExitStack() as st:
                hin = [sbt(st, "hin%d" % i, [128, 6, 512], F32) for i in range(2)]
                ta = sbt(st, "hta", [128, 6, 512], F32)
                ho = [sbt(st, "hho%d" % i, [128, 6, 512], F32) for i in range(2)]
                hz = [sbt(st, "hhz%d" % i, [128, 2, 512], BF16) for i in range(2)]
                wins = list(range(0, L, WO))

                def load(wi):
                    b = wi % 2
                    s0 = wins[wi]
                    n = min(WO, L - s0)
                    t0, t1_ = max(0, s0 - 1), min(L, s0 + n + 1)
                    if t0 > s0 - 1 or t1_ < s0 + n + 1:
                        S.op("pool", lambda e: e.memset(hin[b][:], 0.0), writes=["hin%d" % b])
                    o0 = t0 - (s0 - 1)
                    S.dma("sp", hin[b][:, :, o0:o0 + (t1_ - t0)], PHv[:, 2:8, t0:t1_],
                          reads=rnames("PH%d" % s, t0, t1_), writes=["hin%d" % b])

                load(0)
                for wi, s0 in enumerate(wins):
                    b = wi % 2
                    n = min(WO, L - s0)
                    if wi + 1 < len(wins):
                        load(wi + 1)
                    o, _ = PP["hcw"]
                    for k in range(6):
                        w0 = ppT[:, o + 3 * k:o + 3 * k + 1]
                        w1 = ppT[:, o + 3 * k + 1:o + 3 * k + 2]
                        w2 = ppT[:, o + 3 * k + 2:o + 3 * k + 3]
                        S.op("act", lambda e, k=k, w1=w1: e.activation(out=ta[:, k, 0:n], in_=hin[b][:, k, 1:n + 1], func=AF.Identity,
                                                                      bias=ppc("hcb", k), scale=w1),
                             reads=["hin%d" % b, "ppT"], writes=["hta%d" % k])
                        S.op("dve", lambda e, k=k, w0=w0: e.scalar_tensor_tensor(
                            out=ta[:, k, 0:n], in0=hin[b][:, k, 0:n], scalar=w0, in1=ta[:, k, 0:n], op0=ALU.mult, op1=ALU.add),
                            reads=["hin%d" % b, "hta%d" % k, "ppT"], writes=["hta%d" % k])
                        S.op("dve", lambda e, k=k, w2=w2: e.scalar_tensor_tensor(
                            out=ho[b][:, k, 0:n], in0=hin[b][:, k, 2:n + 2], scalar=w2, in1=ta[:, k, 0:n], op0=ALU.mult, op1=ALU.add),
                            reads=["hin%d" % b, "hta%d" % k, "ppT"], writes=["hho%d" % b])
                    S.op("pool", lambda e: e.tensor_copy(out=hz[b][:, :, 0:n], in_=ho[b][:, 0:2, 0:n]), reads=["hho%d" % b], writes=["hhz%d" % b])
                    S.dma("sp", ZTv[:, :, s0:s0 + n], hz[b][:, :, 0:n], reads=["hhz%d" % b], writes=["ZT%d:w%d" % (s, wi)])
                    S.dma("sp", UUv[:, :, s0:s0 + n], ho[b][:, 0:2, 0:n], reads=["hho%d" % b], writes=["UU%d:w%d" % (s, wi)])
                    S.dma("sp", GTv[:, :, s0:s0 + n], ho[b][:, 2:6, 0:n], reads=["hho%d" % b], writes=["GT%d:w%d" % (s, wi)])
                return len(wins)

        def sin_reduced(out_ap, arg_ap, nI, nF, npart, rd, wr, sfx=""):
            S.op("dve", lambda e: e.tensor_scalar(out=nI, in0=arg_ap, scalar1=1.0 / TWO_PI, scalar2=None, op0=ALU.mult),
                 reads=rd, writes=["nI" + sfx])
            S.op("dve", lambda e: e.tensor_copy(out=nF, in_=nI), reads=["nI" + sfx], writes=["nF" + sfx])
            S.op("dve", lambda e: e.scalar_tensor_tensor(out=arg_ap, in0=nF, scalar=-TWO_PI, in1=arg_ap, op0=ALU.mult, op1=ALU.add),
                 reads=["nF" + sfx] + rd, writes=rd)
            S.op("act", lambda e: e.activation(out=out_ap, in_=arg_ap, func=AF.Sin), reads=rd, writes=wr)

        def filters(l, L):
            nchp = L // 512
            FLv = FILT[L].rearrange("f (k p) t -> p f k t", p=128)
            with contextlib.ExitStack() as st:
                w1 = sbt(st, "fw1", [33, 64], F32)
                w2 = sbt(st, "fw2", [64, 64], F32)
                w3 = sbt(st, "fw3", [64, 1024], F32)
                nd = sbt(st, "fnd", [1, 256], F32)
                S.dma("sp", w1[:], fw1[l], writes=["fw1"])
                S.dma("sp", w2[:], fw2[l], writes=["fw2"])
                S.dma("sp", w3[:], fw3[l], writes=["fw3"])
                S.dma("sp", nd[:], c_ndelta, writes=["fnd"])
                zt = [sbt(st, "fzt%d" % i, [33, 512], F32) for i in range(2)]
                arg_ = [sbt(st, "farg%d" % i, [64, 512], F32) for i in range(2)]
                nI_ = [sbt(st, "fnI%d" % i, [64, 512], I32) for i in range(2)]
                nF_ = [sbt(st, "fnF%d" % i, [64, 512], F32) for i in range(2)]
                h1_ = [sbt(st, "fh1%d" % i, [64, 512], F32) for i in range(2)]
                h2_ = [sbt(st, "fh2%d" % i, [64, 512], F32) for i in range(2)]
                dec_ = [sbt(st, "fdec%d" % i, [128, 2, 512], F32) for i in range(2)]
                fo = [sbt(st, "ffo%d" % i, [128, 4, 2, 512], BF16) for i in range(2)]
                S.dma("sp", zt[0][:], cz[L][:, 0:512], writes=["fzt0"])
                for pc in range(nchp):
                    b = pc % 2
                    arg, nI, nF, h1, h2, dec = arg_[b], nI_[b], nF_[b], h1_[b], h2_[b], dec_[b]
                    if pc + 1 < nchp:
                        S.dma("sp", zt[1 - b][:], cz[L][:, (pc + 1) * 512:(pc + 2) * 512], writes=["fzt%d" % (1 - b)])
                    p1 = nextps(0, 7)
                    S.op("pe", lambda e: e.matmul(psb[p1][0:64, :], w1[:], zt[b][:], start=True, stop=True),
                         reads=["fw1", "fzt%d" % b], writes=["ps%d" % p1])
                    S.op("dve", lambda e: e.tensor_scalar(out=arg[:], in0=psb[p1][0:64, :], scalar1=ppT[0:64, PP["ff1"][0]:PP["ff1"][0] + 1],
                                                          scalar2=fb1f[0:64, 0:1], op0=ALU.mult, op1=ALU.add),
                         reads=["ps%d" % p1, "ppT", "fb1f"], writes=["farg%d" % b])
                    sin_reduced(h1[:], arg[:], nI[:], nF[:], 64, ["farg%d" % b], ["fh1%d" % b], sfx=str(b))
                    p2 = nextps(0, 7)
                    S.op("pe", lambda e: e.matmul(psb[p2][0:64, :], w2[:], h1[:], start=True, stop=True),
                         reads=["fw2", "fh1%d" % b], writes=["ps%d" % p2])
                    S.op("dve", lambda e: e.tensor_scalar(out=arg[:], in0=psb[p2][0:64, :], scalar1=ppT[0:64, PP["ff2"][0]:PP["ff2"][0] + 1],
                                                          scalar2=fb1f[0:64, 1:2], op0=ALU.mult, op1=ALU.add),
                         reads=["ps%d" % p2, "ppT", "fb1f"], writes=["farg%d" % b])
                    sin_reduced(h2[:], arg[:], nI[:], nF[:], 64, ["farg%d" % b], ["fh2%d" % b], sfx=str(b))
                    for cc in range(2):
                        pd = nextps(0, 7)
                        S.op("pe", lambda e, cc=cc, pd=pd: e.matmul(psb[pd][:], nd[0:1, cc * 128:(cc + 1) * 128], zt[b][0:1, :], start=True, stop=True),
                             reads=["fnd", "fzt%d" % b], writes=["ps%d" % pd])
                        S.op("act", lambda e, cc=cc, pd=pd: e.activation(out=dec[:, cc, :], in_=psb[pd][:], func=AF.Exp),
                             reads=["ps%d" % pd], writes=["fdec%d_%d" % (b, cc)])
                    for fi in range(4):
                        for cc in range(2):
                            m = fi * 2 + cc
                            p3 = nextps(0, 7)
                            S.op("pe", lambda e, m=m, p3=p3: e.matmul(psb[p3][:], w3[:, m * 128:(m + 1) * 128], h2[:], start=True, stop=True),
                                 reads=["fw3", "fh2%d" % b], writes=["ps%d" % p3])
                            S.op("dve", lambda e, fi=fi, cc=cc, p3=p3: e.tensor_tensor(out=fo[b][:, fi, cc, :], in0=psb[p3][:], in1=dec[:, cc, :], op=ALU.mult),
                                 reads=["ps%d" % p3, "fdec%d_%d" % (b, cc)], writes=["ffo%d" % b])
                            if pc == 0 and fi % 2 == 1:
                                S.op("dve", lambda e, fi=fi, cc=cc: e.memset(fo[b][:, fi, cc, 0:1], 0.0), reads=["ffo%d" % b], writes=["ffo%d" % b])
                    S.dma("sp", FLv[:, :, :, pc * 512:(pc + 1) * 512], fo[b][:], reads=["ffo%d" % b], writes=["FILT%d:%d" % (L, pc)])
            return ["FILT%d:%d" % (L, pc) for pc in range(nchp)]

        def fft_pass(L, src, src_names, mode, order, dst=None, dst_tag=None, C=256, ks=None, kstag=None):
            J = L // 128
            N1 = 2 * J
            N = 128 * N1
            CC = min(64, max(16, 8192 // N1), C)
            if ks is None:
                ks = KS[L][order]
                kstag = "KS%d_%d" % (L, order)
            KG = 256 // CC
            KC = max(1, N1 // 128)
            MK = min(N1, 128)
            NG = 512 // CC
            srcv = src.rearrange("c (a j) -> a c j", j=128)
            GC = min(512 // N1, CC)
            ncol = GC * N1
            KSo = ks.rearrange("p (cc g r n) -> p cc g r n", cc=C // CC, g=CC // GC, r=2)
            TDv = cTD[L].rearrange("n (kc k) r j -> k n kc r j", k=MK)
            with contextlib.ExitStack() as st:
                FAs = sbt(st, "FAs", [J, 2 * N1], BF16)
                S.dma("sp", FAs[:], cFA[L], writes=["FAs"])
                Gs = sbt(st, "Gs", [128, 2, 256], BF16)
                S.dma("sp", Gs[:], cG[L], writes=["Gs"])
                Fs = sbt(st, "Fs", [128, 3, 128], BF16)
                S.dma("sp", Fs[:], cTB[L], writes=["Fs"])
                Tw = sbt(st, "Tw", [128, 4, N1], F32)
                S.dma("sp", Tw[:], cTW[L], writes=["Tw"])
                Ysb = sbt(st, "Ysb", [128, 2, CC, N1], BF16)
                ksl = [sbt(st, "ksl%d" % i, [128, 2, ncol], BF16) for i in range(3)]
                kso = [sbt(st, "kso%d" % i, [128, 2, ncol], BF16) for i in range(3)]
                tm = [sbt(st, "ftm%d" % i, [128, ncol], F32) for i in range(8)]
                NPQ = 4
                PQ = [sbt(st, "fPQ%d" % i, [128, 2, 2, N1], BF16) for i in range(NPQ)]
                EV = [sbt(st, "fEV%d" % i, [128, 2, N1], BF16) for i in range(NPQ)]
                TwB = sbt(st, "TwB", [128, 4, N1], BF16)
                S.op("dve", lambda e: e.tensor_copy(out=TwB[:], in_=Tw[:]), reads=["Tw"], writes=["TwB"])
                xin = sbt(st, "xin", [J, CC, 128], BF16)
                A = sbt(st, "Abuf", [128, 2, CC, N1], BF16)
                if mode == "conv":
                    Zs = sbt(st, "Zsbuf", [128, KC, 2, CC, 128], BF16)
                    yo = sbt(st, "yobuf", [J, CC, 128], F32)
                    TDs = [sbt(st, "TDs%d" % i, [128, 8, KC, 2, J], BF16) for i in range(2)]
                for cc in range(C // CC):
                    c0 = cc * CC
                    S.dma("sp", xin[:], srcv[:, c0:c0 + CC, :], reads=src_names, writes=["xin"])
                    if True:
                        for c in range(CC):
                            pi = nextps(0, 7)
                            S.op("pe", lambda e, c=c, pi=pi: e.matmul(psb[pi][:, 0:2 * N1], xin[:, c, :], FAs[:], start=True, stop=True),
                                 reads=["xin", "FAs"], writes=["ps%d" % pi])
                            pv = psb[pi][:, 0:2 * N1].rearrange("p (r k) -> p r k", r=2)
                            pq = PQ[c % NPQ]
                            pqn = "fPQ%d" % (c % NPQ)
                            ev = EV[c % NPQ]
                            evn = "fEV%d" % (c % NPQ)
                            S.op("act", lambda e, pv=pv, ev=ev: e.activation(out=ev[:], in_=pv, func=AF.Copy), reads=["ps%d" % pi], writes=[evn])
                            S.op("dve", lambda e, ev=ev, pq=pq: e.tensor_tensor(out=pq[:, 0, :, :], in0=ev[:], in1=TwB[:, 0:2, :], op=ALU.mult),
                                 reads=[evn, "TwB"], writes=[pqn + "P"])
                            S.op("dve", lambda e, ev=ev, pq=pq: e.tensor_tensor(out=pq[:, 1, :, :], in0=ev[:], in1=TwB[:, 2:4, :], op=ALU.mult),
                                 reads=[evn, "TwB"], writes=[pqn + "Q"])
                            S.op("dve", lambda e, c=c, pq=pq: e.tensor_tensor(out=A[:, 0, c, :], in0=pq[:, 0, 0, :], in1=pq[:, 1, 1, :], op=ALU.subtract),
                                 reads=[pqn + "P", pqn + "Q"], writes=["AZr:%d" % c])
                            S.op("pool", lambda e, c=c, pq=pq: e.tensor_tensor(out=A[:, 1, c, :], in0=pq[:, 1, 0, :], in1=pq[:, 0, 1, :], op=ALU.add),
                                 reads=[pqn + "P", pqn + "Q"], writes=["AZi:%d" % c])
                        ngrp = CC // GC
                        if mode != "kf":
                            S.dma("sp", ksl[0][:], KSo[:, cc, 0, :, :], reads=["%s:%d:%d" % (kstag, cc, 0)], writes=["ksl0"])
                        for g in range(ngrp):
                            if mode != "kf" and g + 1 < ngrp:
                                S.dma("sp", ksl[(g + 1) % 3][:], KSo[:, cc, g + 1, :, :],
                                      reads=["%s:%d:%d" % (kstag, cc, g + 1)], writes=["ksl%d" % ((g + 1) % 3)])
                            pr_ = nextps(0, 7)
                            pi_ = nextps(0, 7)
                            gsl = slice(g * GC, (g + 1) * GC)
                            rdr = ["AZr:%d" % c for c in range(g * GC, (g + 1) * GC)] + ["Fs"]
                            rdi = ["AZi:%d" % c for c in range(g * GC, (g + 1) * GC)] + ["Fs"]
                            Ar_ = A[:, 0, gsl, :].rearrange("p c k -> p (c k)")
                            Ai_ = A[:, 1, gsl, :].rearrange("p c k -> p (c k)")
                            S.op("pe", lambda e: e.matmul(psb[pr_][:, 0:ncol], Fs[:, 0, :], Ar_, start=True, stop=False), reads=rdr, writes=["ps%d" % pr_])
                            S.op("pe", lambda e: e.matmul(psb[pr_][:, 0:ncol], Fs[:, 2, :], Ai_, start=False, stop=True), reads=rdi, writes=["ps%d" % pr_])
                            S.op("pe", lambda e: e.matmul(psb[pi_][:, 0:ncol], Fs[:, 1, :], Ar_, start=True, stop=False), reads=rdr, writes=["ps%d" % pi_])
                            S.op("pe", lambda e: e.matmul(psb[pi_][:, 0:ncol], Fs[:, 0, :], Ai_, start=False, stop=True), reads=rdi, writes=["ps%d" % pi_])
                            Xr = psb[pr_][:, 0:ncol]
                            Xi = psb[pi_][:, 0:ncol]
                            pnr, pni = ["ps%d" % pr_], ["ps%d" % pi_]
                            ksn = "%s:%d:%d" % (kstag, cc, g)
                            if mode == "kf":
                                ko = kso[g % 3]
                                S.op("act", lambda e, ko=ko: e.activation(out=ko[:, 0, :], in_=Xr, func=AF.Copy, scale=1.0 / N),
                                     reads=pnr, writes=["kso%d" % (g % 3)])
                                S.op("act", lambda e, ko=ko: e.activation(out=ko[:, 1, :], in_=Xi, func=AF.Copy, scale=1.0 / N),
                                     reads=pni, writes=["kso%d" % (g % 3)])
                                S.dma("sp", KSo[:, cc, g, :, :], ko[:], reads=["kso%d" % (g % 3)], writes=[ksn])
                            elif mode == "kb":
                                ko = kso[g % 3]
                                kl = ksl[g % 3]
                                S.op("dve", lambda e, ko=ko, kl=kl: e.scalar_tensor_tensor(out=ko[:, 0, :], in0=Xr, scalar=1.0 / N, in1=kl[:, 0, :],
                                                                                           op0=ALU.mult, op1=ALU.add),
                                     reads=pnr + ["ksl%d" % (g % 3)], writes=["kso%d" % (g % 3)])
                                S.op("dve", lambda e, ko=ko, kl=kl: e.scalar_tensor_tensor(out=ko[:, 1, :], in0=Xi, scalar=-1.0 / N, in1=kl[:, 1, :],
                                                                                           op0=ALU.mult, op1=ALU.add),
                                     reads=pni + ["ksl%d" % (g % 3), "kso%d" % (g % 3)], writes=["kso%d" % (g % 3)])
                                S.dma("sp", KSo[:, cc, g, :, :], ko[:], reads=["kso%d" % (g % 3)], writes=[ksn])
                            else:
                                kl = ksl[g % 3]
                                kn = "ksl%d" % (g % 3)
                                Kr, Ki = kl[:, 0, :], kl[:, 1, :]
                                t4 = tm[4 * (g % 2):4 * (g % 2) + 4]
                                tn = ["ftm%d" % (4 * (g % 2) + i) for i in range(4)]
                                Yr_ = Ysb[:, 0, gsl, :].rearrange("p c k -> p (c k)")
                                Yi_ = Ysb[:, 1, gsl, :].rearrange("p c k -> p (c k)")
                                S.op("dve", lambda e: e.tensor_tensor(out=t4[0][:], in0=Xr, in1=Kr, op=ALU.mult), reads=pnr + [kn], writes=[tn[0]])
                                S.op("dve", lambda e: e.tensor_tensor(out=t4[1][:], in0=Xi, in1=Ki, op=ALU.mult), reads=pni + [kn], writes=[tn[1]])
                                S.op("dve", lambda e: e.tensor_tensor(out=t4[2][:], in0=Xr, in1=Ki, op=ALU.mult), reads=pnr + [kn], writes=[tn[2]])
                                S.op("dve", lambda e: e.tensor_tensor(out=t4[3][:], in0=Xi, in1=Kr, op=ALU.mult), reads=pni + [kn], writes=[tn[3]])
                                S.op("pool", lambda e: e.tensor_tensor(out=Yr_, in0=t4[0][:], in1=t4[1][:], op=ALU.subtract),
                                     reads=[tn[0], tn[1]], writes=["Ysr:%d" % g])
                                S.op("pool", lambda e: e.tensor_tensor(out=Yi_, in0=t4[2][:], in1=t4[3][:], op=ALU.add),
                                     reads=[tn[2], tn[3]], writes=["Ysi:%d" % g])
                    if mode != "conv":
                        continue
                    if True:
                        for c in range(CC):
                            for kc in range(KC):
                                pi = nextps(0, 7)
                                S.op("pe", lambda e, c=c, kc=kc, pi=pi: e.matmul(psb[pi][0:MK, 0:256], Ysb[:, 0, c, kc * 128:kc * 128 + MK], Gs[:, 0, :], start=True, stop=False),
                                     reads=["Ysr:%d" % (c // GC), "Gs"], writes=["ps%d" % pi])
                                S.op("pe", lambda e, c=c, kc=kc, pi=pi: e.matmul(psb[pi][0:MK, 0:256], Ysb[:, 1, c, kc * 128:kc * 128 + MK], Gs[:, 1, :], start=False, stop=True),
                                     reads=["Ysi:%d" % (c // GC), "Gs"], writes=["ps%d" % pi])
                                pv = psb[pi][0:MK, 0:256].rearrange("p (r n) -> p r n", r=2)
                                if (c + kc) % 2 == 0:
                                    S.op("act", lambda e, c=c, kc=kc, pv=pv: e.activation(out=Zs[0:MK, kc, :, c, :], in_=pv, func=AF.Copy),
                                         reads=["ps%d" % pi], writes=["Zsa"])
                                else:
                                    S.op("dve", lambda e, c=c, kc=kc, pv=pv: e.tensor_copy(out=Zs[0:MK, kc, :, c, :], in_=pv),
                                         reads=["ps%d" % pi], writes=["Zsd"])
                        def loadTD(gi):
                            S.dma("sp", TDs[gi % 2][0:MK, :, :, :, :], TDv[:, gi * 8:(gi + 1) * 8, :, :, :], writes=["TDs%d" % (gi % 2)])

                        loadTD(0)
                        for ng in range(128 // NG):
                            pi = nextps(0, 7)
                            pY = psb[pi][0:J, :].rearrange("p (g c) -> p g c", g=NG)
                            for gg in range(NG):
                                n1 = ng * NG + gg
                                gi, ti = n1 // 8, n1 % 8
                                if ti == 0 and (gi + 1) * 8 < 128:
                                    loadTD(gi + 1)
                                td = TDs[gi % 2]
                                cnt = 0
                                for kc in range(KC):
                                    for r in range(2):
                                        S.op("pe", lambda e, gg=gg, n1=n1, td=td, ti=ti, kc=kc, r=r, cnt=cnt: e.matmul(
                                            pY[:, gg, :], td[0:MK, ti, kc, r, :], Zs[0:MK, kc, r, :, n1],
                                            start=(cnt == 0), stop=(cnt == 2 * KC - 1)),
                                            reads=["Zsa", "Zsd", "TDs%d" % (gi % 2)], writes=["ps%d" % pi])
                                        cnt += 1
                            ov = yo[:, :, ng * NG:(ng + 1) * NG].rearrange("p c g -> p g c")
                            if ng % 2 == 0:
                                S.op("act", lambda e, ov=ov, pY=pY: e.activation(out=ov, in_=pY, func=AF.Copy), reads=["ps%d" % pi], writes=["yo_a"])
                            else:
                                S.op("dve", lambda e, ov=ov, pY=pY: e.tensor_copy(out=ov, in_=pY), reads=["ps%d" % pi], writes=["yo_d"])
                        S.dma("sp", dst.rearrange("c (a j) -> a c j", j=128)[:, c0:c0 + CC, :], yo[:], reads=["yo_a", "yo_d"],
                              writes=["%s:%d" % (dst_tag, cc)])
            return ["%s:%d" % (dst_tag, cc) for cc in range(C // CC)]

        def hy_post(l, s, order, ynames, unames, gnames):
            L = Ls[s]
            nch = L // 512
            YTv = YT[s].rearrange("(k p) t -> p k t", p=128)
            UUv = UU[s].rearrange("(k p) t -> p k t", p=128)
            GTv = GT[s].rearrange("(k p) t -> p k t", p=128)
            ZTv = ZT[s].rearrange("(k p) t -> p k t", p=128)
            CTv = CT[s].rearrange("(k p) t -> p k t", p=128)
            outn = []
            with contextlib.ExitStack() as st:
                yb = [sbt(st, "qy%d" % i, [128, 2, 512], F32) for i in range(2)]
                ub = [sbt(st, "qu%d" % i, [128, 2, 512], F32) for i in range(2)]
                gb = [sbt(st, "qg%d" % i, [128, 2, 512], F32) for i in range(2)]
                zz = [sbt(st, "qz%d" % i, [128, 2, 512], F32) for i in range(2)]
                zb = [sbt(st, "qzb%d" % i, [128, 2, 512], BF16) for i in range(2)]
                sqz = sbt(st, "qsq", [128, 2, 512], BF16)
                rs = sbt(st, "qrs", [128, 512], F32)

                def load(c):
                    b = c % 2
                    sl = slice(c * 512, (c + 1) * 512)
                    S.dma("sp", yb[b][:], YTv[:, :, sl], reads=ynames, writes=["qy%d" % b])
                    S.dma("sp", ub[b][:], UUv[:, :, sl], reads=unames, writes=["qu%d" % b])
                    S.dma("sp", gb[b][:], GTv[:, 2 * order:2 * order + 2, sl], reads=gnames, writes=["qg%d" % b])

                load(0)
                for c in range(nch):
                    b = c % 2
                    sl = slice(c * 512, (c + 1) * 512)
                    if c + 1 < nch:
                        load(c + 1)
                    o, _ = PP["hyb"]
                    for cc in range(2):
                        S.op("dve", lambda e, cc=cc: e.scalar_tensor_tensor(
                            out=zz[b][:, cc, :], in0=ub[b][:, cc, :], scalar=ppT[:, o + 2 * order + cc:o + 2 * order + cc + 1],
                            in1=yb[b][:, cc, :], op0=ALU.mult, op1=ALU.add),
                            reads=["qu%d" % b, "qy%d" % b, "ppT"], writes=["qz%d:%d" % (b, cc)])
                    zn = ["qz%d:0" % b, "qz%d:1" % b]
                    S.op("pool", lambda e: e.tensor_tensor(out=zz[b][:], in0=zz[b][:], in1=gb[b][:], op=ALU.mult),
                         reads=zn + ["qg%d" % b], writes=zn)
                    if order == 0:
                        S.op("act", lambda e: e.activation(out=zb[b][:], in_=zz[b][:], func=AF.Copy), reads=zn, writes=["qzb%d" % b])
                        S.dma("sp", ZTv[:, :, sl], zb[b][:], reads=["qzb%d" % b], writes=["ZT%d:p%d" % (s, c)])
                        S.dma("sp", UUv[:, :, sl], zz[b][:], reads=zn, writes=["UU%d:p%d" % (s, c)])
                        outn.append(c)
                    else:
                        S.op("pool", lambda e: e.tensor_tensor(out=sqz[:], in0=zz[b][:], in1=zz[b][:], op=ALU.mult), reads=zn, writes=["qsq"])
                        pi = nextps(0, 7)
                        for cc in range(2):
                            S.op("pe", lambda e, cc=cc: e.matmul(psb[pi][:], onesB[:], sqz[:, cc, :], start=(cc == 0), stop=(cc == 1)),
                                 reads=["qsq", "onesB"], writes=["ps%d" % pi])
                        rsqrt_ln(rs[:], psb[pi][:], 1.0 / 256, ["ps%d" % pi, "epsT"], ["qrs"])
                        for cc in range(2):
                            S.op("dve", lambda e, cc=cc: e.scalar_tensor_tensor(
                                out=zb[b][:, cc, :], in0=zz[b][:, cc, :], scalar=ppc("gout", 6 + cc), in1=rs[:],
                                op0=ALU.mult, op1=ALU.mult), reads=zn + ["qrs", "ppT"], writes=["qzb%d" % b])
                        S.dma("sp", CTv[:, 6:8, sl], zb[b][:], reads=["qzb%d" % b], writes=["CTc%d:%d" % (s, c)])
            return outn


        FORD = [1, 3, 0, 2]

        def filters_sh(l, L):
            nchp = L // 512
            w3v = fw3[l].rearrange("k (f c) -> k f c", f=4)
            with contextlib.ExitStack() as st:
                w1 = sbt(st, "gw1", [33, 64], F32)
                w2 = sbt(st, "gw2", [64, 64], F32)
                w3o = sbt(st, "gw3", [64, 4, NCO], F32)
                nd4 = sbt(st, "gnd", [1, 4, NCO], F32)
                S.dma("sp", w1[:], fw1[l], writes=["fw1"])
                S.dma("sp", w2[:], fw2[l], writes=["fw2"])
                for sl, fi in enumerate(FORD):
                    S.dma("sp", w3o[:, sl, :], w3v[:, fi, bass.ds(pid * NCO, NCO)], writes=["fw3:%d" % sl])
                    S.dma("sp", nd4[:, sl, :], c_ndelta[:, bass.ds(pid * NCO, NCO)], writes=["fnd:%d" % sl])
                w3n = ["fw3:%d" % sl for sl in range(4)]
                ndn = ["fnd:%d" % sl for sl in range(4)]
                zt = [sbt(st, "gzt%d" % i, [33, 512], F32) for i in range(2)]
                arg = sbt(st, "garg", [64, 512], F32)
                nI = sbt(st, "gnI", [64, 512], I32)
                nF = sbt(st, "gnF", [64, 512], F32)
                h1 = sbt(st, "gh1", [64, 512], F32)
                h2 = sbt(st, "gh2", [64, 512], F32)
                dec = sbt(st, "gdec", [128, 512], F32)
                fo = [sbt(st, "gfo%d" % i, [128, 512], BF16) for i in range(2)]
                S.dma("sp", zt[0][:], cz[L][:, 0:512], writes=["fzt0"])
                for pc in range(nchp):
                    b = pc % 2
                    if pc + 1 < nchp:
                        S.dma("sp", zt[1 - b][:], cz[L][:, (pc + 1) * 512:(pc + 2) * 512], writes=["fzt%d" % (1 - b)])
                    p1 = nextps(0, 7)
                    S.op("pe", lambda e: e.matmul(psb[p1][0:64, :], w1[:], zt[b][:], start=True, stop=True),
                         reads=["fw1", "fzt%d" % b], writes=["ps%d" % p1])
                    S.op("dve", lambda e: e.tensor_scalar(out=arg[:], in0=psb[p1][0:64, :], scalar1=ppT[0:64, PP["ff1"][0]:PP["ff1"][0] + 1],
                                                          scalar2=fb1f[0:64, 0:1], op0=ALU.mult, op1=ALU.add),
                         reads=["ps%d" % p1, "ppT", "fb1f"], writes=["farg"])
                    sin_reduced(h1[:], arg[:], nI[:], nF[:], 64, ["farg"], ["fh1"])
                    p2 = nextps(0, 7)
                    S.op("pe", lambda e: e.matmul(psb[p2][0:64, :], w2[:], h1[:], start=True, stop=True),
                         reads=["fw2", "fh1"], writes=["ps%d" % p2])
                    S.op("dve", lambda e: e.tensor_scalar(out=arg[:], in0=psb[p2][0:64, :], scalar1=ppT[0:64, PP["ff2"][0]:PP["ff2"][0] + 1],
                                                          scalar2=fb1f[0:64, 1:2], op0=ALU.mult, op1=ALU.add),
                         reads=["ps%d" % p2, "ppT", "fb1f"], writes=["farg"])
                    sin_reduced(h2[:], arg[:], nI[:], nF[:], 64, ["farg"], ["fh2"])
                    pd = nextps(0, 7)
                    S.op("pe", lambda e: e.matmul(psb[pd][:], nd4[:].rearrange("p f c -> p (f c)"), zt[b][0:1, :], start=True, stop=True),
                         reads=ndn + ["fzt%d" % b], writes=["ps%d" % pd])
                    S.op("act", lambda e: e.activation(out=dec[:], in_=psb[pd][:], func=AF.Exp), reads=["ps%d" % pd], writes=["fdec"])
                    p3 = nextps(0, 7)
                    S.op("pe", lambda e: e.matmul(psb[p3][:], w3o[:].rearrange("k f c -> k (f c)"), h2[:], start=True, stop=True),
                         reads=w3n + ["fh2"], writes=["ps%d" % p3])
                    S.op("dve", lambda e: e.tensor_tensor(out=fo[b][:], in0=psb[p3][:], in1=dec[:], op=ALU.mult),
                         reads=["ps%d" % p3, "fdec"], writes=["gfo%d" % b])
                    if pc == 0:
                        S.op("dve", lambda e: e.memset(fo[b][0:64, 0:1], 0.0), reads=["gfo%d" % b], writes=["gfo%d" % b])
                    S.dma("sp", FILTs[L][:, pc * 512:(pc + 1) * 512], fo[b][:], reads=["gfo%d" % b], writes=["FILTs%d:%d" % (L, pc)])
            return ["FILTs%d:%d" % (L, pc) for pc in range(nchp)]

        def extract_own(s, zn, un, gn):
            rows = bass.ds(pid * NCO, NCO)
            S.dma("sp", ZTo[s], ZT[s][rows, :], reads=zn, writes=["ZTo%d" % s])
            S.dma("sp", Uo[s], UU[s][rows, :], reads=un, writes=["Uo%d" % s])
            S.dma("sp", Go[s][0], GT[s][rows, :], reads=gn, writes=["Go%d:0" % s])
            S.dma("sp", Go[s][1], GT[s][bass.ds(pid * NCO + 256, NCO), :], reads=gn, writes=["Go%d:1" % s])

        def hy_gate_sh(l, s, order, ynames):
            L = Ls[s]
            LQ = L // 4
            CW = min(512, LQ)
            nch = LQ // CW
            with contextlib.ExitStack() as st:
                hb4 = sbt(st, "hb4", [128, 1], F32)
                for q in range(4):
                    S.dma("sp", hb4[q * NCO:(q + 1) * NCO, :], hyb_raw[l, order, bass.ds(pid * NCO, NCO)].rearrange("(c o) -> c o", o=1),
                          writes=["hb4:%d" % q])
                hbn = ["hb4:%d" % q for q in range(4)]
                yb = [sbt(st, "sy%d" % i, [128, CW], F32) for i in range(2)]
                ub = [sbt(st, "su%d" % i, [128, CW], F32) for i in range(2)]
                gb = [sbt(st, "sg%d" % i, [128, CW], F32) for i in range(2)]
                zz = [sbt(st, "sz%d" % i, [128, CW], F32) for i in range(2)]
                zb = [sbt(st, "szb%d" % i, [128, CW], BF16) for i in range(2)]

                def load(c):
                    b = c % 2
                    for q in range(4):
                        sl = slice(q * LQ + c * CW, q * LQ + (c + 1) * CW)
                        pr = slice(q * NCO, (q + 1) * NCO)
                        S.dma("sp", yb[b][pr, :], YTo[s][:, sl], reads=ynames, writes=["sy%d:%d" % (b, q)])
                        S.dma("sp", ub[b][pr, :], Uo[s][:, sl], reads=["Uo%d" % s] + ["Uo%d:%d:%d" % (s, q, c)], writes=["su%d:%d" % (b, q)])
                        S.dma("sp", gb[b][pr, :], Go[s][order][:, sl], reads=["Go%d:%d" % (s, order)], writes=["sg%d:%d" % (b, q)])

                load(0)
                for c in range(nch):
                    b = c % 2
                    if c + 1 < nch:
                        load(c + 1)
                    qn = lambda nm: ["%s%d:%d" % (nm, b, q) for q in range(4)]
                    S.op("dve", lambda e: e.scalar_tensor_tensor(out=zz[b][:], in0=ub[b][:], scalar=hb4[:, 0:1], in1=yb[b][:],
                                                                 op0=ALU.mult, op1=ALU.add),
                         reads=qn("su") + qn("sy") + hbn, writes=["sz%d" % b])
                    S.op("pool", lambda e: e.tensor_tensor(out=zz[b][:], in0=zz[b][:], in1=gb[b][:], op=ALU.mult),
                         reads=["sz%d" % b] + qn("sg"), writes=["sz%d" % b])
                    if order == 0:
                        S.op("act", lambda e: e.activation(out=zb[b][:], in_=zz[b][:], func=AF.Copy), reads=["sz%d" % b], writes=["szb%d" % b])
                    for q in range(4):
                        sl = slice(q * LQ + c * CW, q * LQ + (c + 1) * CW)
                        pr = slice(q * NCO, (q + 1) * NCO)
                        if order == 0:
                            S.dma("sp", ZTo[s][:, sl], zb[b][pr, :], reads=["szb%d" % b], writes=["ZTo%d:%d:%d" % (s, q, c)])
                            S.dma("sp", Uo[s][:, sl], zz[b][pr, :], reads=["sz%d" % b], writes=["Uo%d:%d:%d" % (s, q, c)])
                        else:
                            S.dma("sp", Co[s].ap()[:, sl], zz[b][pr, :], reads=["sz%d" % b], writes=["Co%d:%d:%d" % (s, q, c)])
            return ["ZTo%d:%d:%d" % (s, q, c) for q in range(4) for c in range(nch)]

        def hy_norm_all(l, s):
            L = Ls[s]
            nch = L // 512
            Cv = Cg[s].ap().rearrange("(k p) t -> p k t", p=128)
            CTv = CT[s].rearrange("(k p) t -> p k t", p=128)
            with contextlib.ExitStack() as st:
                zz = [sbt(st, "nz%d" % i, [128, 2, 512], F32) for i in range(2)]
                zb = [sbt(st, "nzb%d" % i, [128, 2, 512], BF16) for i in range(2)]
                sqz = sbt(st, "nsq", [128, 2, 512], BF16)
                rs = sbt(st, "nrs", [128, 512], F32)
                S.dma("sp", zz[0][:], Cv[:, :, 0:512], reads=["Cg%d" % s], writes=["nz0"])
                for c in range(nch):
                    b = c % 2
                    sl = slice(c * 512, (c + 1) * 512)
                    if c + 1 < nch:
                        S.dma("sp", zz[1 - b][:], Cv[:, :, (c + 1) * 512:(c + 2) * 512], reads=["Cg%d" % s], writes=["nz%d" % (1 - b)])
                    S.op("pool", lambda e: e.tensor_tensor(out=sqz[:], in0=zz[b][:], in1=zz[b][:], op=ALU.mult), reads=["nz%d" % b], writes=["nsq"])
                    pi = nextps(0, 7)
                    for cc in range(2):
                        S.op("pe", lambda e, cc=cc: e.matmul(psb[pi][:], onesB[:], sqz[:, cc, :], start=(cc == 0), stop=(cc == 1)),
                             reads=["nsq", "onesB"], writes=["ps%d" % pi])
                    rsqrt_ln(rs[:], psb[pi][:], 1.0 / 256, ["ps%d" % pi, "epsT"], ["nrs"])
                    for cc in range(2):
                        S.op("dve", lambda e, cc=cc: e.scalar_tensor_tensor(
                            out=zb[b][:, cc, :], in0=zz[b][:, cc, :], scalar=ppc("gout", 6 + cc), in1=rs[:],
                            op0=ALU.mult, op1=ALU.mult), reads=["nz%d" % b, "nrs", "ppT"], writes=["nzb%d" % b])
                    S.dma("sp", CTv[:, 6:8, sl], zb[b][:], reads=["nzb%d" % b], writes=["CTc%d:%d" % (s, c)])

        def phase5(l, s, n3):
            L = Ls[s]
            nch = L // 512
            XAv = XA[s].rearrange("(k p) t -> p k t", p=128)
            XBv = XB[s].rearrange("(k p) t -> p k t", p=128)
            CTv = CT[s].rearrange("(k p) t -> p k t", p=128)
            ctb_names = ["CTb%d:%d" % (s, wi) for wi in range(n3)]
            with contextlib.ExitStack() as st:
                wo = sbt(st, "wo", [128, 8, D], BF16)
                S.dma("sp", wo[:], WB_out.rearrange("(k p) n -> p k n", p=128), reads=["WB_out:%d" % r for r in range(0, D, 128)], writes=["wo"])
                ct = [sbt(st, "ct%d" % i, [128, 8, 512], BF16) for i in range(2)]
                xt = [sbt(st, "x5_%d" % i, [128, 8, 512], F32) for i in range(2)]
                xo = [sbt(st, "xo5_%d" % i, [128, 8, 512], F32) for i in range(2)]

                def load(c):
                    b = c % 2
                    sl = slice(c * 512, (c + 1) * 512)
                    S.dma("sp", ct[b][:], CTv[:, :, sl],
                          reads=rnames("CTa%d" % s, c * 512, (c + 1) * 512) + ctb_names + ["CTc%d:%d" % (s, c)], writes=["ct%d" % b])
                    S.dma("sp", xt[b][:], XAv[:, :, sl], reads=rnames("XA%d" % s, c * 512, (c + 1) * 512), writes=["x5_%d" % b])

                load(0)
                for c in range(nch):
                    b = c % 2
                    if c + 1 < nch:
                        load(c + 1)
                    for m in range(8):
                        pi = nextps(0, 7)
                        for k in range(8):
                            S.op("pe", lambda e, m=m, k=k, pi=pi: e.matmul(psb[pi][:], wo[:, k, m * 128:(m + 1) * 128], ct[b][:, k, :],
                                                                         start=(k == 0), stop=(k == 7)),
                                 reads=["wo", "ct%d" % b], writes=["ps%d" % pi])
                        S.op("dve", lambda e, m=m, pi=pi: e.tensor_tensor(out=xo[b][:, m, :], in0=psb[pi][:], in1=xt[b][:, m, :], op=ALU.add),
                             reads=["ps%d" % pi, "x5_%d" % b], writes=["xo5_%d" % b])
                    S.dma("sp", XBv[:, :, c * 512:(c + 1) * 512], xo[b][:], reads=["xo5_%d" % b],
                          writes=rnames("XB%d" % s, c * 512, (c + 1) * 512))

        def phase6(l, seqs):
            with contextlib.ExitStack() as st:
                wi_ = sbt(st, "wfi", [128, 8, 2 * DFF], BF16)
                wo_ = sbt(st, "wfo", [128, 22, D], BF16)
                for k in range(8):
                    S.dma("sp", wi_[:, k, :], WB_fi[k * 128:(k + 1) * 128, :], reads=["WB_fi:%d" % (k * 128)], writes=["wfi%d" % k])
                S.dma("sp", wo_[:], WB_fo.rearrange("(f p) n -> p f n", p=128), reads=["WB_fo:%d" % r for r in range(0, DFF, 128)], writes=["wfo"])
                wfin = ["wfi%d" % k for k in range(8)]
                hT = sbt(st, "h6", [128, 8, 512], BF16)
                act = sbt(st, "act6", [128, 22, 512], BF16)
                xk = [sbt(st, "xk%d" % i, [128, 512], F32) for i in range(3)]
                sqk = [sbt(st, "sqk%d" % i, [128, 512], BF16) for i in range(2)]
                rstd = sbt(st, "rstd6", [128, 512], F32)
                cg = [sbt(st, "cg%d" % i, [128, 512], F32) for i in range(2)]
                cu = [sbt(st, "cu%d" % i, [128, 512], F32) for i in range(2)]
                gg = [sbt(st, "gg%d" % i, [128, 512], F32) for i in range(2)]
                xo = [sbt(st, "xo6_%d" % i, [128, 512], F32) for i in range(2)]
                xkc = 0
                for s in seqs:
                    L = Ls[s]
                    WO = 510
                    wins = list(range(0, L, WO))
                    for wi, s0 in enumerate(wins):
                        n = min(WO, L - s0)
                        ncol = n + 2
                        t0, t1_ = max(0, s0 - 1), min(L, s0 + n + 1)
                        o0 = t0 - (s0 - 1)
                        edge = (t0 > s0 - 1) or (t1_ < s0 + n + 1)
                        xbn = rnames("XB%d" % s, t0, t1_)

                        def loadk(k):
                            nonlocal xkc
                            bi = xkc % 3
                            xkc += 1
                            if edge:
                                S.op("pool", lambda e: e.memset(xk[bi][:], 0.0), writes=["xk%d" % bi])
                            S.dma("sp", xk[bi][:, o0:o0 + (t1_ - t0)], XB[s][k * 128:(k + 1) * 128, t0:t1_], reads=xbn, writes=["xk%d" % bi])
                            return bi

                        pss = nextps(0, 7)
                        for k in range(8):
                            bi = loadk(k)
                            S.op("pool", lambda e, bi=bi, k=k: e.tensor_tensor(out=sqk[k % 2][:, 0:ncol], in0=xk[bi][:, 0:ncol], in1=xk[bi][:, 0:ncol], op=ALU.mult),
                                 reads=["xk%d" % bi], writes=["sqk%d" % (k % 2)])
                            S.op("pe", lambda e, k=k: e.matmul(psb[pss][:, 0:ncol], onesB[:], sqk[k % 2][:, 0:ncol], start=(k == 0), stop=(k == 7)),
                                 reads=["sqk%d" % (k % 2), "onesB"], writes=["ps%d" % pss])
                        rsqrt_ln(rstd[:, 0:ncol], psb[pss][:, 0:ncol], 1.0 / D, ["ps%d" % pss, "epsT"], ["rstd6"])
                        for k in range(8):
                            bi = loadk(k)
                            S.op("dve", lambda e, bi=bi, k=k: e.scalar_tensor_tensor(
                                out=hT[:, k, 0:ncol], in0=xk[bi][:, 0:ncol], scalar=ppc("g2", k), in1=rstd[:, 0:ncol],
                                op0=ALU.mult, op1=ALU.mult), reads=["xk%d" % bi, "rstd6", "ppT"], writes=["h6"])
                        ow, _ = PP["fcw"]
                        ob, _ = PP["fcb"]
                        for f in range(22):
                            fb = f % 2
                            res = []
                            for half in range(2):
                                ch = half * 22 + f
                                col = half * DFF + f * 128
                                pi = nextps(0, 7)
                                for k in range(8):
                                    S.op("pe", lambda e, k=k, col=col, pi=pi: e.matmul(psb[pi][:, 0:ncol], wi_[:, k, col:col + 128], hT[:, k, 0:ncol],
                                                                                  start=(k == 0), stop=(k == 7)),
                                         reads=["wfi%d" % k, "h6"], writes=["ps%d" % pi])
                                dst = cg[fb] if half == 0 else cu[fb]
                                dn = ("cg%d" if half == 0 else "cu%d") % fb
                                w0 = ppT[:, ow + 3 * ch:ow + 3 * ch + 1]
                                w1 = ppT[:, ow + 3 * ch + 1:ow + 3 * ch + 2]
                                w2 = ppT[:, ow + 3 * ch + 2:ow + 3 * ch + 3]
                                bb = ppT[:, ob + ch:ob + ch + 1]
                                S.op("act", lambda e, dst=dst, pi=pi, w1=w1, bb=bb: e.activation(out=dst[:, 0:n], in_=psb[pi][:, 1:n + 1], func=AF.Identity, bias=bb, scale=w1),
                                     reads=["ps%d" % pi, "ppT"], writes=[dn])
                                S.op("dve", lambda e, dst=dst, pi=pi, w0=w0: e.scalar_tensor_tensor(out=dst[:, 0:n], in0=psb[pi][:, 0:n], scalar=w0, in1=dst[:, 0:n],
                                                                                              op0=ALU.mult, op1=ALU.add),
                                     reads=["ps%d" % pi, dn, "ppT"], writes=[dn])
                                S.op("dve", lambda e, dst=dst, pi=pi, w2=w2: e.scalar_tensor_tensor(out=dst[:, 0:n], in0=psb[pi][:, 2:n + 2], scalar=w2, in1=dst[:, 0:n],
                                                                                              op0=ALU.mult, op1=ALU.add),
                                     reads=["ps%d" % pi, dn, "ppT"], writes=[dn])
                            S.op("act", lambda e, fb=fb: e.activation(out=gg[fb][:, 0:n], in_=cg[fb][:, 0:n], func=AF.Gelu),
                                 reads=["cg%d" % fb], writes=["gg%d" % fb])
                            S.op("pool", lambda e, fb=fb, f=f: e.tensor_tensor(out=act[:, f, 0:n], in0=gg[fb][:, 0:n], in1=cu[fb][:, 0:n], op=ALU.mult),
                                 reads=["gg%d" % fb, "cu%d" % fb], writes=["act6"])
                        for m in range(8):
                            pi = nextps(0, 7)
                            for f in range(22):
                                S.op("pe", lambda e, m=m, f=f, pi=pi: e.matmul(psb[pi][:, 0:n], wo_[:, f, m * 128:(m + 1) * 128], act[:, f, 0:n],
                                                                             start=(f == 0), stop=(f == 21)),
                                     reads=["wfo", "act6"], writes=["ps%d" % pi])
                            bi = xkc % 3
                            xkc += 1
                            S.dma("sp", xk[bi][:, 0:n], XB[s][m * 128:(m + 1) * 128, s0:s0 + n], reads=xbn, writes=["xk%d" % bi])
                            ob_ = m % 2
                            S.op("dve", lambda e, bi=bi, pi=pi, ob_=ob_: e.tensor_tensor(out=xo[ob_][:, 0:n], in0=psb[pi][:, 0:n], in1=xk[bi][:, 0:n], op=ALU.add),
                                 reads=["ps%d" % pi, "xk%d" % bi], writes=["xo6_%d" % ob_])
                            S.dma("sp", XA[s][m * 128:(m + 1) * 128, s0:s0 + n], xo[ob_][:, 0:n], reads=["xo6_%d" % ob_],
                                  writes=["XAw%d:%d:%d" % (s, wi, m)] + rnames("XA%d" % s, s0, s0 + n))

        def phase7(s):
            L = Ls[s]
            nch = L // 512
            XAv = XA[s].rearrange("(k p) t -> p k t", p=128)
            allw = []
            with contextlib.ExitStack() as st:
                xt = [sbt(st, "x7_%d" % i, [128, 8, 512], F32) for i in range(2)]
                yt = [sbt(st, "y7_%d" % i, [128, 4, D], F32) for i in range(2)]

                def load(c):
                    S.dma("sp", xt[c % 2][:], XAv[:, :, c * 512:(c + 1) * 512],
                          reads=rnames("XA%d" % s, c * 512, (c + 1) * 512), writes=["x7_%d" % (c % 2)])

                load(0)
                for c in range(nch):
                    b = c % 2
                    if c + 1 < nch:
                        load(c + 1)
                    for n in range(4):
                        for kh in range(2):
                            pi = nextps(0, 7)
                            for kq in range(4):
                                k = kh * 4 + kq
                                S.op("pe", lambda e, n=n, k=k, kq=kq, pi=pi: e.transpose(psb[pi][:, kq * 128:(kq + 1) * 128], xt[b][:, k, n * 128:(n + 1) * 128], identF[:]),
                                     reads=["x7_%d" % b, "identF"], writes=["ps%d" % pi])
                            if (n + kh) % 2 == 0:
                                S.op("act", lambda e, n=n, kh=kh, pi=pi: e.activation(out=yt[b][:, n, kh * 512:(kh + 1) * 512], in_=psb[pi][:], func=AF.Copy),
                                     reads=["ps%d" % pi], writes=["y7_%d:%d%d" % (b, n, kh)])
                            else:
                                S.op("dve", lambda e, n=n, kh=kh, pi=pi: e.tensor_copy(out=yt[b][:, n, kh * 512:(kh + 1) * 512], in_=psb[pi][:]),
                                     reads=["ps%d" % pi], writes=["y7_%d:%d%d" % (b, n, kh)])
                    S.dma("sp", y_out[s][c * 512:(c + 1) * 512, :].rearrange("(n p) d -> p n d", p=128), yt[b][:],
                          reads=["y7_%d:%d%d" % (b, n, kh) for n in range(4) for kh in range(2)], writes=["y%d:%d" % (s, c)])

        for l in range(depth):
            load_layer_params(l)
            n3 = {}
            for s in range(nseq):
                phase1(l, s, first=(l == 0))
                S.join()
            if stop_after == "p1":
                break
            for s in range(nseq):
                phase2(l, s)
                S.join()
            for s in range(nseq):
                n3[s] = phase3(l, s)
                S.join()
            n4 = {}
            for s in range(nseq):
                n4[s] = phase4a(l, s)
                S.join()
            if stop_after == "p4a":
                break
            for L in uL:
                fn = filters(l, L)
                S.join()
                for o in range(2):
                    fft_pass(L, FILT[L][2 * o], fn, "kf", o)
                    S.join()
                    fft_pass(L, FILT[L][2 * o + 1], fn, "kb", o)
                    S.join()
            for L in uLs:
                fn = filters_sh(l, L)
                S.join()
                for o in range(2):
                    sf, sb_ = FORD.index(2 * o), FORD.index(2 * o + 1)
                    fft_pass(L, FILTs[L][sf * NCO:(sf + 1) * NCO, :], fn, "kf", o, C=NCO, ks=KSs[L][o], kstag="KSs%d_%d" % (L, o))
                    S.join()
                    fft_pass(L, FILTs[L][sb_ * NCO:(sb_ + 1) * NCO, :], fn, "kb", o, C=NCO, ks=KSs[L][o], kstag="KSs%d_%d" % (L, o))
                    S.join()
            if stop_after == "filt":
                break
            for s in range(nseq):
                L = Ls[s]
                zn = ["ZT%d:w%d" % (s, wi) for wi in range(n4[s])]
                un = ["UU%d:w%d" % (s, wi) for wi in range(n4[s])]
                gn = ["GT%d:w%d" % (s, wi) for wi in range(n4[s])]
                if shard[s]:
                    extract_own(s, zn, un, gn)
                    S.join()
                    if stop_after == "ext":
                        break
                    yn = fft_pass(L, ZTo[s], ["ZTo%d" % s], "conv", 0, dst=YTo[s], dst_tag="YTo%d_0" % s, C=NCO, ks=KSs[L][0], kstag="KSs%d_0" % L)
                    S.join()
                    if stop_after == "c0":
                        break
                    zn2 = hy_gate_sh(l, s, 0, yn)
                    S.join()
                    if stop_after == "g0":
                        break
                    yn = fft_pass(L, ZTo[s], zn2, "conv", 1, dst=YTo[s], dst_tag="YTo%d_1" % s, C=NCO, ks=KSs[L][1], kstag="KSs%d_1" % L)
                    S.join()
                    hy_gate_sh(l, s, 1, yn)
                    S.join()
                    if stop_after == "g1":
                        break
                    S.collective(es, "AllGather", Co[s].ap(), Cg[s].ap(), ccs[:], reads=[], writes=["Cg%d" % s])
                    S.join()
                    hy_norm_all(l, s)
                    S.join()
                    continue
                yn = fft_pass(L, ZT[s], zn, "conv", 0, dst=YT[s], dst_tag="YT%d_0" % s)
                S.join()
                pc = hy_post(l, s, 0, yn, un, gn)
                S.join()
                zn = ["ZT%d:p%d" % (s, c) for c in pc]
                un = ["UU%d:p%d" % (s, c) for c in pc]
                yn = fft_pass(L, ZT[s], zn, "conv", 1, dst=YT[s], dst_tag="YT%d_1" % s)
                S.join()
                hy_post(l, s, 1, yn, un, gn)
                S.join()
            if stop_after in ("mix", "ext", "c0", "g0", "g1"):
                break
            for s in range(nseq):
                phase5(l, s, n3[s])
                S.join()
            if stop_after == "p5":
                break
            phase6(l, list(range(nseq)))
            S.join()
        if stop_after is None:
            for s in range(nseq):
                phase7(s)
        S.finish("sp")
        nc._n_sched_instr = S.ninstr
    return nc


_CACHE = {}


def host_inputs(inp, Ls, depth):
    qp = q_perm()
    w_in = np.array(inp["w_in"][:depth], np.float32, copy=True)
    w_in[:, :, :512] = w_in[:, :, :512][:, :, qp]
    w_out = np.array(inp["w_out"][:depth], np.float32, copy=True)
    w_out[:, :512, :] = w_out[:, :512, :][:, qp, :]
    m = {
        "w_in": w_in, "w_out": w_out,
        "w_ffn_in": np.ascontiguousarray(inp["w_ffn_in"][:depth], np.float32),
        "w_ffn_out": np.ascontiguousarray(inp["w_ffn_out"][:depth], np.float32),
        "pool_w": np.ascontiguousarray(inp["pool_w"][:depth], np.float32),
        "filt_w1": np.ascontiguousarray(inp["filt_w1"][:depth], np.float32),
        "filt_w2": np.ascontiguousarray(inp["filt_w2"][:depth], np.float32),
        "filt_w3": np.ascontiguousarray(inp["filt_w3"][:depth], np.float32),
        "pp": np.stack([pack_pp(inp, l) for l in range(depth)]),
        "hyb_raw": np.ascontiguousarray(inp["hy_bias"][:depth], np.float32),
        "c_alibi": alibi_table().reshape(128, -1),
        "c_ndelta": neg_deltas(),
        "c_ident": np.eye(128, dtype=np.float32),
        "c_bones": np.kron(np.eye(2, dtype=np.float32), np.ones((64, 64), np.float32)),
    }
    for L in sorted(set(Ls)):
        FA, TB, G, TD, TW = fft_tables(L)
        m["c_TW%d" % L] = TW
        m["c_z%d" % L] = z_table(L)
        m["c_FA%d" % L] = FA
        m["c_TB%d" % L] = TB
        m["c_G%d" % L] = G
        m["c_TD%d" % L] = TD
    return m


def run(inp, xs_per_core, Ls, depth, dbg=(), stop_after=None, shard=None):
    key = (tuple(Ls), depth, tuple(dbg), stop_after, tuple(shard) if shard else None)
    if key not in _CACHE:
        _CACHE[key] = build_program(Ls, depth, dbg, stop_after, shard)
    nc = _CACHE[key]
    shared = host_inputs(inp, Ls, depth)
    in_maps = []
    for core in range(8):
        mm = dict(shared)
        for s in range(len(Ls)):
            mm["x%d" % s] = np.ascontiguousarray(xs_per_core[core][s], np.float32)
        in_maps.append(mm)
    res = run_bass_kernel_spmd(nc, in_maps, core_ids=list(range(8)))
    return res.results


def kernel(**inputs):
    inp = {k: np.asarray(v) for k, v in inputs.items()}
    xp = inp["x_prompt"]
    xs = inp["x_sample"]
    LA, LB = xp.shape[1], xs.shape[1]
    depth = inp["w_in"].shape[0]
    xs_per_core = [[xp[c], xs[0]] for c in range(8)]
    results = run(inp, xs_per_core, [LA, LB], depth)
    y_prompt = np.stack([results[c]["y0"] for c in range(8)]).astype(np.float32)
    y_sample = results[0]["y1"][None].astype(np.float32)
    return (y_prompt, y_sample)
```

```python
import contextlib
import math

import numpy as np
import ml_dtypes

import concourse.bass as bass
import concourse.mybir as mybir
from concourse.bass_utils import run_bass_kernel_spmd

F32 = mybir.dt.float32
BF16 = mybir.dt.bfloat16
I32 = mybir.dt.int32
ALU = mybir.AluOpType
AF = mybir.ActivationFunctionType

D = 1024
DIN = 1792
DFF = 2816
EPS = 1e-6
NQH = 8
TWO_PI = 2.0 * math.pi


class Sched:
    def __init__(self, nc, estack, n_dma_sems=16):
        self.nc = nc
        self.eng = {"pe": nc.tensor, "act": nc.scalar, "dve": nc.vector,
                    "pool": nc.gpsimd, "sp": nc.sync}
        self.sem = {k: estack.enter_context(nc.semaphore("s_" + k)) for k in self.eng}
        self.cnt = {k: 0 for k in self.eng}
        self.waited = {k: {} for k in self.eng}
        self.dsems = [estack.enter_context(nc.semaphore("d%d" % i)) for i in range(n_dma_sems)]
        self.dtarget = [0] * n_dma_sems
        self.dnext = 0
        self.last_write = {}
        self.readers = {}
        self.ninstr = 0

    def _wait(self, e, dep):
        h = self.eng[e]
        if dep[0] == "dma":
            _, si, tgt = dep
            key = ("dma", si)
            if self.waited[e].get(key, 0) >= tgt:
                return
            h.wait_ge(self.dsems[si], tgt)
            self.waited[e][key] = tgt
        else:
            oe, idx = dep
            if oe == e and e == "pe":
                return
            if self.waited[e].get(oe, 0) >= idx:
                return
            h.wait_ge(self.sem[oe], idx)
            self.waited[e][oe] = idx
        self.ninstr += 1

    def _deps(self, e, reads, writes):
        deps = []
        for b in reads:
            lw = self.last_write.get(b)
            if lw is not None:
                deps.append(lw)
        for b in writes:
            lw = self.last_write.get(b)
            if lw is not None and lw[0] != e:
                deps.append(lw)
            for d in self.readers.get(b, {}).values():
                if d[0] != e:
                    deps.append(d)
        return deps

    def join(self, q="sp"):
        for e in self.eng:
            for si, t in enumerate(self.dtarget):
                if t > 0:
                    self._wait(e, ("dma", si, t))
            for oe in self.eng:
                if oe != e and self.cnt[oe] > 0:
                    self._wait(e, (oe, self.cnt[oe]))

    def _record(self, me, reads, writes):
        key = me[0] if me[0] != "dma" else ("dma", me[1])
        for b in reads:
            self.readers.setdefault(b, {})[key] = me
        for b in writes:
            self.last_write[b] = me
            self.readers[b] = {}

    def op(self, e, fn, reads=(), writes=()):
        for d in self._deps(e, reads, writes):
            self._wait(e, d)
        inst = fn(self.eng[e])
        self.cnt[e] += 1
        inst.then_inc(self.sem[e], 1)
        self._record((e, self.cnt[e]), reads, writes)
        self.ninstr += 1

    def dma(self, q, out, in_, reads=(), writes=()):
        for d in self._deps(None, reads, writes):
            self._wait(q, d)
        si = self.dnext
        self.dnext = (self.dnext + 1) % len(self.dsems)
        if self.dtarget[si] > 0:
            self._wait(q, ("dma", si, self.dtarget[si]))
        inst = self.eng[q].dma_start(out=out, in_=in_)
        self.dtarget[si] += 16
        inst.then_inc(self.dsems[si], 16)
        self._record(("dma", si, self.dtarget[si]), reads, writes)
        self.ninstr += 1

    def collective(self, estack, kind, in_ap, out_ap, scratch_ap, reads=(), writes=()):
        for d in self._deps("pool", reads, writes):
            self._wait("pool", d)
        g = self.nc.gpsimd
        csem = estack.enter_context(self.nc.semaphore("cc%d" % self.ninstr))
        g.collective_compute(kind, mybir.AluOpType.bypass, replica_groups=[list(range(8))],
                             ins=[in_ap], outs=[out_ap]).then_inc(csem)
        g.wait_ge(csem, 1)
        self.op("pool", lambda e: e.memset(scratch_ap, 0.0), reads=reads, writes=list(writes) + ["ccscratch"])

    def finish(self, q="sp"):
        for si, t in enumerate(self.dtarget):
            if t > 0:
                self._wait(q, ("dma", si, t))
        for e in self.eng:
            if e != q and self.cnt[e] > 0:
                self._wait(q, (e, self.cnt[e]))


def rnames(base, t0, t1, gran=512):
    return ["%s:%d" % (base, i) for i in range(t0 // gran, (t1 - 1) // gran + 1)]


def q_perm():
    idx = []
    for i in range(4):
        idx.extend(range(i * 64, i * 64 + 64))
        idx.extend(range((4 + i) * 64, (4 + i) * 64 + 64))
    return np.array(idx)


def alibi_table():
    h = np.arange(NQH, dtype=np.float32)
    slopes = (2.0 ** (-8.0 * (h + 1.0) / NQH)).astype(np.float32)
    j = np.arange(128)[:, None]
    q = np.arange(128)[None, :]
    out = np.zeros((128, 3, 2, 4, 128), np.float32)
    for rel in range(3):
        dist = np.abs(q - (j + (rel - 1) * 128)).astype(np.float32)
        for kvh in range(2):
            for i in range(4):
                b = -slopes[4 * kvh + i] * dist
                b = np.where(dist <= 128, b, -1e30)
                out[:, rel, kvh, i, :] = b
    return out


def fft_tables(L):
    J = L // 128
    N1 = 2 * J
    N = 128 * N1
    bf = ml_dtypes.bfloat16
    Jp = np.arange(J)[:, None].astype(np.float64)
    k1 = np.arange(N1)[None, :].astype(np.float64)
    ang = TWO_PI * Jp * k1 / N1
    FA = np.concatenate([np.cos(ang), -np.sin(ang)], axis=1)
    j = np.arange(128).astype(np.float64)
    k2 = np.arange(128).astype(np.float64)
    kk1 = np.arange(N1).astype(np.float64)
    ph = TWO_PI * np.outer(j, k2) / 128.0
    TB = np.stack([np.cos(ph), -np.sin(ph), np.sin(ph)], axis=1)
    pt = TWO_PI * np.outer(j, kk1) / N
    TW = np.stack([np.cos(pt), np.cos(pt), -np.sin(pt), -np.sin(pt)], axis=1).astype(np.float32)
    a = TWO_PI * np.outer(k2, j) / 128.0
    Gr, Gi = np.cos(a), np.sin(a)
    G = np.stack([np.concatenate([Gr, Gi], 1), np.concatenate([-Gi, Gr], 1)], axis=1)
    n1 = np.arange(128).astype(np.float64)
    n2 = np.arange(J).astype(np.float64)
    e = ((n1[:, None, None] + 128 * n2[None, None, :]) * kk1[None, :, None]) % N
    psi = TWO_PI * e / N
    TD = np.stack([np.cos(psi), -np.sin(psi)], axis=2)
    return (FA.astype(np.float32).astype(bf), TB.astype(np.float32).astype(bf),
            G.astype(np.float32).astype(bf), TD.astype(np.float32).astype(bf), TW)


def z_table(L):
    t_norm = np.linspace(0.0, 1.0, L, dtype=np.float32)[:, None]
    n = np.arange(L, dtype=np.float32)[:, None]
    bands = np.linspace(1e-4, 15, 16, dtype=np.float32)[None, :]
    ang = (np.float32(TWO_PI) * n * bands / np.float32(L)).astype(np.float32)
    z = np.concatenate([t_norm, np.cos(ang), -np.sin(ang)], axis=-1).astype(np.float32)
    return np.ascontiguousarray(z.T)


def neg_deltas():
    max_decay = math.log(1e-2) / 0.3
    min_decay = math.log(1e-2) / 1.5
    d = np.abs(np.linspace(min_decay, max_decay, 256, dtype=np.float32))
    return (-d).reshape(1, 256).astype(np.float32)


PP = {}
_o = 0
for _n, _w in [("g1", 8), ("g2", 8), ("gq", 1), ("gk", 1), ("sink", 8), ("pscale", 2),
               ("hcw", 18), ("hcb", 6), ("fb1", 1), ("ff1", 1), ("fb2", 1), ("ff2", 1),
               ("hyb", 4), ("gout", 8), ("fcw", 132), ("fcb", 44)]:
    PP[_n] = (_o, _w)
    _o += _w
NPP = _o


def pack_pp(inp, l):
    pp = np.zeros((128, NPP), np.float32)

    def put(name, arr):
        o, w = PP[name]
        pp[:, o:o + w] = arr.reshape(128, w)

    qp = q_perm()
    put("g1", inp["norm1_g"][l].reshape(8, 128).T)
    put("g2", inp["norm2_g"][l].reshape(8, 128).T)
    put("gq", np.tile(inp["q_norm_g"][l], 2))
    put("gk", np.tile(inp["k_norm_g"][l], 2))
    put("sink", np.tile(inp["attn_sink"][l][None, :], (128, 1)))
    put("pscale", inp["pool_scale"][l].reshape(2, 128).T)
    put("hcw", inp["hy_conv_w"][l].reshape(3, 6, 128).transpose(2, 1, 0))
    put("hcb", inp["hy_conv_b"][l].reshape(6, 128).T)
    z64 = np.zeros(64, np.float32)
    put("fb1", np.concatenate([inp["filt_b1"][l], z64]))
    put("ff1", np.concatenate([inp["filt_freq1"][l], z64]))
    put("fb2", np.concatenate([inp["filt_b2"][l], z64]))
    put("ff2", np.concatenate([inp["filt_freq2"][l], z64]))
    put("hyb", inp["hy_bias"][l].reshape(2, 2, 128).transpose(2, 0, 1))
    g = inp["out_norm_g"][l].copy()
    g[:512] = g[:512][qp]
    put("gout", g.reshape(8, 128).T)
    put("fcw", inp["ffn_conv_w"][l].reshape(3, 44, 128).transpose(2, 1, 0))
    put("fcb", inp["ffn_conv_b"][l].reshape(44, 128).T)
    return pp


def build_program(Ls, depth, dbg=(), stop_after=None, shard=None):
    nc = bass.Bass("TRN2", target_bir_lowering=False)
    nseq = len(Ls)
    shard = list(shard) if shard is not None else [False] * nseq
    uL = sorted(set(L for L, sh in zip(Ls, shard) if not sh))
    uLs = sorted(set(L for L, sh in zip(Ls, shard) if sh))
    allL = sorted(set(Ls))
    NCO = 32

    def din(name, shape, dt=F32):
        return nc.dram_tensor(name, list(shape), dt, kind="ExternalInput").ap()

    def dscr(name, shape, dt=F32):
        kind = "ExternalOutput" if name in dbg else "Internal"
        return nc.dram_tensor(name, list(shape), dt, kind=kind).ap()

    x_in = [din("x%d" % s, [Ls[s], D]) for s in range(nseq)]
    y_out = [nc.dram_tensor("y%d" % s, [Ls[s], D], F32, kind="ExternalOutput").ap() for s in range(nseq)]
    w_in = din("w_in", [depth, D, DIN])
    w_out = din("w_out", [depth, D, D])
    w_fi = din("w_ffn_in", [depth, D, 2 * DFF])
    w_fo = din("w_ffn_out", [depth, DFF, D])
    pool_w = din("pool_w", [depth, 4, 64, 64])
    fw1 = din("filt_w1", [depth, 33, 64])
    fw2 = din("filt_w2", [depth, 64, 64])
    fw3 = din("filt_w3", [depth, 64, 1024])
    pp_in = din("pp", [depth, 128, NPP])
    hyb_raw = din("hyb_raw", [depth, 2, 256])
    c_alibi = din("c_alibi", [128, 3 * 2 * 4 * 128])
    c_ndelta = din("c_ndelta", [1, 256])
    c_ident = din("c_ident", [128, 128])
    c_bones = din("c_bones", [128, 128])
    cz, cFA, cTB, cG, cTD, cTW = {}, {}, {}, {}, {}, {}
    for L in allL:
        J = L // 128
        N1 = 2 * J
        cz[L] = din("c_z%d" % L, [33, L])
        cFA[L] = din("c_FA%d" % L, [J, 2 * N1], BF16)
        cTB[L] = din("c_TB%d" % L, [128, 3, 128], BF16)
        cTW[L] = din("c_TW%d" % L, [128, 4, N1])
        cG[L] = din("c_G%d" % L, [128, 2, 256], BF16)
        cTD[L] = din("c_TD%d" % L, [128, N1, 2, J], BF16)

    XA = [dscr("XA%d" % s, [D, Ls[s]]) for s in range(nseq)]
    XB = [dscr("XB%d" % s, [D, Ls[s]]) for s in range(nseq)]
    QK = [dscr("QK%d" % s, [640, Ls[s]], BF16) for s in range(nseq)]
    VV = [dscr("VV%d" % s, [Ls[s], 128], BF16) for s in range(nseq)]
    PH = [dscr("PH%d" % s, [1024, Ls[s]]) for s in range(nseq)]
    CT = [dscr("CT%d" % s, [D, Ls[s]], BF16) for s in range(nseq)]
    ZT = [dscr("ZT%d" % s, [256, Ls[s]], BF16) for s in range(nseq)]
    UU = [dscr("UU%d" % s, [256, Ls[s]]) for s in range(nseq)]
    GT = [dscr("GT%d" % s, [512, Ls[s]]) for s in range(nseq)]
    YT = [dscr("YT%d" % s, [256, Ls[s]]) for s in range(nseq)]
    FILT = {L: dscr("FILT%d" % L, [4, 256, L], BF16) for L in uL}
    KS = {}
    for L in uL:
        N1 = 2 * (L // 128)
        KS[L] = dscr("KS%d" % L, [2, 128, 256 * N1 * 2], BF16)
    FILTs = {L: dscr("FILTs%d" % L, [128, L], BF16) for L in uLs}
    KSs = {L: dscr("KSs%d" % L, [2, 128, NCO * 2 * (L // 128) * 2], BF16) for L in uLs}
    shs = [s_ for s_ in range(nseq) if shard[s_]]
    ZTo = {s_: dscr("ZTo%d" % s_, [NCO, Ls[s_]], BF16) for s_ in shs}
    Uo = {s_: dscr("Uo%d" % s_, [NCO, Ls[s_]]) for s_ in shs}
    Go = {s_: dscr("Go%d" % s_, [2, NCO, Ls[s_]]) for s_ in shs}
    YTo = {s_: dscr("YTo%d" % s_, [NCO, Ls[s_]]) for s_ in shs}
    Co = {s_: nc.dram_tensor("Co%d" % s_, [NCO, Ls[s_]], F32) for s_ in shs}
    Cg = {s_: nc.dram_tensor("Cg%d" % s_, [256, Ls[s_]], F32) for s_ in shs}
    WB_in = dscr("WB_in", [D, DIN], BF16)
    WB_out = dscr("WB_out", [D, D], BF16)
    WB_fi = dscr("WB_fi", [D, 2 * DFF], BF16)
    WB_fo = dscr("WB_fo", [DFF, D], BF16)

    with contextlib.ExitStack() as es:
        S = Sched(nc, es)

        uid = [0]

        def sbt(st, name, shape, dt):
            uid[0] += 1
            return st.enter_context(nc.sbuf_tensor("%s_%d" % (name, uid[0]), list(shape), dt))

        dumped = set()

        def dump(name, ap, shape, dt, reads):
            if name not in dbg or name in dumped:
                return
            dumped.add(name)
            d = nc.dram_tensor(name, list(shape), dt, kind="ExternalOutput").ap()
            S.dma("sp", d, ap, reads=reads)

        pid = nc.partition_id()
        ccs = sbt(es, "ccs", [128, 1], F32)

        identF = sbt(es, "identF", [128, 128], F32)
        identB = sbt(es, "identB", [128, 128], BF16)
        onesB = sbt(es, "onesB", [128, 128], BF16)
        bonesB = sbt(es, "bonesB", [128, 128], BF16)
        epsT = sbt(es, "epsT", [128, 1], F32)
        ppT = sbt(es, "ppT", [128, NPP], F32)
        esink = sbt(es, "esink", [128, 8], F32)
        fb1f = sbt(es, "fb1f", [128, 2], F32)
        S.dma("sp", identF[:], c_ident, writes=["identF"])
        S.dma("pool", identB[:], c_ident, writes=["identB"])
        S.dma("pool", bonesB[:], c_bones, writes=["bonesB"])
        S.op("pool", lambda e: e.memset(onesB[:], 1.0), writes=["onesB"])
        S.op("pool", lambda e: e.memset(epsT[:], EPS), writes=["epsT"])

        psb = [es.enter_context(nc.psum_tensor("ps%d" % i, [128, 512], F32)) for i in range(7)]
        psT = es.enter_context(nc.psum_tensor("psT", [128, 1024], BF16))
        psctr = [0]

        def nextps(lo=0, hi=7):
            i = lo + psctr[0] % (hi - lo)
            psctr[0] += 1
            return i

        def ppc(name, j=0, n=1):
            o, w = PP[name]
            return ppT[:, o + j:o + j + n]

        def rsqrt_ln(out_ap, in_ap, scale, rd, wr, npart=128):
            S.op("act", lambda e: e.activation(out=out_ap, in_=in_ap, func=AF.Ln,
                                               bias=epsT[0:npart, :], scale=scale), reads=rd, writes=wr)
            S.op("act", lambda e: e.activation(out=out_ap, in_=out_ap, func=AF.Exp, scale=-0.5),
                 reads=wr, writes=wr)

        def load_layer_params(l):
            S.dma("sp", ppT[:], pp_in[l], writes=["ppT"])
            o, w = PP["sink"]
            S.op("act", lambda e: e.activation(out=esink[:], in_=ppT[:, o:o + w], func=AF.Exp),
                 reads=["ppT"], writes=["esink"])
            S.op("dve", lambda e: e.tensor_tensor(out=fb1f[:, 0:1], in0=ppc("fb1"), in1=ppc("ff1"), op=ALU.mult),
                 reads=["ppT"], writes=["fb1f"])
            S.op("dve", lambda e: e.tensor_tensor(out=fb1f[:, 1:2], in0=ppc("fb2"), in1=ppc("ff2"), op=ALU.mult),
                 reads=["ppT", "fb1f"], writes=["fb1f"])
            for r in range(0, D, 128):
                S.dma("pool", WB_in[r:r + 128, :], w_in[l, r:r + 128, :], writes=["WB_in:%d" % r])
                S.dma("pool", WB_out[r:r + 128, :], w_out[l, r:r + 128, :], writes=["WB_out:%d" % r])
                S.dma("pool", WB_fi[r:r + 128, :], w_fi[l, r:r + 128, :], writes=["WB_fi:%d" % r])
            for r in range(0, DFF, 128):
                S.dma("pool", WB_fo[r:r + 128, :], w_fo[l, r:r + 128, :], writes=["WB_fo:%d" % r])

        def phase1(l, s, first):
            L = Ls[s]
            nch = L // 512
            XAv = XA[s].rearrange("(k p) t -> p k t", p=128)
            QKv = QK[s].rearrange("(k p) t -> p k t", p=128)
            PHv = PH[s].rearrange("(k p) t -> p k t", p=128)
            VVv = VV[s].rearrange("(n p) f -> p n f", p=128)
            with contextlib.ExitStack() as st:
                win = sbt(st, "win", [128, 8, DIN], BF16)
                S.dma("sp", win[:], WB_in.rearrange("(k p) n -> p k n", p=128), reads=["WB_in:%d" % r for r in range(0, D, 128)], writes=["win"])
                xt = [sbt(st, "xt%d" % i, [128, 8, 512], F32) for i in range(2)]
                xtok = [sbt(st, "xtok%d" % i, [128, 4, D], F32) for i in range(2)] if first else None
                sq = sbt(st, "sq", [128, 8, 512], BF16)
                hT = [sbt(st, "hT%d" % i, [128, 8, 512], BF16) for i in range(2)]
                rstd = sbt(st, "rstd", [128, 512], F32)
                sqh = [sbt(st, "sqh%d" % i, [128, 512], BF16) for i in range(2)]
                rq = [sbt(st, "rq%d" % i, [128, 512], F32) for i in range(2)]
                qko = [sbt(st, "qko%d" % i, [128, 5, 512], BF16) for i in range(2)]
                vo = [sbt(st, "vo%d" % i, [128, 4, 128], BF16) for i in range(2)]
                pho = [sbt(st, "pho%d" % i, [128, 8, 512], F32) for i in range(2)]

                def load(c):
                    b = c % 2
                    if first:
                        S.dma("sp", xtok[b][:], x_in[s][c * 512:(c + 1) * 512, :].rearrange("(n p) d -> p n d", p=128),
                              writes=["xtok%d" % b])
                    else:
                        S.dma("sp", xt[b][:], XAv[:, :, c * 512:(c + 1) * 512],
                              reads=rnames("XA%d" % s, c * 512, (c + 1) * 512), writes=["xt%d" % b])

                load(0)
                for c in range(nch):
                    b = c % 2
                    if c + 1 < nch:
                        load(c + 1)
                    if first:
                        for k in range(8):
                            pi = nextps(0, 2)
                            for n in range(4):
                                S.op("pe", lambda e, pi=pi, n=n, k=k: e.transpose(
                                    psb[pi][:, n * 128:(n + 1) * 128], xtok[b][:, n, k * 128:(k + 1) * 128], identF[:]),
                                    reads=["xtok%d" % b, "identF"], writes=["ps%d" % pi])
                            eng = "act" if k % 2 == 0 else "dve"
                            if eng == "act":
                                S.op("act", lambda e, pi=pi, k=k: e.activation(out=xt[b][:, k, :], in_=psb[pi][:], func=AF.Copy),
                                     reads=["ps%d" % pi], writes=["xt%d:%d" % (b, k)])
                            else:
                                S.op("dve", lambda e, pi=pi, k=k: e.tensor_copy(out=xt[b][:, k, :], in_=psb[pi][:]),
                                     reads=["ps%d" % pi], writes=["xt%d:%d" % (b, k)])
                        xtn = ["xt%d:%d" % (b, k) for k in range(8)]
                        S.dma("sp", XAv[:, :, c * 512:(c + 1) * 512], xt[b][:], reads=xtn,
                              writes=rnames("XA%d" % s, c * 512, (c + 1) * 512))
                    else:
                        xtn = ["xt%d" % b]
                    S.op("pool", lambda e: e.tensor_tensor(out=sq[:], in0=xt[b][:], in1=xt[b][:], op=ALU.mult),
                         reads=xtn, writes=["sq"])
                    pss = nextps(2, 7)
                    for k in range(8):
                        S.op("pe", lambda e, k=k: e.matmul(psb[pss][:], onesB[:], sq[:, k, :], start=(k == 0), stop=(k == 7)),
                             reads=["sq", "onesB"], writes=["ps%d" % pss])
                    rsqrt_ln(rstd[:], psb[pss][:], 1.0 / D, ["ps%d" % pss, "epsT"], ["rstd"])
                    for k in range(8):
                        S.op("dve", lambda e, k=k: e.scalar_tensor_tensor(
                            out=hT[b][:, k, :], in0=xt[b][:, k, :], scalar=ppc("g1", k), in1=rstd[:],
                            op0=ALU.mult, op1=ALU.mult), reads=xtn + ["rstd", "ppT"], writes=["hT%d" % b])

                    def proj(m, pi):
                        for k in range(8):
                            S.op("pe", lambda e, k=k: e.matmul(psb[pi][:], win[:, k, m * 128:(m + 1) * 128], hT[b][:, k, :],
                                                               start=(k == 0), stop=(k == 7)),
                                 reads=["win", "hT%d" % b], writes=["ps%d" % pi])

                    pend = None
                    for m in range(5):
                        pi = nextps(2, 7)
                        proj(m, pi)
                        hb = m % 2
                        S.op("act", lambda e, pi=pi, hb=hb: e.activation(out=sqh[hb][:], in_=psb[pi][:], func=AF.Square),
                             reads=["ps%d" % pi], writes=["sqh%d" % hb])
                        if pend is not None:
                            pend()

                        def fin(m=m, pi=pi, hb=hb):
                            p2 = nextps(2, 7)
                            S.op("pe", lambda e: e.matmul(psb[p2][:], bonesB[:], sqh[hb][:], start=True, stop=True),
                                 reads=["sqh%d" % hb, "bonesB"], writes=["ps%d" % p2])
                            rsqrt_ln(rq[hb][:], psb[p2][:], 1.0 / 64, ["ps%d" % p2, "epsT"], ["rq%d" % hb])
                            gname = "gq" if m < 4 else "gk"
                            S.op("dve", lambda e: e.scalar_tensor_tensor(
                                out=qko[b][:, m, :], in0=psb[pi][:], scalar=ppc(gname), in1=rq[hb][:],
                                op0=ALU.mult, op1=ALU.mult), reads=["ps%d" % pi, "rq%d" % hb, "ppT"], writes=["qko%d" % b])
                        pend = fin
                    pend()
                    S.dma("sp", QKv[:, :, c * 512:(c + 1) * 512], qko[b][:], reads=["qko%d" % b],
                          writes=rnames("QK%d" % s, c * 512, (c + 1) * 512))
                    pv = nextps(2, 7)
                    for n in range(4):
                        for k in range(8):
                            S.op("pe", lambda e, n=n, k=k: e.matmul(psb[pv][:, n * 128:(n + 1) * 128], hT[b][:, k, n * 128:(n + 1) * 128],
                                                                   win[:, k, 640:768], start=(k == 0), stop=(k == 7)),
                                 reads=["win", "hT%d" % b], writes=["ps%d" % pv])
                    S.op("act", lambda e: e.activation(out=vo[b][:].rearrange("p n f -> p (n f)"), in_=psb[pv][:], func=AF.Copy),
                         reads=["ps%d" % pv], writes=["vo%d" % b])
                    S.dma("sp", VVv[:, c * 4:(c + 1) * 4, :], vo[b][:], reads=["vo%d" % b],
                          writes=rnames("VV%d" % s, c * 512, (c + 1) * 512))
                    for m in range(6, 14):
                        pi = nextps(2, 7)
                        proj(m, pi)
                        if m % 2 == 0:
                            S.op("act", lambda e, pi=pi, m=m: e.activation(out=pho[b][:, m - 6, :], in_=psb[pi][:], func=AF.Copy),
                                 reads=["ps%d" % pi], writes=["pho%d:%d" % (b, m)])
                        else:
                            S.op("dve", lambda e, pi=pi, m=m: e.tensor_copy(out=pho[b][:, m - 6, :], in_=psb[pi][:]),
                                 reads=["ps%d" % pi], writes=["pho%d:%d" % (b, m)])
                    S.dma("sp", PHv[:, :, c * 512:(c + 1) * 512], pho[b][:],
                          reads=["pho%d:%d" % (b, m) for m in range(6, 14)],
                          writes=rnames("PH%d" % s, c * 512, (c + 1) * 512))

        def phase2(l, s):
            L = Ls[s]
            nch = L // 512
            nb = L // 128
            QKv = QK[s].rearrange("(k p) t -> p k t", p=128)
            VVv = VV[s].rearrange("(n p) (h d) -> p n h d", p=128, h=2)
            CTv = CT[s].rearrange("(k p) t -> p k t", p=128)
            with contextlib.ExitStack() as st:
                bias = sbt(st, "abias", [128, 3, 2, 512], F32)
                S.dma("sp", bias[:].rearrange("p a b c -> p (a b c)"), c_alibi, writes=["abias"])
                qT = [sbt(st, "qT%d" % i, [128, 4, 512], BF16) for i in range(2)]
                kT = [sbt(st, "kT%d" % i, [128, 768], BF16) for i in range(2)]
                v1 = [sbt(st, "v1%d" % i, [128, 6, 2, 65], BF16) for i in range(2)]
                for i in range(2):
                    S.op("pool", lambda e, i=i: e.memset(v1[i][:], 1.0), writes=["v1%d:0" % i, "v1%d:1" % i])
                sT = [sbt(st, "sT%d" % i, [128, 512], F32) for i in range(3)]
                pT = [sbt(st, "pT%d" % i, [128, 512], BF16) for i in range(6)]
                den = sbt(st, "den", [128, 8], F32)
                aa = [sbt(st, "aa%d" % i, [128, 4, 2, 64], F32) for i in range(2)]
                junk = sbt(st, "junk", [128, 512], F32)
                ssa = sbt(st, "ssa", [128, 1], F32)
                an = [sbt(st, "an%d" % i, [128, 512], BF16) for i in range(2)]
                catA = [sbt(st, "catA%d" % i, [128, 4, 512], BF16) for i in range(2)]

                def load(c):
                    b = c % 2
                    S.dma("sp", qT[b][:], QKv[:, 0:4, c * 512:(c + 1) * 512],
                          reads=rnames("QK%d" % s, c * 512, (c + 1) * 512), writes=["qT%d" % b])
                    t0 = max(0, c * 512 - 128)
                    t1 = min(L, c * 512 + 640)
                    o0 = t0 - (c * 512 - 128)
                    S.dma("sp", kT[b][:, o0:o0 + (t1 - t0)], QK[s][512:640, t0:t1],
                          reads=rnames("QK%d" % s, t0, t1), writes=["kT%d" % b])
                    for hh in range(2):
                        S.dma("sp", v1[b][:, o0 // 128:o0 // 128 + (t1 - t0) // 128, hh, 0:64], VVv[:, t0 // 128:t1 // 128, hh, :],
                              reads=rnames("VV%d" % s, t0, t1), writes=["v1%d:%d" % (b, hh)])

                pT2 = pT + [sbt(st, "pTb%d" % i, [128, 512], BF16) for i in range(6)]
                den2 = [den, sbt(st, "den_b", [128, 8], F32)]
                ssa2 = [ssa, sbt(st, "ssa_b", [128, 1], F32)]
                sT4 = sT + [sbt(st, "sT3", [128, 512], F32)]
                sctr = [0]

                def kbs_of(g):
                    return [kb for kb in (g - 1, g, g + 1) if 0 <= kb < nb]

                def stage1(c, qb):
                    b = c % 2
                    g = 4 * c + qb
                    st_ = g % 2
                    for kvh in range(2):
                        pr = slice(64 * kvh, 64 * kvh + 64)
                        for kb in kbs_of(g):
                            rel = kb - g + 1
                            slot = kb - (4 * c - 1)
                            pi = nextps(0, 3)
                            for i in range(4):
                                S.op("pe", lambda e, pi=pi, i=i, slot=slot: e.matmul(
                                    psb[pi][:, i * 128:(i + 1) * 128], kT[b][pr, slot * 128:(slot + 1) * 128],
                                    qT[b][pr, i, qb * 128:(qb + 1) * 128], start=True, stop=True),
                                    reads=["kT%d" % b, "qT%d" % b], writes=["ps%d" % pi])
                            si = sctr[0] % 4
                            sctr[0] += 1
                            S.op("dve", lambda e, pi=pi, si=si, rel=rel: e.scalar_tensor_tensor(
                                out=sT4[si][:], in0=psb[pi][:], scalar=0.125, in1=bias[:, rel, kvh, :],
                                op0=ALU.mult, op1=ALU.add), reads=["ps%d" % pi, "abias"], writes=["sT%d" % si])
                            pidx = st_ * 6 + kvh * 3 + rel
                            S.op("act", lambda e, si=si, pidx=pidx: e.activation(out=pT2[pidx][:], in_=sT4[si][:], func=AF.Exp),
                                 reads=["sT%d" % si], writes=["pT%d" % pidx])

                def stage2(c, qb):
                    b = c % 2
                    g = 4 * c + qb
                    st_ = g % 2
                    ab = g % 2
                    kbs = kbs_of(g)
                    pO = [3 + 2 * (g % 2), 4 + 2 * (g % 2)]
                    den_, ssa_ = den2[ab], ssa2[ab]
                    dn, sn = "den%d" % ab, "ssa%d" % ab
                    for kvh in range(2):
                        for i in range(4):
                            for n_, kb in enumerate(kbs):
                                rel = kb - g + 1
                                slot = kb - (4 * c - 1)
                                pidx = st_ * 6 + kvh * 3 + rel
                                S.op("pe", lambda e, i=i, slot=slot, pidx=pidx, n_=n_, kvh=kvh: e.matmul(
                                    psb[pO[kvh]][:, i * 65:(i + 1) * 65], pT2[pidx][:, i * 128:(i + 1) * 128],
                                    v1[b][:, slot, kvh, :], start=(n_ == 0), stop=(n_ == len(kbs) - 1)),
                                    reads=["pT%d" % pidx, "v1%d:%d" % (b, kvh)], writes=["ps%d" % pO[kvh]])
                    for kvh in range(2):
                        pov = psb[pO[kvh]][:, 0:260].rearrange("p (i d) -> p i d", d=65)
                        S.op("dve", lambda e, kvh=kvh, pov=pov: e.tensor_tensor(
                            out=den_[:, kvh * 4:(kvh + 1) * 4], in0=pov[:, :, 64], in1=esink[:, kvh * 4:(kvh + 1) * 4], op=ALU.add),
                            reads=["ps%d" % pO[kvh], "esink"], writes=[dn])
                    S.op("dve", lambda e: e.reciprocal(out=den_[:], in_=den_[:]), reads=[dn], writes=[dn])
                    for kvh in range(2):
                        pov = psb[pO[kvh]][:, 0:260].rearrange("p (i d) -> p i d", d=65)
                        for i in range(4):
                            S.op("dve", lambda e, kvh=kvh, i=i, pov=pov: e.tensor_scalar(
                                out=aa[ab][:, i, kvh, :], in0=pov[:, i, 0:64], scalar1=den_[:, kvh * 4 + i:kvh * 4 + i + 1],
                                scalar2=None, op0=ALU.mult), reads=["ps%d" % pO[kvh], dn], writes=["aa%d" % ab])
                    aflat = aa[ab][:].rearrange("p i k d -> p (i k d)")
                    S.op("act", lambda e: e.activation(out=junk[:], in_=aflat, func=AF.Square, accum_out=ssa_[:]),
                         reads=["aa%d" % ab], writes=["junk", sn])
                    rsqrt_ln(ssa_[:], ssa_[:], 1.0 / 512, [sn, "epsT"], [sn])
                    S.op("dve", lambda e: e.tensor_scalar(out=an[ab][:], in0=aflat, scalar1=ssa_[:, 0:1], scalar2=None, op0=ALU.mult),
                         reads=["aa%d" % ab, sn], writes=["an%d" % ab])
                    for i in range(4):
                        S.op("pe", lambda e, i=i: e.transpose(psT[:, i * 128:(i + 1) * 128], an[ab][:, i * 128:(i + 1) * 128], identB[:]),
                             reads=["an%d" % ab, "identB"], writes=["psT"])
                    for i in range(4):
                        S.op("dve", lambda e, i=i: e.tensor_scalar(
                            out=catA[b][:, i, qb * 128:(qb + 1) * 128], in0=psT[:, i * 128:(i + 1) * 128],
                            scalar1=ppc("gout", i), scalar2=None, op0=ALU.mult),
                            reads=["psT", "ppT"], writes=["catA%d" % b])
                    if qb == 3:
                        S.dma("sp", CTv[:, 0:4, c * 512:(c + 1) * 512], catA[b][:], reads=["catA%d" % b],
                              writes=rnames("CTa%d" % s, c * 512, (c + 1) * 512))

                blocks = [(c, qb) for c in range(nch) for qb in range(4)]
                load(0)
                if nch > 1:
                    load(1)
                stage1(*blocks[0])
                for i_, (c, qb) in enumerate(blocks):
                    if i_ + 1 < len(blocks):
                        stage1(*blocks[i_ + 1])
                    stage2(c, qb)
                    if qb == 3 and c + 2 < nch:
                        load(c + 2)

        def phase3(l, s):
            L = Ls[s]
            WO = 496
            PHv = PH[s].rearrange("(k p) t -> p k t", p=128)
            CTv = CT[s].rearrange("(k p) t -> p k t", p=128)
            with contextlib.ExitStack() as st:
                wbd = sbt(st, "wbd", [128, 2, 128], BF16)
                S.op("pool", lambda e: e.memset(wbd[:], 0.0), writes=["wbd"])
                for g in range(4):
                    ci, hf = g // 2, g % 2
                    S.dma("pool", wbd[64 * hf:64 * hf + 64, ci, 64 * hf:64 * hf + 64], pool_w[l, g], reads=["wbd"], writes=["wbd%d" % g])
                wbdn = ["wbd%d" % g for g in range(4)]
                u = [sbt(st, "pu%d" % i, [128, 2, 512], F32) for i in range(2)]
                a1_ = [sbt(st, "pa1_%d" % i, [128, 2, 512], F32) for i in range(2)]
                a2_ = [sbt(st, "pa2_%d" % i, [128, 2, 512], F32) for i in range(2)]
                t1_ = [sbt(st, "pt1_%d" % i, [128, 2, 512], F32) for i in range(2)]
                dT_ = [sbt(st, "pdT_%d" % i, [128, 2, 512], BF16) for i in range(2)]
                yb_ = [sbt(st, "pyb_%d" % i, [128, 2, 512], F32) for i in range(2)]
                sqb_ = [sbt(st, "psqb_%d" % i, [128, 2, 512], BF16) for i in range(2)]
                rsb_ = [sbt(st, "prsb_%d" % i, [128, 512], F32) for i in range(2)]
                cb = [sbt(st, "pcb%d" % i, [128, 2, 512], BF16) for i in range(2)]
                wins = list(range(0, L, WO))

                def load(wi):
                    b = wi % 2
                    s0 = wins[wi]
                    n = min(WO, L - s0)
                    t0, t1_ = max(0, s0 - 8), min(L, s0 + n + 8)
                    if t0 > s0 - 8 or t1_ < s0 + n + 8 or n < WO:
                        S.op("pool", lambda e: e.memset(u[b][:], 0.0), writes=["pu%d" % b])
                    o0 = t0 - (s0 - 8)
                    S.dma("sp", u[b][:, :, o0:o0 + (t1_ - t0)], PHv[:, 0:2, t0:t1_],
                          reads=rnames("PH%d" % s, t0, t1_), writes=["pu%d" % b])

                load(0)
                for wi, s0 in enumerate(wins):
                    b = wi % 2
                    n = min(WO, L - s0)
                    if wi + 1 < len(wins):
                        load(wi + 1)
                    W = n + 16
                    ub = u[b]
                    a1, a2, t1, dT, yb, sqb, rsb = a1_[b], a2_[b], t1_[b], dT_[b], yb_[b], sqb_[b], rsb_[b]
                    S.op("dve", lambda e: e.tensor_tensor(out=a1[:, :, 0:W - 1], in0=ub[:, :, 0:W - 1], in1=ub[:, :, 1:W], op=ALU.add),
                         reads=["pu%d" % b], writes=["pa1_%d" % b])
                    S.op("dve", lambda e: e.tensor_tensor(out=a2[:, :, 0:W - 3], in0=a1[:, :, 0:W - 3], in1=a1[:, :, 2:W - 1], op=ALU.add),
                         reads=["pa1_%d" % b], writes=["pa2_%d" % b])
                    S.op("dve", lambda e: e.tensor_tensor(out=a1[:, 1, 0:W - 7], in0=a2[:, 1, 0:W - 7], in1=a2[:, 1, 4:W - 3], op=ALU.add),
                         reads=["pa2_%d" % b, "pa1_%d" % b], writes=["pa1b_%d" % b])
                    S.op("dve", lambda e: e.tensor_tensor(out=a2[64:128, 1, 0:W - 15], in0=a1[64:128, 1, 0:W - 15], in1=a1[64:128, 1, 8:W - 7], op=ALU.add),
                         reads=["pa1b_%d" % b, "pa2_%d" % b], writes=["pa2b_%d" % b])
                    srcs = [(a1, 0, 0, 7, 2, "pa1_%d" % b), (a2, 0, 1, 6, 4, "pa2_%d" % b), (a1, 1, 0, 4, 8, "pa1b_%d" % b), (a2, 1, 1, 0, 16, "pa2b_%d" % b)]
                    for (src, ci, hf, off, w, nm) in srcs:
                        pr = slice(64 * hf, 64 * hf + 64)
                        S.op("dve", lambda e, src=src, ci=ci, pr=pr, off=off, w=w: e.scalar_tensor_tensor(
                            out=t1[pr, ci, 0:n], in0=src[pr, ci, off:off + n], scalar=1.0 / w, in1=ub[pr, ci, 8:8 + n],
                            op0=ALU.mult, op1=ALU.subtract), reads=[nm, "pu%d" % b, "pa1_%d" % b, "pa2_%d" % b], writes=["pt1_%d" % b])
                        h = w // 2
                        edge = []
                        for tok in list(range(0, h)) + list(range(L - h, L)):
                            lo = max(0, tok - h)
                            hi = min(L, tok - h + w)
                            cnt = hi - lo
                            if cnt != w and s0 <= tok < s0 + n:
                                edge.append((tok - s0, cnt))
                        for (col, cnt) in edge:
                            S.op("dve", lambda e, src=src, ci=ci, pr=pr, off=off, col=col, cnt=cnt: e.scalar_tensor_tensor(
                                out=t1[pr, ci, col:col + 1], in0=src[pr, ci, off + col:off + col + 1], scalar=1.0 / cnt,
                                in1=ub[pr, ci, 8 + col:9 + col], op0=ALU.mult, op1=ALU.subtract),
                                reads=[nm, "pu%d" % b, "pt1_%d" % b], writes=["pt1_%d" % b])
                    S.op("dve", lambda e: e.tensor_copy(out=dT[:, :, 0:n], in_=t1[:, :, 0:n]), reads=["pt1_%d" % b], writes=["pdT_%d" % b])
                    dump("d_t1", t1[:], [128, 2, 512], F32, ["pt1_%d" % b])
                    dump("d_a1", a1[:], [128, 2, 512], F32, ["pa1_%d" % b, "pa1b_%d" % b])
                    dump("d_u", ub[:], [128, 2, 512], F32, ["pu%d" % b])
                    dump("d_wbd", wbd[:], [128, 2, 128], BF16, wbdn)
                    pis = []
                    for ci in range(2):
                        pi = nextps(0, 7)
                        pis.append(pi)
                        S.op("pe", lambda e, ci=ci, pi=pi: e.matmul(psb[pi][:, 0:n], wbd[:, ci, :], dT[:, ci, 0:n], start=True, stop=True),
                             reads=["pdT_%d" % b] + wbdn, writes=["ps%d" % pi])
                        S.op("act", lambda e, ci=ci, pi=pi: e.activation(out=yb[:, ci, 0:n], in_=psb[pi][:, 0:n], func=AF.Identity,
                                                                         scale=ppc("pscale", ci)),
                             reads=["ps%d" % pi, "ppT"], writes=["pyb%d_%d" % (ci, b)])
                    ybn = ["pyb0_%d" % b, "pyb1_%d" % b]
                    S.op("dve", lambda e: e.tensor_tensor(out=sqb[:, :, 0:n], in0=yb[:, :, 0:n], in1=yb[:, :, 0:n], op=ALU.mult),
                         reads=ybn, writes=["psqb_%d" % b])
                    pi = nextps(0, 7)
                    for ci in range(2):
                        S.op("pe", lambda e, ci=ci: e.matmul(psb[pi][:, 0:n], onesB[:], sqb[:, ci, 0:n], start=(ci == 0), stop=(ci == 1)),
                             reads=["psqb_%d" % b, "onesB"], writes=["ps%d" % pi])
                    rsqrt_ln(rsb[:, 0:n], psb[pi][:, 0:n], 1.0 / 256, ["ps%d" % pi, "epsT"], ["prsb_%d" % b])
                    dump("d_yb", yb[:], [128, 2, 512], F32, ybn)
                    dump("d_rsb", rsb[:], [128, 512], F32, ["prsb_%d" % b])
                    for ci in range(2):
                        S.op("dve", lambda e, ci=ci: e.scalar_tensor_tensor(
                            out=cb[b][:, ci, 0:n], in0=yb[:, ci, 0:n], scalar=ppc("gout", 4 + ci), in1=rsb[:, 0:n],
                            op0=ALU.mult, op1=ALU.mult), reads=ybn + ["prsb_%d" % b, "ppT"], writes=["pcb%d" % b])
                    S.dma("sp", CTv[:, 4:6, s0:s0 + n], cb[b][:, :, 0:n], reads=["pcb%d" % b],
                          writes=["CTb%d:%d" % (s, wi)])
                return len(wins)

        def phase4a(l, s):
            L = Ls[s]
            WO = 510
            PHv = PH[s].rearrange("(k p) t -> p k t", p=128)
            ZTv = ZT[s].rearrange("(k p) t -> p k t", p=128)
            UUv = UU[s].rearrange("(k p) t -> p k t", p=128)
            GTv = GT[s].rearrange("(k p) t -> p k t", p=128)
            with contextlib.ExitStack() as st:
                hin = [sbt(st, "hin%d" % i, [128, 6, 512], F32) for i in range(2)]
                ta = sbt(st, "hta", [128, 6, 512], F32)
                ho = [sbt(st, "hho%d" % i, [128, 6, 512], F32) for i in range(2)]
                hz = [sbt(st, "hhz%d" % i, [128, 2, 512], BF16) for i in range(2)]
                wins = list(range(0, L, WO))

                def load(wi):
                    b = wi % 2
                    s0 = wins[wi]
                    n = min(WO, L - s0)
                    t0, t1_ = max(0, s0 - 1), min(L, s0 + n + 1)
                    if t0 > s0 - 1 or t1_ < s0 + n + 1:
                        S.op("pool", lambda e: e.memset(hin[b][:], 0.0), writes=["hin%d" % b])
                    o0 = t0 - (s0 - 1)
                    S.dma("sp", hin[b][:, :, o0:o0 + (t1_ - t0)], PHv[:, 2:8, t0:t1_],
                          reads=rnames("PH%d" % s, t0, t1_), writes=["hin%d" % b])

                load(0)
                for wi, s0 in enumerate(wins):
                    b = wi % 2
                    n = min(WO, L - s0)
                    if wi + 1 < len(wins):
                        load(wi + 1)
                    o, _ = PP["hcw"]
                    for k in range(6):
                        w0 = ppT[:, o + 3 * k:o + 3 * k + 1]
                        w1 = ppT[:, o + 3 * k + 1:o + 3 * k + 2]
                        w2 = ppT[:, o + 3 * k + 2:o + 3 * k + 3]
                        S.op("act", lambda e, k=k, w1=w1: e.activation(out=ta[:, k, 0:n], in_=hin[b][:, k, 1:n + 1], func=AF.Identity,
                                                                      bias=ppc("hcb", k), scale=w1),
                             reads=["hin%d" % b, "ppT"], writes=["hta%d" % k])
                        S.op("dve", lambda e, k=k, w0=w0: e.scalar_tensor_tensor(
                            out=ta[:, k, 0:n], in0=hin[b][:, k, 0:n], scalar=w0, in1=ta[:, k, 0:n], op0=ALU.mult, op1=ALU.add),
                            reads=["hin%d" % b, "hta%d" % k, "ppT"], writes=["hta%d" % k])
                        S.op("dve", lambda e, k=k, w2=w2: e.scalar_tensor_tensor(
                            out=ho[b][:, k, 0:n], in0=hin[b][:, k, 2:n + 2], scalar=w2, in1=ta[:, k, 0:n], op0=ALU.mult, op1=ALU.add),
                            reads=["hin%d" % b, "hta%d" % k, "ppT"], writes=["hho%d" % b])
                    S.op("pool", lambda e: e.tensor_copy(out=hz[b][:, :, 0:n], in_=ho[b][:, 0:2, 0:n]), reads=["hho%d" % b], writes=["hhz%d" % b])
                    S.dma("sp", ZTv[:, :, s0:s0 + n], hz[b][:, :, 0:n], reads=["hhz%d" % b], writes=["ZT%d:w%d" % (s, wi)])
                    S.dma("sp", UUv[:, :, s0:s0 + n], ho[b][:, 0:2, 0:n], reads=["hho%d" % b], writes=["UU%d:w%d" % (s, wi)])
                    S.dma("sp", GTv[:, :, s0:s0 + n], ho[b][:, 2:6, 0:n], reads=["hho%d" % b], writes=["GT%d:w%d" % (s, wi)])
                return len(wins)

        def sin_reduced(out_ap, arg_ap, nI, nF, npart, rd, wr, sfx=""):
            S.op("dve", lambda e: e.tensor_scalar(out=nI, in0=arg_ap, scalar1=1.0 / TWO_PI, scalar2=None, op0=ALU.mult),
                 reads=rd, writes=["nI" + sfx])
            S.op("dve", lambda e: e.tensor_copy(out=nF, in_=nI), reads=["nI" + sfx], writes=["nF" + sfx])
            S.op("dve", lambda e: e.scalar_tensor_tensor(out=arg_ap, in0=nF, scalar=-TWO_PI, in1=arg_ap, op0=ALU.mult, op1=ALU.add),
                 reads=["nF" + sfx] + rd, writes=rd)
            S.op("act", lambda e: e.activation(out=out_ap, in_=arg_ap, func=AF.Sin), reads=rd, writes=wr)

        def filters(l, L):
            nchp = L // 512
            FLv = FILT[L].rearrange("f (k p) t -> p f k t", p=128)
            with contextlib.ExitStack() as st:
                w1 = sbt(st, "fw1", [33, 64], F32)
                w2 = sbt(st, "fw2", [64, 64], F32)
                w3 = sbt(st, "fw3", [64, 1024], F32)
                nd = sbt(st, "fnd", [1, 256], F32)
                S.dma("sp", w1[:], fw1[l], writes=["fw1"])
                S.dma("sp", w2[:], fw2[l], writes=["fw2"])
                S.dma("sp", w3[:], fw3[l], writes=["fw3"])
                S.dma("sp", nd[:], c_ndelta, writes=["fnd"])
                zt = [sbt(st, "fzt%d" % i, [33, 512], F32) for i in range(2)]
                arg_ = [sbt(st, "farg%d" % i, [64, 512], F32) for i in range(2)]
                nI_ = [sbt(st, "fnI%d" % i, [64, 512], I32) for i in range(2)]
                nF_ = [sbt(st, "fnF%d" % i, [64, 512], F32) for i in range(2)]
                h1_ = [sbt(st, "fh1%d" % i, [64, 512], F32) for i in range(2)]
                h2_ = [sbt(st, "fh2%d" % i, [64, 512], F32) for i in range(2)]
                dec_ = [sbt(st, "fdec%d" % i, [128, 2, 512], F32) for i in range(2)]
                fo = [sbt(st, "ffo%d" % i, [128, 4, 2, 512], BF16) for i in range(2)]
                S.dma("sp", zt[0][:], cz[L][:, 0:512], writes=["fzt0"])
                for pc in range(nchp):
                    b = pc % 2
                    arg, nI, nF, h1, h2, dec = arg_[b], nI_[b], nF_[b], h1_[b], h2_[b], dec_[b]
                    if pc + 1 < nchp:
                        S.dma("sp", zt[1 - b][:], cz[L][:, (pc + 1) * 512:(pc + 2) * 512], writes=["fzt%d" % (1 - b)])
                    p1 = nextps(0, 7)
                    S.op("pe", lambda e: e.matmul(psb[p1][0:64, :], w1[:], zt[b][:], start=True, stop=True),
                         reads=["fw1", "fzt%d" % b], writes=["ps%d" % p1])
                    S.op("dve", lambda e: e.tensor_scalar(out=arg[:], in0=psb[p1][0:64, :], scalar1=ppT[0:64, PP["ff1"][0]:PP["ff1"][0] + 1],
                                                          scalar2=fb1f[0:64, 0:1], op0=ALU.mult, op1=ALU.add),
                         reads=["ps%d" % p1, "ppT", "fb1f"], writes=["farg%d" % b])
                    sin_reduced(h1[:], arg[:], nI[:], nF[:], 64, ["farg%d" % b], ["fh1%d" % b], sfx=str(b))
                    p2 = nextps(0, 7)
                    S.op("pe", lambda e: e.matmul(psb[p2][0:64, :], w2[:], h1[:], start=True, stop=True),
                         reads=["fw2", "fh1%d" % b], writes=["ps%d" % p2])
                    S.op("dve", lambda e: e.tensor_scalar(out=arg[:], in0=psb[p2][0:64, :], scalar1=ppT[0:64, PP["ff2"][0]:PP["ff2"][0] + 1],
                                                          scalar2=fb1f[0:64, 1:2], op0=ALU.mult, op1=ALU.add),
                         reads=["ps%d" % p2, "ppT", "fb1f"], writes=["farg%d" % b])
                    sin_reduced(h2[:], arg[:], nI[:], nF[:], 64, ["farg%d" % b], ["fh2%d" % b], sfx=str(b))
                    for cc in range(2):
                        pd = nextps(0, 7)
                        S.op("pe", lambda e, cc=cc, pd=pd: e.matmul(psb[pd][:], nd[0:1, cc * 128:(cc + 1) * 128], zt[b][0:1, :], start=True, stop=True),
                             reads=["fnd", "fzt%d" % b], writes=["ps%d" % pd])
                        S.op("act", lambda e, cc=cc, pd=pd: e.activation(out=dec[:, cc, :], in_=psb[pd][:], func=AF.Exp),
                             reads=["ps%d" % pd], writes=["fdec%d_%d" % (b, cc)])
                    for fi in range(4):
                        for cc in range(2):
                            m = fi * 2 + cc
                            p3 = nextps(0, 7)
                            S.op("pe", lambda e, m=m, p3=p3: e.matmul(psb[p3][:], w3[:, m * 128:(m + 1) * 128], h2[:], start=True, stop=True),
                                 reads=["fw3", "fh2%d" % b], writes=["ps%d" % p3])
                            S.op("dve", lambda e, fi=fi, cc=cc, p3=p3: e.tensor_tensor(out=fo[b][:, fi, cc, :], in0=psb[p3][:], in1=dec[:, cc, :], op=ALU.mult),
                                 reads=["ps%d" % p3, "fdec%d_%d" % (b, cc)], writes=["ffo%d" % b])
                            if pc == 0 and fi % 2 == 1:
                                S.op("dve", lambda e, fi=fi, cc=cc: e.memset(fo[b][:, fi, cc, 0:1], 0.0), reads=["ffo%d" % b], writes=["ffo%d" % b])
                    S.dma("sp", FLv[:, :, :, pc * 512:(pc + 1) * 512], fo[b][:], reads=["ffo%d" % b], writes=["FILT%d:%d" % (L, pc)])
            return ["FILT%d:%d" % (L, pc) for pc in range(nchp)]

        def fft_pass(L, src, src_names, mode, order, dst=None, dst_tag=None, C=256, ks=None, kstag=None):
            J = L // 128
            N1 = 2 * J
            N = 128 * N1
            CC = min(64, max(16, 8192 // N1), C)
            if ks is None:
                ks = KS[L][order]
                kstag = "KS%d_%d" % (L, order)
            KG = 256 // CC
            KC = max(1, N1 // 128)
            MK = min(N1, 128)
            NG = 512 // CC
            srcv = src.rearrange("c (a j) -> a c j", j=128)
            GC = min(512 // N1, CC)
            ncol = GC * N1
            KSo = ks.rearrange("p (cc g r n) -> p cc g r n", cc=C // CC, g=CC // GC, r=2)
            TDv = cTD[L].rearrange("n (kc k) r j -> k n kc r j", k=MK)
            with contextlib.ExitStack() as st:
                FAs = sbt(st, "FAs", [J, 2 * N1], BF16)
                S.dma("sp", FAs[:], cFA[L], writes=["FAs"])
                Gs = sbt(st, "Gs", [128, 2, 256], BF16)
                S.dma("sp", Gs[:], cG[L], writes=["Gs"])
                Fs = sbt(st, "Fs", [128, 3, 128], BF16)
                S.dma("sp", Fs[:], cTB[L], writes=["Fs"])
                Tw = sbt(st, "Tw", [128, 4, N1], F32)
                S.dma("sp", Tw[:], cTW[L], writes=["Tw"])
                Ysb = sbt(st, "Ysb", [128, 2, CC, N1], BF16)
                ksl = [sbt(st, "ksl%d" % i, [128, 2, ncol], BF16) for i in range(3)]
                kso = [sbt(st, "kso%d" % i, [128, 2, ncol], BF16) for i in range(3)]
                tm = [sbt(st, "ftm%d" % i, [128, ncol], F32) for i in range(8)]
                NPQ = 4
                nbk = max(1, min(512 // (2 * N1), CC))
                PQ = [sbt(st, "fPQ%d" % i, [128, 2, nbk, 2, N1], F32) for i in range(NPQ)]
                TwR = sbt(st, "TwR", [128, nbk, 4, N1], F32)
                for i in range(nbk):
                    S.dma("sp", TwR[:, i, :, :], cTW[L], writes=["TwR"] if i == nbk - 1 else ["TwR_%d" % i])
                S.join()
                xin = sbt(st, "xin", [J, CC, 128], BF16)
                A = sbt(st, "Abuf", [128, 2, CC, N1], BF16)
                if mode == "conv":
                    Zs = sbt(st, "Zsbuf", [128, KC, 2, CC, 128], BF16)
                    yo = sbt(st, "yobuf", [J, CC, 128], F32)
                    TDs = [sbt(st, "TDs%d" % i, [128, 8, KC, 2, J], BF16) for i in range(2)]
                for cc in range(C // CC):
                    c0 = cc * CC
                    S.dma("sp", xin[:], srcv[:, c0:c0 + CC, :], reads=src_names, writes=["xin"])
                    if True:
                        for c0_ in range(0, CC, nbk):
                            pi = nextps(0, 7)
                            for ci_ in range(nbk):
                                c = c0_ + ci_
                                S.op("pe", lambda e, c=c, ci_=ci_, pi=pi: e.matmul(psb[pi][:, ci_ * 2 * N1:(ci_ + 1) * 2 * N1], xin[:, c, :], FAs[:], start=True, stop=True),
                                     reads=["xin", "FAs"], writes=["ps%d" % pi])
                            pv = psb[pi][:, 0:nbk * 2 * N1].rearrange("p (c r k) -> p c r k", c=nbk, r=2)
                            qi = (c0_ // nbk) % NPQ
                            pq = PQ[qi]
                            pqn = "fPQ%d" % qi
                            S.op("dve", lambda e, pv=pv, pq=pq: e.tensor_tensor(out=pq[:, 0, :, :, :], in0=pv, in1=TwR[:, :, 0:2, :], op=ALU.mult),
                                 reads=["ps%d" % pi, "TwR"], writes=[pqn + "P"])
                            S.op("dve", lambda e, pv=pv, pq=pq: e.tensor_tensor(out=pq[:, 1, :, :, :], in0=pv, in1=TwR[:, :, 2:4, :], op=ALU.mult),
                                 reads=["ps%d" % pi, "TwR"], writes=[pqn + "Q"])
                            csl = slice(c0_, c0_ + nbk)
                            S.op("pool", lambda e, csl=csl, pq=pq: e.tensor_tensor(out=A[:, 0, csl, :], in0=pq[:, 0, :, 0, :], in1=pq[:, 1, :, 1, :], op=ALU.subtract),
                                 reads=[pqn + "P", pqn + "Q"], writes=["AZr:%d" % c for c in range(c0_, c0_ + nbk)])
                            S.op("pool", lambda e, csl=csl, pq=pq: e.tensor_tensor(out=A[:, 1, csl, :], in0=pq[:, 1, :, 0, :], in1=pq[:, 0, :, 1, :], op=ALU.add),
                                 reads=[pqn + "P", pqn + "Q"], writes=["AZi:%d" % c for c in range(c0_, c0_ + nbk)])
                        ngrp = CC // GC
                        if mode != "kf":
                            S.dma("sp", ksl[0][:], KSo[:, cc, 0, :, :], reads=["%s:%d:%d" % (kstag, cc, 0)], writes=["ksl0"])
                        for g in range(ngrp):
                            if mode != "kf" and g + 1 < ngrp:
                                S.dma("sp", ksl[(g + 1) % 3][:], KSo[:, cc, g + 1, :, :],
                                      reads=["%s:%d:%d" % (kstag, cc, g + 1)], writes=["ksl%d" % ((g + 1) % 3)])
                            pr_ = nextps(0, 7)
                            pi_ = nextps(0, 7)
                            gsl = slice(g * GC, (g + 1) * GC)
                            rdr = ["AZr:%d" % c for c in range(g * GC, (g + 1) * GC)] + ["Fs"]
                            rdi = ["AZi:%d" % c for c in range(g * GC, (g + 1) * GC)] + ["Fs"]
                            Ar_ = A[:, 0, gsl, :].rearrange("p c k -> p (c k)")
                            Ai_ = A[:, 1, gsl, :].rearrange("p c k -> p (c k)")
                            S.op("pe", lambda e: e.matmul(psb[pr_][:, 0:ncol], Fs[:, 0, :], Ar_, start=True, stop=False), reads=rdr, writes=["ps%d" % pr_])
                            S.op("pe", lambda e: e.matmul(psb[pr_][:, 0:ncol], Fs[:, 2, :], Ai_, start=False, stop=True), reads=rdi, writes=["ps%d" % pr_])
                            S.op("pe", lambda e: e.matmul(psb[pi_][:, 0:ncol], Fs[:, 1, :], Ar_, start=True, stop=False), reads=rdr, writes=["ps%d" % pi_])
                            S.op("pe", lambda e: e.matmul(psb[pi_][:, 0:ncol], Fs[:, 0, :], Ai_, start=False, stop=True), reads=rdi, writes=["ps%d" % pi_])
                            Xr = psb[pr_][:, 0:ncol]
                            Xi = psb[pi_][:, 0:ncol]
                            pnr, pni = ["ps%d" % pr_], ["ps%d" % pi_]
                            ksn = "%s:%d:%d" % (kstag, cc, g)
                            if mode == "kf":
                                ko = kso[g % 3]
                                S.op("act", lambda e, ko=ko: e.activation(out=ko[:, 0, :], in_=Xr, func=AF.Copy, scale=1.0 / N),
                                     reads=pnr, writes=["kso%d" % (g % 3)])
                                S.op("act", lambda e, ko=ko: e.activation(out=ko[:, 1, :], in_=Xi, func=AF.Copy, scale=1.0 / N),
                                     reads=pni, writes=["kso%d" % (g % 3)])
                                S.dma("sp", KSo[:, cc, g, :, :], ko[:], reads=["kso%d" % (g % 3)], writes=[ksn])
                            elif mode == "kb":
                                ko = kso[g % 3]
                                kl = ksl[g % 3]
                                S.op("dve", lambda e, ko=ko, kl=kl: e.scalar_tensor_tensor(out=ko[:, 0, :], in0=Xr, scalar=1.0 / N, in1=kl[:, 0, :],
                                                                                           op0=ALU.mult, op1=ALU.add),
                                     reads=pnr + ["ksl%d" % (g % 3)], writes=["kso%d" % (g % 3)])
                                S.op("dve", lambda e, ko=ko, kl=kl: e.scalar_tensor_tensor(out=ko[:, 1, :], in0=Xi, scalar=-1.0 / N, in1=kl[:, 1, :],
                                                                                           op0=ALU.mult, op1=ALU.add),
                                     reads=pni + ["ksl%d" % (g % 3), "kso%d" % (g % 3)], writes=["kso%d" % (g % 3)])
                                S.dma("sp", KSo[:, cc, g, :, :], ko[:], reads=["kso%d" % (g % 3)], writes=[ksn])
                            else:
                                kl = ksl[g % 3]
                                kn = "ksl%d" % (g % 3)
                                Kr, Ki = kl[:, 0, :], kl[:, 1, :]
                                t4 = tm[4 * (g % 2):4 * (g % 2) + 4]
                                tn = ["ftm%d" % (4 * (g % 2) + i) for i in range(4)]
                                Yr_ = Ysb[:, 0, gsl, :].rearrange("p c k -> p (c k)")
                                Yi_ = Ysb[:, 1, gsl, :].rearrange("p c k -> p (c k)")
                                S.op("dve", lambda e: e.tensor_tensor(out=t4[0][:], in0=Xr, in1=Kr, op=ALU.mult), reads=pnr + [kn], writes=[tn[0]])
                                S.op("dve", lambda e: e.tensor_tensor(out=t4[1][:], in0=Xi, in1=Ki, op=ALU.mult), reads=pni + [kn], writes=[tn[1]])
                                S.op("dve", lambda e: e.tensor_tensor(out=t4[2][:], in0=Xr, in1=Ki, op=ALU.mult), reads=pnr + [kn], writes=[tn[2]])
                                S.op("dve", lambda e: e.tensor_tensor(out=t4[3][:], in0=Xi, in1=Kr, op=ALU.mult), reads=pni + [kn], writes=[tn[3]])
                                S.op("pool", lambda e: e.tensor_tensor(out=Yr_, in0=t4[0][:], in1=t4[1][:], op=ALU.subtract),
                                     reads=[tn[0], tn[1]], writes=["Ysr:%d" % g])
                                S.op("pool", lambda e: e.tensor_tensor(out=Yi_, in0=t4[2][:], in1=t4[3][:], op=ALU.add),
                                     reads=[tn[2], tn[3]], writes=["Ysi:%d" % g])
                    if mode != "conv":
                        continue
                    if True:
                        for c in range(CC):
                            for kc in range(KC):
                                pi = nextps(0, 7)
                                S.op("pe", lambda e, c=c, kc=kc, pi=pi: e.matmul(psb[pi][0:MK, 0:256], Ysb[:, 0, c, kc * 128:kc * 128 + MK], Gs[:, 0, :], start=True, stop=False),
                                     reads=["Ysr:%d" % (c // GC), "Gs"], writes=["ps%d" % pi])
                                S.op("pe", lambda e, c=c, kc=kc, pi=pi: e.matmul(psb[pi][0:MK, 0:256], Ysb[:, 1, c, kc * 128:kc * 128 + MK], Gs[:, 1, :], start=False, stop=True),
                                     reads=["Ysi:%d" % (c // GC), "Gs"], writes=["ps%d" % pi])
                                pv = psb[pi][0:MK, 0:256].rearrange("p (r n) -> p r n", r=2)
                                if (c + kc) % 2 == 0:
                                    S.op("act", lambda e, c=c, kc=kc, pv=pv: e.activation(out=Zs[0:MK, kc, :, c, :], in_=pv, func=AF.Copy),
                                         reads=["ps%d" % pi], writes=["Zsa"])
                                else:
                                    S.op("dve", lambda e, c=c, kc=kc, pv=pv: e.tensor_copy(out=Zs[0:MK, kc, :, c, :], in_=pv),
                                         reads=["ps%d" % pi], writes=["Zsd"])
                        def loadTD(gi):
                            S.dma("sp", TDs[gi % 2][0:MK, :, :, :, :], TDv[:, gi * 8:(gi + 1) * 8, :, :, :], writes=["TDs%d" % (gi % 2)])

                        loadTD(0)
                        for ng in range(128 // NG):
                            pi = nextps(0, 7)
                            pY = psb[pi][0:J, :].rearrange("p (g c) -> p g c", g=NG)
                            for gg in range(NG):
                                n1 = ng * NG + gg
                                gi, ti = n1 // 8, n1 % 8
                                if ti == 0 and (gi + 1) * 8 < 128:
                                    loadTD(gi + 1)
                                td = TDs[gi % 2]
                                cnt = 0
                                for kc in range(KC):
                                    for r in range(2):
                                        S.op("pe", lambda e, gg=gg, n1=n1, td=td, ti=ti, kc=kc, r=r, cnt=cnt: e.matmul(
                                            pY[:, gg, :], td[0:MK, ti, kc, r, :], Zs[0:MK, kc, r, :, n1],
                                            start=(cnt == 0), stop=(cnt == 2 * KC - 1)),
                                            reads=["Zsa", "Zsd", "TDs%d" % (gi % 2)], writes=["ps%d" % pi])
                                        cnt += 1
                            ov = yo[:, :, ng * NG:(ng + 1) * NG].rearrange("p c g -> p g c")
                            if ng % 2 == 0:
                                S.op("act", lambda e, ov=ov, pY=pY: e.activation(out=ov, in_=pY, func=AF.Copy), reads=["ps%d" % pi], writes=["yo_a"])
                            else:
                                S.op("dve", lambda e, ov=ov, pY=pY: e.tensor_copy(out=ov, in_=pY), reads=["ps%d" % pi], writes=["yo_d"])
                        S.dma("sp", dst.rearrange("c (a j) -> a c j", j=128)[:, c0:c0 + CC, :], yo[:], reads=["yo_a", "yo_d"],
                              writes=["%s:%d" % (dst_tag, cc)])
            return ["%s:%d" % (dst_tag, cc) for cc in range(C // CC)]

        def hy_post(l, s, order, ynames, unames, gnames):
            L = Ls[s]
            nch = L // 512
            YTv = YT[s].rearrange("(k p) t -> p k t", p=128)
            UUv = UU[s].rearrange("(k p) t -> p k t", p=128)
            GTv = GT[s].rearrange("(k p) t -> p k t", p=128)
            ZTv = ZT[s].rearrange("(k p) t -> p k t", p=128)
            CTv = CT[s].rearrange("(k p) t -> p k t", p=128)
            outn = []
            with contextlib.ExitStack() as st:
                yb = [sbt(st, "qy%d" % i, [128, 2, 512], F32) for i in range(2)]
                ub = [sbt(st, "qu%d" % i, [128, 2, 512], F32) for i in range(2)]
                gb = [sbt(st, "qg%d" % i, [128, 2, 512], F32) for i in range(2)]
                zz = [sbt(st, "qz%d" % i, [128, 2, 512], F32) for i in range(2)]
                zb = [sbt(st, "qzb%d" % i, [128, 2, 512], BF16) for i in range(2)]
                sqz = sbt(st, "qsq", [128, 2, 512], BF16)
                rs = sbt(st, "qrs", [128, 512], F32)

                def load(c):
                    b = c % 2
                    sl = slice(c * 512, (c + 1) * 512)
                    S.dma("sp", yb[b][:], YTv[:, :, sl], reads=ynames, writes=["qy%d" % b])
                    S.dma("sp", ub[b][:], UUv[:, :, sl], reads=unames, writes=["qu%d" % b])
                    S.dma("sp", gb[b][:], GTv[:, 2 * order:2 * order + 2, sl], reads=gnames, writes=["qg%d" % b])

                load(0)
                for c in range(nch):
                    b = c % 2
                    sl = slice(c * 512, (c + 1) * 512)
                    if c + 1 < nch:
                        load(c + 1)
                    o, _ = PP["hyb"]
                    for cc in range(2):
                        S.op("dve", lambda e, cc=cc: e.scalar_tensor_tensor(
                            out=zz[b][:, cc, :], in0=ub[b][:, cc, :], scalar=ppT[:, o + 2 * order + cc:o + 2 * order + cc + 1],
                            in1=yb[b][:, cc, :], op0=ALU.mult, op1=ALU.add),
                            reads=["qu%d" % b, "qy%d" % b, "ppT"], writes=["qz%d:%d" % (b, cc)])
                    zn = ["qz%d:0" % b, "qz%d:1" % b]
                    S.op("pool", lambda e: e.tensor_tensor(out=zz[b][:], in0=zz[b][:], in1=gb[b][:], op=ALU.mult),
                         reads=zn + ["qg%d" % b], writes=zn)
                    if order == 0:
                        S.op("act", lambda e: e.activation(out=zb[b][:], in_=zz[b][:], func=AF.Copy), reads=zn, writes=["qzb%d" % b])
                        S.dma("sp", ZTv[:, :, sl], zb[b][:], reads=["qzb%d" % b], writes=["ZT%d:p%d" % (s, c)])
                        S.dma("sp", UUv[:, :, sl], zz[b][:], reads=zn, writes=["UU%d:p%d" % (s, c)])
                        outn.append(c)
                    else:
                        S.op("pool", lambda e: e.tensor_tensor(out=sqz[:], in0=zz[b][:], in1=zz[b][:], op=ALU.mult), reads=zn, writes=["qsq"])
                        pi = nextps(0, 7)
                        for cc in range(2):
                            S.op("pe", lambda e, cc=cc: e.matmul(psb[pi][:], onesB[:], sqz[:, cc, :], start=(cc == 0), stop=(cc == 1)),
                                 reads=["qsq", "onesB"], writes=["ps%d" % pi])
                        rsqrt_ln(rs[:], psb[pi][:], 1.0 / 256, ["ps%d" % pi, "epsT"], ["qrs"])
                        for cc in range(2):
                            S.op("dve", lambda e, cc=cc: e.scalar_tensor_tensor(
                                out=zb[b][:, cc, :], in0=zz[b][:, cc, :], scalar=ppc("gout", 6 + cc), in1=rs[:],
                                op0=ALU.mult, op1=ALU.mult), reads=zn + ["qrs", "ppT"], writes=["qzb%d" % b])
                        S.dma("sp", CTv[:, 6:8, sl], zb[b][:], reads=["qzb%d" % b], writes=["CTc%d:%d" % (s, c)])
            return outn


        FORD = [1, 3, 0, 2]

        def filters_sh(l, L):
            nchp = L // 512
            w3v = fw3[l].rearrange("k (f c) -> k f c", f=4)
            with contextlib.ExitStack() as st:
                w1 = sbt(st, "gw1", [33, 64], F32)
                w2 = sbt(st, "gw2", [64, 64], F32)
                w3o = sbt(st, "gw3", [64, 4, NCO], F32)
                nd4 = sbt(st, "gnd", [1, 4, NCO], F32)
                S.dma("sp", w1[:], fw1[l], writes=["fw1"])
                S.dma("sp", w2[:], fw2[l], writes=["fw2"])
                for sl, fi in enumerate(FORD):
                    S.dma("sp", w3o[:, sl, :], w3v[:, fi, bass.ds(pid * NCO, NCO)], writes=["fw3:%d" % sl])
                    S.dma("sp", nd4[:, sl, :], c_ndelta[:, bass.ds(pid * NCO, NCO)], writes=["fnd:%d" % sl])
                w3n = ["fw3:%d" % sl for sl in range(4)]
                ndn = ["fnd:%d" % sl for sl in range(4)]
                zt = [sbt(st, "gzt%d" % i, [33, 512], F32) for i in range(2)]
                arg = sbt(st, "garg", [64, 512], F32)
                nI = sbt(st, "gnI", [64, 512], I32)
                nF = sbt(st, "gnF", [64, 512], F32)
                h1 = sbt(st, "gh1", [64, 512], F32)
                h2 = sbt(st, "gh2", [64, 512], F32)
                dec = sbt(st, "gdec", [128, 512], F32)
                fo = [sbt(st, "gfo%d" % i, [128, 512], BF16) for i in range(2)]
                S.dma("sp", zt[0][:], cz[L][:, 0:512], writes=["fzt0"])
                for pc in range(nchp):
                    b = pc % 2
                    if pc + 1 < nchp:
                        S.dma("sp", zt[1 - b][:], cz[L][:, (pc + 1) * 512:(pc + 2) * 512], writes=["fzt%d" % (1 - b)])
                    p1 = nextps(0, 7)
                    S.op("pe", lambda e: e.matmul(psb[p1][0:64, :], w1[:], zt[b][:], start=True, stop=True),
                         reads=["fw1", "fzt%d" % b], writes=["ps%d" % p1])
                    S.op("dve", lambda e: e.tensor_scalar(out=arg[:], in0=psb[p1][0:64, :], scalar1=ppT[0:64, PP["ff1"][0]:PP["ff1"][0] + 1],
                                                          scalar2=fb1f[0:64, 0:1], op0=ALU.mult, op1=ALU.add),
                         reads=["ps%d" % p1, "ppT", "fb1f"], writes=["farg"])
                    sin_reduced(h1[:], arg[:], nI[:], nF[:], 64, ["farg"], ["fh1"])
                    p2 = nextps(0, 7)
                    S.op("pe", lambda e: e.matmul(psb[p2][0:64, :], w2[:], h1[:], start=True, stop=True),
                         reads=["fw2", "fh1"], writes=["ps%d" % p2])
                    S.op("dve", lambda e: e.tensor_scalar(out=arg[:], in0=psb[p2][0:64, :], scalar1=ppT[0:64, PP["ff2"][0]:PP["ff2"][0] + 1],
                                                          scalar2=fb1f[0:64, 1:2], op0=ALU.mult, op1=ALU.add),
                         reads=["ps%d" % p2, "ppT", "fb1f"], writes=["farg"])
                    sin_reduced(h2[:], arg[:], nI[:], nF[:], 64, ["farg"], ["fh2"])
                    pd = nextps(0, 7)
                    S.op("pe", lambda e: e.matmul(psb[pd][:], nd4[:].rearrange("p f c -> p (f c)"), zt[b][0:1, :], start=True, stop=True),
                         reads=ndn + ["fzt%d" % b], writes=["ps%d" % pd])
                    S.op("act", lambda e: e.activation(out=dec[:], in_=psb[pd][:], func=AF.Exp), reads=["ps%d" % pd], writes=["fdec"])
                    p3 = nextps(0, 7)
                    S.op("pe", lambda e: e.matmul(psb[p3][:], w3o[:].rearrange("k f c -> k (f c)"), h2[:], start=True, stop=True),
                         reads=w3n + ["fh2"], writes=["ps%d" % p3])
                    S.op("dve", lambda e: e.tensor_tensor(out=fo[b][:], in0=psb[p3][:], in1=dec[:], op=ALU.mult),
                         reads=["ps%d" % p3, "fdec"], writes=["gfo%d" % b])
                    if pc == 0:
                        S.op("dve", lambda e: e.memset(fo[b][0:64, 0:1], 0.0), reads=["gfo%d" % b], writes=["gfo%d" % b])
                    S.dma("sp", FILTs[L][:, pc * 512:(pc + 1) * 512], fo[b][:], reads=["gfo%d" % b], writes=["FILTs%d:%d" % (L, pc)])
            return ["FILTs%d:%d" % (L, pc) for pc in range(nchp)]

        def extract_own(s, zn, un, gn):
            rows = bass.ds(pid * NCO, NCO)
            S.dma("sp", ZTo[s], ZT[s][rows, :], reads=zn, writes=["ZTo%d" % s])
            S.dma("sp", Uo[s], UU[s][rows, :], reads=un, writes=["Uo%d" % s])
            S.dma("sp", Go[s][0], GT[s][rows, :], reads=gn, writes=["Go%d:0" % s])
            S.dma("sp", Go[s][1], GT[s][bass.ds(pid * NCO + 256, NCO), :], reads=gn, writes=["Go%d:1" % s])

        def hy_gate_sh(l, s, order, ynames):
            L = Ls[s]
            LQ = L // 4
            CW = min(512, LQ)
            nch = LQ // CW
            with contextlib.ExitStack() as st:
                hb4 = sbt(st, "hb4", [128, 1], F32)
                for q in range(4):
                    S.dma("sp", hb4[q * NCO:(q + 1) * NCO, :], hyb_raw[l, order, bass.ds(pid * NCO, NCO)].rearrange("(c o) -> c o", o=1),
                          writes=["hb4:%d" % q])
                hbn = ["hb4:%d" % q for q in range(4)]
                yb = [sbt(st, "sy%d" % i, [128, CW], F32) for i in range(2)]
                ub = [sbt(st, "su%d" % i, [128, CW], F32) for i in range(2)]
                gb = [sbt(st, "sg%d" % i, [128, CW], F32) for i in range(2)]
                zz = [sbt(st, "sz%d" % i, [128, CW], F32) for i in range(2)]
                zb = [sbt(st, "szb%d" % i, [128, CW], BF16) for i in range(2)]

                def load(c):
                    b = c % 2
                    for q in range(4):
                        sl = slice(q * LQ + c * CW, q * LQ + (c + 1) * CW)
                        pr = slice(q * NCO, (q + 1) * NCO)
                        S.dma("sp", yb[b][pr, :], YTo[s][:, sl], reads=ynames, writes=["sy%d:%d" % (b, q)])
                        S.dma("sp", ub[b][pr, :], Uo[s][:, sl], reads=["Uo%d" % s] + ["Uo%d:%d:%d" % (s, q, c)], writes=["su%d:%d" % (b, q)])
                        S.dma("sp", gb[b][pr, :], Go[s][order][:, sl], reads=["Go%d:%d" % (s, order)], writes=["sg%d:%d" % (b, q)])

                load(0)
                for c in range(nch):
                    b = c % 2
                    if c + 1 < nch:
                        load(c + 1)
                    qn = lambda nm: ["%s%d:%d" % (nm, b, q) for q in range(4)]
                    S.op("dve", lambda e: e.scalar_tensor_tensor(out=zz[b][:], in0=ub[b][:], scalar=hb4[:, 0:1], in1=yb[b][:],
                                                                 op0=ALU.mult, op1=ALU.add),
                         reads=qn("su") + qn("sy") + hbn, writes=["sz%d" % b])
                    S.op("pool", lambda e: e.tensor_tensor(out=zz[b][:], in0=zz[b][:], in1=gb[b][:], op=ALU.mult),
                         reads=["sz%d" % b] + qn("sg"), writes=["sz%d" % b])
                    if order == 0:
                        S.op("act", lambda e: e.activation(out=zb[b][:], in_=zz[b][:], func=AF.Copy), reads=["sz%d" % b], writes=["szb%d" % b])
                    for q in range(4):
                        sl = slice(q * LQ + c * CW, q * LQ + (c + 1) * CW)
                        pr = slice(q * NCO, (q + 1) * NCO)
                        if order == 0:
                            S.dma("sp", ZTo[s][:, sl], zb[b][pr, :], reads=["szb%d" % b], writes=["ZTo%d:%d:%d" % (s, q, c)])
                            S.dma("sp", Uo[s][:, sl], zz[b][pr, :], reads=["sz%d" % b], writes=["Uo%d:%d:%d" % (s, q, c)])
                        else:
                            S.dma("sp", Co[s].ap()[:, sl], zz[b][pr, :], reads=["sz%d" % b], writes=["Co%d:%d:%d" % (s, q, c)])
            return ["ZTo%d:%d:%d" % (s, q, c) for q in range(4) for c in range(nch)]

        def hy_norm_all(l, s):
            L = Ls[s]
            nch = L // 512
            Cv = Cg[s].ap().rearrange("(k p) t -> p k t", p=128)
            CTv = CT[s].rearrange("(k p) t -> p k t", p=128)
            with contextlib.ExitStack() as st:
                zz = [sbt(st, "nz%d" % i, [128, 2, 512], F32) for i in range(2)]
                zb = [sbt(st, "nzb%d" % i, [128, 2, 512], BF16) for i in range(2)]
                sqz = sbt(st, "nsq", [128, 2, 512], BF16)
                rs = sbt(st, "nrs", [128, 512], F32)
                S.dma("sp", zz[0][:], Cv[:, :, 0:512], reads=["Cg%d" % s], writes=["nz0"])
                for c in range(nch):
                    b = c % 2
                    sl = slice(c * 512, (c + 1) * 512)
                    if c + 1 < nch:
                        S.dma("sp", zz[1 - b][:], Cv[:, :, (c + 1) * 512:(c + 2) * 512], reads=["Cg%d" % s], writes=["nz%d" % (1 - b)])
                    S.op("pool", lambda e: e.tensor_tensor(out=sqz[:], in0=zz[b][:], in1=zz[b][:], op=ALU.mult), reads=["nz%d" % b], writes=["nsq"])
                    pi = nextps(0, 7)
                    for cc in range(2):
                        S.op("pe", lambda e, cc=cc: e.matmul(psb[pi][:], onesB[:], sqz[:, cc, :], start=(cc == 0), stop=(cc == 1)),
                             reads=["nsq", "onesB"], writes=["ps%d" % pi])
                    rsqrt_ln(rs[:], psb[pi][:], 1.0 / 256, ["ps%d" % pi, "epsT"], ["nrs"])
                    for cc in range(2):
                        S.op("dve", lambda e, cc=cc: e.scalar_tensor_tensor(
                            out=zb[b][:, cc, :], in0=zz[b][:, cc, :], scalar=ppc("gout", 6 + cc), in1=rs[:],
                            op0=ALU.mult, op1=ALU.mult), reads=["nz%d" % b, "nrs", "ppT"], writes=["nzb%d" % b])
                    S.dma("sp", CTv[:, 6:8, sl], zb[b][:], reads=["nzb%d" % b], writes=["CTc%d:%d" % (s, c)])

        def phase5(l, s, n3):
            L = Ls[s]
            nch = L // 512
            XAv = XA[s].rearrange("(k p) t -> p k t", p=128)
            XBv = XB[s].rearrange("(k p) t -> p k t", p=128)
            CTv = CT[s].rearrange("(k p) t -> p k t", p=128)
            ctb_names = ["CTb%d:%d" % (s, wi) for wi in range(n3)]
            with contextlib.ExitStack() as st:
                wo = sbt(st, "wo", [128, 8, D], BF16)
                S.dma("sp", wo[:], WB_out.rearrange("(k p) n -> p k n", p=128), reads=["WB_out:%d" % r for r in range(0, D, 128)], writes=["wo"])
                ct = [sbt(st, "ct%d" % i, [128, 8, 512], BF16) for i in range(2)]
                xt = [sbt(st, "x5_%d" % i, [128, 8, 512], F32) for i in range(2)]
                xo = [sbt(st, "xo5_%d" % i, [128, 8, 512], F32) for i in range(2)]

                def load(c):
                    b = c % 2
                    sl = slice(c * 512, (c + 1) * 512)
                    S.dma("sp", ct[b][:], CTv[:, :, sl],
                          reads=rnames("CTa%d" % s, c * 512, (c + 1) * 512) + ctb_names + ["CTc%d:%d" % (s, c)], writes=["ct%d" % b])
                    S.dma("sp", xt[b][:], XAv[:, :, sl], reads=rnames("XA%d" % s, c * 512, (c + 1) * 512), writes=["x5_%d" % b])

                load(0)
                for c in range(nch):
                    b = c % 2
                    if c + 1 < nch:
                        load(c + 1)
                    for m in range(8):
                        pi = nextps(0, 7)
                        for k in range(8):
                            S.op("pe", lambda e, m=m, k=k, pi=pi: e.matmul(psb[pi][:], wo[:, k, m * 128:(m + 1) * 128], ct[b][:, k, :],
                                                                         start=(k == 0), stop=(k == 7)),
                                 reads=["wo", "ct%d" % b], writes=["ps%d" % pi])
                        S.op("dve", lambda e, m=m, pi=pi: e.tensor_tensor(out=xo[b][:, m, :], in0=psb[pi][:], in1=xt[b][:, m, :], op=ALU.add),
                             reads=["ps%d" % pi, "x5_%d" % b], writes=["xo5_%d" % b])
                    S.dma("sp", XBv[:, :, c * 512:(c + 1) * 512], xo[b][:], reads=["xo5_%d" % b],
                          writes=rnames("XB%d" % s, c * 512, (c + 1) * 512))

        def phase6(l, seqs):
            with contextlib.ExitStack() as st:
                wi_ = sbt(st, "wfi", [128, 8, 2 * DFF], BF16)
                wo_ = sbt(st, "wfo", [128, 22, D], BF16)
                for k in range(8):
                    S.dma("sp", wi_[:, k, :], WB_fi[k * 128:(k + 1) * 128, :], reads=["WB_fi:%d" % (k * 128)], writes=["wfi%d" % k])
                S.dma("sp", wo_[:], WB_fo.rearrange("(f p) n -> p f n", p=128), reads=["WB_fo:%d" % r for r in range(0, DFF, 128)], writes=["wfo"])
                wfin = ["wfi%d" % k for k in range(8)]
                hT = sbt(st, "h6", [128, 8, 512], BF16)
                act = sbt(st, "act6", [128, 22, 512], BF16)
                xk = [sbt(st, "xk%d" % i, [128, 512], F32) for i in range(3)]
                sqk = [sbt(st, "sqk%d" % i, [128, 512], BF16) for i in range(2)]
                rstd = sbt(st, "rstd6", [128, 512], F32)
                cg = [sbt(st, "cg%d" % i, [128, 512], F32) for i in range(2)]
                cu = [sbt(st, "cu%d" % i, [128, 512], F32) for i in range(2)]
                gg = [sbt(st, "gg%d" % i, [128, 512], F32) for i in range(2)]
                xo = [sbt(st, "xo6_%d" % i, [128, 512], F32) for i in range(2)]
                xkc = 0
                for s in seqs:
                    L = Ls[s]
                    WO = 510
                    wins = list(range(0, L, WO))
                    for wi, s0 in enumerate(wins):
                        n = min(WO, L - s0)
                        ncol = n + 2
                        t0, t1_ = max(0, s0 - 1), min(L, s0 + n + 1)
                        o0 = t0 - (s0 - 1)
                        edge = (t0 > s0 - 1) or (t1_ < s0 + n + 1)
                        xbn = rnames("XB%d" % s, t0, t1_)

                        def loadk(k):
                            nonlocal xkc
                            bi = xkc % 3
                            xkc += 1
                            if edge:
                                S.op("pool", lambda e: e.memset(xk[bi][:], 0.0), writes=["xk%d" % bi])
                            S.dma("sp", xk[bi][:, o0:o0 + (t1_ - t0)], XB[s][k * 128:(k + 1) * 128, t0:t1_], reads=xbn, writes=["xk%d" % bi])
                            return bi

                        pss = nextps(0, 7)
                        for k in range(8):
                            bi = loadk(k)
                            S.op("pool", lambda e, bi=bi, k=k: e.tensor_tensor(out=sqk[k % 2][:, 0:ncol], in0=xk[bi][:, 0:ncol], in1=xk[bi][:, 0:ncol], op=ALU.mult),
                                 reads=["xk%d" % bi], writes=["sqk%d" % (k % 2)])
                            S.op("pe", lambda e, k=k: e.matmul(psb[pss][:, 0:ncol], onesB[:], sqk[k % 2][:, 0:ncol], start=(k == 0), stop=(k == 7)),
                                 reads=["sqk%d" % (k % 2), "onesB"], writes=["ps%d" % pss])
                        rsqrt_ln(rstd[:, 0:ncol], psb[pss][:, 0:ncol], 1.0 / D, ["ps%d" % pss, "epsT"], ["rstd6"])
                        for k in range(8):
                            bi = loadk(k)
                            S.op("dve", lambda e, bi=bi, k=k: e.scalar_tensor_tensor(
                                out=hT[:, k, 0:ncol], in0=xk[bi][:, 0:ncol], scalar=ppc("g2", k), in1=rstd[:, 0:ncol],
                                op0=ALU.mult, op1=ALU.mult), reads=["xk%d" % bi, "rstd6", "ppT"], writes=["h6"])
                        ow, _ = PP["fcw"]
                        ob, _ = PP["fcb"]
                        for f in range(22):
                            fb = f % 2
                            res = []
                            for half in range(2):
                                ch = half * 22 + f
                                col = half * DFF + f * 128
                                pi = nextps(0, 7)
                                for k in range(8):
                                    S.op("pe", lambda e, k=k, col=col, pi=pi: e.matmul(psb[pi][:, 0:ncol], wi_[:, k, col:col + 128], hT[:, k, 0:ncol],
                                                                                  start=(k == 0), stop=(k == 7)),
                                         reads=["wfi%d" % k, "h6"], writes=["ps%d" % pi])
                                dst = cg[fb] if half == 0 else cu[fb]
                                dn = ("cg%d" if half == 0 else "cu%d") % fb
                                w0 = ppT[:, ow + 3 * ch:ow + 3 * ch + 1]
                                w1 = ppT[:, ow + 3 * ch + 1:ow + 3 * ch + 2]
                                w2 = ppT[:, ow + 3 * ch + 2:ow + 3 * ch + 3]
                                bb = ppT[:, ob + ch:ob + ch + 1]
                                S.op("act", lambda e, dst=dst, pi=pi, w1=w1, bb=bb: e.activation(out=dst[:, 0:n], in_=psb[pi][:, 1:n + 1], func=AF.Identity, bias=bb, scale=w1),
                                     reads=["ps%d" % pi, "ppT"], writes=[dn])
                                S.op("dve", lambda e, dst=dst, pi=pi, w0=w0: e.scalar_tensor_tensor(out=dst[:, 0:n], in0=psb[pi][:, 0:n], scalar=w0, in1=dst[:, 0:n],
                                                                                              op0=ALU.mult, op1=ALU.add),
                                     reads=["ps%d" % pi, dn, "ppT"], writes=[dn])
                                S.op("dve", lambda e, dst=dst, pi=pi, w2=w2: e.scalar_tensor_tensor(out=dst[:, 0:n], in0=psb[pi][:, 2:n + 2], scalar=w2, in1=dst[:, 0:n],
                                                                                              op0=ALU.mult, op1=ALU.add),
                                     reads=["ps%d" % pi, dn, "ppT"], writes=[dn])
                            S.op("act", lambda e, fb=fb: e.activation(out=gg[fb][:, 0:n], in_=cg[fb][:, 0:n], func=AF.Gelu),
                                 reads=["cg%d" % fb], writes=["gg%d" % fb])
                            S.op("pool", lambda e, fb=fb, f=f: e.tensor_tensor(out=act[:, f, 0:n], in0=gg[fb][:, 0:n], in1=cu[fb][:, 0:n], op=ALU.mult),
                                 reads=["gg%d" % fb, "cu%d" % fb], writes=["act6"])
                        for m in range(8):
                            pi = nextps(0, 7)
                            for f in range(22):
                                S.op("pe", lambda e, m=m, f=f, pi=pi: e.matmul(psb[pi][:, 0:n], wo_[:, f, m * 128:(m + 1) * 128], act[:, f, 0:n],
                                                                             start=(f == 0), stop=(f == 21)),
                                     reads=["wfo", "act6"], writes=["ps%d" % pi])
                            bi = xkc % 3
                            xkc += 1
                            S.dma("sp", xk[bi][:, 0:n], XB[s][m * 128:(m + 1) * 128, s0:s0 + n], reads=xbn, writes=["xk%d" % bi])
                            ob_ = m % 2
                            S.op("dve", lambda e, bi=bi, pi=pi, ob_=ob_: e.tensor_tensor(out=xo[ob_][:, 0:n], in0=psb[pi][:, 0:n], in1=xk[bi][:, 0:n], op=ALU.add),
                                 reads=["ps%d" % pi, "xk%d" % bi], writes=["xo6_%d" % ob_])
                            S.dma("sp", XA[s][m * 128:(m + 1) * 128, s0:s0 + n], xo[ob_][:, 0:n], reads=["xo6_%d" % ob_],
                                  writes=["XAw%d:%d:%d" % (s, wi, m)] + rnames("XA%d" % s, s0, s0 + n))

        def phase7(s):
            L = Ls[s]
            nch = L // 512
            XAv = XA[s].rearrange("(k p) t -> p k t", p=128)
            allw = []
            with contextlib.ExitStack() as st:
                xt = [sbt(st, "x7_%d" % i, [128, 8, 512], F32) for i in range(2)]
                yt = [sbt(st, "y7_%d" % i, [128, 4, D], F32) for i in range(2)]

                def load(c):
                    S.dma("sp", xt[c % 2][:], XAv[:, :, c * 512:(c + 1) * 512],
                          reads=rnames("XA%d" % s, c * 512, (c + 1) * 512), writes=["x7_%d" % (c % 2)])

                load(0)
                for c in range(nch):
                    b = c % 2
                    if c + 1 < nch:
                        load(c + 1)
                    for n in range(4):
                        for kh in range(2):
                            pi = nextps(0, 7)
                            for kq in range(4):
                                k = kh * 4 + kq
                                S.op("pe", lambda e, n=n, k=k, kq=kq, pi=pi: e.transpose(psb[pi][:, kq * 128:(kq + 1) * 128], xt[b][:, k, n * 128:(n + 1) * 128], identF[:]),
                                     reads=["x7_%d" % b, "identF"], writes=["ps%d" % pi])
                            if (n + kh) % 2 == 0:
                                S.op("act", lambda e, n=n, kh=kh, pi=pi: e.activation(out=yt[b][:, n, kh * 512:(kh + 1) * 512], in_=psb[pi][:], func=AF.Copy),
                                     reads=["ps%d" % pi], writes=["y7_%d:%d%d" % (b, n, kh)])
                            else:
                                S.op("dve", lambda e, n=n, kh=kh, pi=pi: e.tensor_copy(out=yt[b][:, n, kh * 512:(kh + 1) * 512], in_=psb[pi][:]),
                                     reads=["ps%d" % pi], writes=["y7_%d:%d%d" % (b, n, kh)])
                    S.dma("sp", y_out[s][c * 512:(c + 1) * 512, :].rearrange("(n p) d -> p n d", p=128), yt[b][:],
                          reads=["y7_%d:%d%d" % (b, n, kh) for n in range(4) for kh in range(2)], writes=["y%d:%d" % (s, c)])

        for l in range(depth):
            load_layer_params(l)
            n3 = {}
            for s in range(nseq):
                phase1(l, s, first=(l == 0))
                S.join()
            if stop_after == "p1":
                break
            for s in range(nseq):
                phase2(l, s)
                S.join()
            for s in range(nseq):
                n3[s] = phase3(l, s)
                S.join()
            n4 = {}
            for s in range(nseq):
                n4[s] = phase4a(l, s)
                S.join()
            if stop_after == "p4a":
                break
            for L in uL:
                fn = filters(l, L)
                S.join()
                for o in range(2):
                    fft_pass(L, FILT[L][2 * o], fn, "kf", o)
                    S.join()
                    fft_pass(L, FILT[L][2 * o + 1], fn, "kb", o)
                    S.join()
            for L in uLs:
                fn = filters_sh(l, L)
                S.join()
                for o in range(2):
                    sf, sb_ = FORD.index(2 * o), FORD.index(2 * o + 1)
                    fft_pass(L, FILTs[L][sf * NCO:(sf + 1) * NCO, :], fn, "kf", o, C=NCO, ks=KSs[L][o], kstag="KSs%d_%d" % (L, o))
                    S.join()
                    fft_pass(L, FILTs[L][sb_ * NCO:(sb_ + 1) * NCO, :], fn, "kb", o, C=NCO, ks=KSs[L][o], kstag="KSs%d_%d" % (L, o))
                    S.join()
            if stop_after == "filt":
                break
            for s in range(nseq):
                L = Ls[s]
                zn = ["ZT%d:w%d" % (s, wi) for wi in range(n4[s])]
                un = ["UU%d:w%d" % (s, wi) for wi in range(n4[s])]
                gn = ["GT%d:w%d" % (s, wi) for wi in range(n4[s])]
                if shard[s]:
                    extract_own(s, zn, un, gn)
                    S.join()
                    if stop_after == "ext":
                        break
                    yn = fft_pass(L, ZTo[s], ["ZTo%d" % s], "conv", 0, dst=YTo[s], dst_tag="YTo%d_0" % s, C=NCO, ks=KSs[L][0], kstag="KSs%d_0" % L)
                    S.join()
                    if stop_after == "c0":
                        break
                    zn2 = hy_gate_sh(l, s, 0, yn)
                    S.join()
                    if stop_after == "g0":
                        break
                    yn = fft_pass(L, ZTo[s], zn2, "conv", 1, dst=YTo[s], dst_tag="YTo%d_1" % s, C=NCO, ks=KSs[L][1], kstag="KSs%d_1" % L)
                    S.join()
                    hy_gate_sh(l, s, 1, yn)
                    S.join()
                    if stop_after == "g1":
                        break
                    S.collective(es, "AllGather", Co[s].ap(), Cg[s].ap(), ccs[:], reads=[], writes=["Cg%d" % s])
                    S.join()
                    hy_norm_all(l, s)
                    S.join()
                    continue
                yn = fft_pass(L, ZT[s], zn, "conv", 0, dst=YT[s], dst_tag="YT%d_0" % s)
                S.join()
                pc = hy_post(l, s, 0, yn, un, gn)
                S.join()
                zn = ["ZT%d:p%d" % (s, c) for c in pc]
                un = ["UU%d:p%d" % (s, c) for c in pc]
                yn = fft_pass(L, ZT[s], zn, "conv", 1, dst=YT[s], dst_tag="YT%d_1" % s)
                S.join()
                hy_post(l, s, 1, yn, un, gn)
                S.join()
            if stop_after in ("mix", "ext", "c0", "g0", "g1"):
                break
            for s in range(nseq):
                phase5(l, s, n3[s])
                S.join()
            if stop_after == "p5":
                break
            phase6(l, list(range(nseq)))
            S.join()
        if stop_after is None:
            for s in range(nseq):
                phase7(s)
        S.finish("sp")
        nc._n_sched_instr = S.ninstr
    return nc


_CACHE = {}


def host_inputs(inp, Ls, depth):
    qp = q_perm()
    w_in = np.array(inp["w_in"][:depth], np.float32, copy=True)
    w_in[:, :, :512] = w_in[:, :, :512][:, :, qp]
    w_out = np.array(inp["w_out"][:depth], np.float32, copy=True)
    w_out[:, :512, :] = w_out[:, :512, :][:, qp, :]
    m = {
        "w_in": w_in, "w_out": w_out,
        "w_ffn_in": np.ascontiguousarray(inp["w_ffn_in"][:depth], np.float32),
        "w_ffn_out": np.ascontiguousarray(inp["w_ffn_out"][:depth], np.float32),
        "pool_w": np.ascontiguousarray(inp["pool_w"][:depth], np.float32),
        "filt_w1": np.ascontiguousarray(inp["filt_w1"][:depth], np.float32),
        "filt_w2": np.ascontiguousarray(inp["filt_w2"][:depth], np.float32),
        "filt_w3": np.ascontiguousarray(inp["filt_w3"][:depth], np.float32),
        "pp": np.stack([pack_pp(inp, l) for l in range(depth)]),
        "hyb_raw": np.ascontiguousarray(inp["hy_bias"][:depth], np.float32),
        "c_alibi": alibi_table().reshape(128, -1),
        "c_ndelta": neg_deltas(),
        "c_ident": np.eye(128, dtype=np.float32),
        "c_bones": np.kron(np.eye(2, dtype=np.float32), np.ones((64, 64), np.float32)),
    }
    for L in sorted(set(Ls)):
        FA, TB, G, TD, TW = fft_tables(L)
        m["c_TW%d" % L] = TW
        m["c_z%d" % L] = z_table(L)
        m["c_FA%d" % L] = FA
        m["c_TB%d" % L] = TB
        m["c_G%d" % L] = G
        m["c_TD%d" % L] = TD
    return m


def run(inp, xs_per_core, Ls, depth, dbg=(), stop_after=None, shard=None):
    key = (tuple(Ls), depth, tuple(dbg), stop_after, tuple(shard) if shard else None)
    if key not in _CACHE:
        _CACHE[key] = build_program(Ls, depth, dbg, stop_after, shard)
    nc = _CACHE[key]
    shared = host_inputs(inp, Ls, depth)
    in_maps = []
    for core in range(8):
        mm = dict(shared)
        for s in range(len(Ls)):
            mm["x%d" % s] = np.ascontiguousarray(xs_per_core[core][s], np.float32)
        in_maps.append(mm)
    res = run_bass_kernel_spmd(nc, in_maps, core_ids=list(range(8)))
    return res.results


def kernel(**inputs):
    inp = {k: np.asarray(v) for k, v in inputs.items()}
    xp = inp["x_prompt"]
    xs = inp["x_sample"]
    LA, LB = xp.shape[1], xs.shape[1]
    depth = inp["w_in"].shape[0]
    xs_per_core = [[xp[c], xs[0]] for c in range(8)]
    results = run(inp, xs_per_core, [LA, LB], depth)
    y_prompt = np.stack([results[c]["y0"] for c in range(8)]).astype(np.float32)
    y_sample = results[0]["y1"][None].astype(np.float32)
    return (y_prompt, y_sample)
```
